# Optimizing a Trainium2 kernel written in Bass

```python
import jax, jax.numpy as jnp
from jax import lax
import numpy as np

D_MODEL = 1024
BATCH = 2
SEQ = 16384
DEPTH = 4

MLA_HEADS = 4
QK_NOPE_DIM = 128
QK_ROPE_DIM = 64
QK_HEAD_DIM = QK_NOPE_DIM + QK_ROPE_DIM
V_HEAD_DIM = 128
Q_LORA_RANK = 256
KV_LORA_RANK = 128
ROPE_THETA = 10000.0
SB_HEADS = 4
SB_HEAD_DIM = 128
D_FF = 2816
CONV_WIDTH = 3
BLOCK_Q = 128
SUB = 16
N_SUB = BLOCK_Q // SUB
EPS = 1e-6
N_MOD = 6

MLA_WIDTH = MLA_HEADS * V_HEAD_DIM
SB_WIDTH = SB_HEADS * SB_HEAD_DIM
IN_SPLITS = (Q_LORA_RANK, KV_LORA_RANK, QK_ROPE_DIM, SB_WIDTH, SB_WIDTH, SB_WIDTH, D_MODEL, D_MODEL)
D_IN = sum(IN_SPLITS)
IN_SPLIT_POINTS = [sum(IN_SPLITS[:i + 1]) for i in range(len(IN_SPLITS) - 1)]

kernel_name = "hybrid_mla_stickbreak_convffn_adaln"


def rms_norm(x, g):
    x32 = x.astype(jnp.float32)
    y = x32 * lax.rsqrt(jnp.mean(x32 * x32, axis=-1, keepdims=True) + EPS)
    return (y * g.astype(jnp.float32)).astype(x.dtype)


def rope(x, positions):
    dr = x.shape[-1]
    half = dr // 2
    inv_freq = 1.0 / (ROPE_THETA ** (jnp.arange(half, dtype=jnp.float32) * (2.0 / dr)))
    ang = positions.astype(jnp.float32)[..., None] * inv_freq
    ang = ang.reshape(ang.shape[:2] + (1,) * (x.ndim - 3) + (half,))
    cos, sin = jnp.cos(ang), jnp.sin(ang)
    x32 = x.astype(jnp.float32)
    x1, x2 = x32[..., :half], x32[..., half:]
    return jnp.concatenate([x1 * cos - x2 * sin, x2 * cos + x1 * sin], axis=-1).astype(x.dtype)


def causal_softmax_attention(q, k, v):
    b, s, h, _ = q.shape
    scale = QK_HEAD_DIM ** -0.5
    qh = q.transpose(0, 2, 1, 3).astype(jnp.float32)
    kh = k.transpose(0, 2, 1, 3).astype(jnp.float32)
    vh = v.transpose(0, 2, 1, 3).astype(jnp.float32)
    outs = []
    for i in range(s // BLOCK_Q):
        st = i * BLOCK_Q
        ln = st + BLOCK_Q
        sc = jnp.einsum('bhqd,bhkd->bhqk', qh[:, :, st:ln], kh[:, :, :ln]) * scale
        mask = np.arange(ln)[None, :] <= np.arange(st, ln)[:, None]
        p = jax.nn.softmax(jnp.where(mask, sc, -jnp.inf), axis=-1)
        outs.append(jnp.einsum('bhqk,bhkd->bhqd', p, vh[:, :, :ln]))
    o = jnp.concatenate(outs, axis=2)
    return o.transpose(0, 2, 1, 3).reshape(b, s, h * v.shape[-1]).astype(v.dtype)


def reverse_cumsum_keys(l):
    lead = l.shape[:-1]
    nk = l.shape[-1] // BLOCK_Q
    r = l.reshape(lead + (nk, N_SUB, SUB))
    c0 = lax.cumsum(r, axis=r.ndim - 1, reverse=True)
    s0 = c0[..., 0]
    c1 = lax.cumsum(s0, axis=s0.ndim - 1, reverse=True)
    s1 = c1[..., 0]
    c2 = lax.cumsum(s1, axis=s1.ndim - 1, reverse=True) - s1
    tot = c0 + (c1 - s0)[..., None] + c2[..., None, None]
    return tot.reshape(l.shape)


def stick_breaking_attention(q, k, v):
    b, s, h, d = q.shape
    scale = d ** -0.5
    qh = q.transpose(0, 2, 1, 3).astype(jnp.float32)
    kh = k.transpose(0, 2, 1, 3).astype(jnp.float32)
    vh = v.transpose(0, 2, 1, 3).astype(jnp.float32)
    outs = []
    for i in range(s // BLOCK_Q):
        st = i * BLOCK_Q
        ln = st + BLOCK_Q
        z = jnp.einsum('bhqd,bhkd->bhqk', qh[:, :, st:ln], kh[:, :, :ln]) * scale
        past = np.arange(ln)[None, :] < np.arange(st, ln)[:, None]
        l = jnp.where(past, jax.nn.log_sigmoid(-z), 0.0)
        a = jnp.where(past, jnp.exp(z + reverse_cumsum_keys(l)), 0.0)
        outs.append(jnp.einsum('bhqk,bhkd->bhqd', a, vh[:, :, :ln]))
    o = jnp.concatenate(outs, axis=2)
    return o.transpose(0, 2, 1, 3).reshape(b, s, h * d).astype(v.dtype)


def causal_depthwise_conv(u, w, bias):
    ch = u.shape[-1]
    out = lax.conv_general_dilated(
        u, w[:, None, :].astype(u.dtype), window_strides=(1,),
        padding=[(CONV_WIDTH - 1, 0)], dimension_numbers=('NWC', 'WIO', 'NWC'),
        feature_group_count=ch)
    return out + bias


def setup_inputs(seed: int = 0) -> dict:
    key = jax.random.key(seed)
    ks = jax.random.split(key, 24)
    f32 = jnp.float32

    def nrm(k, shape, scale):
        return jax.random.normal(k, shape, f32) * scale

    def gain(k, n):
        return 1.0 + 0.02 * jax.random.normal(k, (DEPTH, n), f32)

    return {
        "x": nrm(ks[0], (BATCH, SEQ, D_MODEL), 1.0),
        "c": nrm(ks[1], (BATCH, D_MODEL), 1.0),
        "positions": jnp.broadcast_to(jnp.arange(SEQ, dtype=jnp.int32)[None, :], (BATCH, SEQ)),
        "w_ada": nrm(ks[2], (DEPTH, D_MODEL, N_MOD * D_MODEL), 0.5 * D_MODEL ** -0.5),
        "b_ada": nrm(ks[3], (DEPTH, N_MOD * D_MODEL), 0.02),
        "g_norm1": gain(ks[4], D_MODEL),
        "w_in": nrm(ks[5], (DEPTH, D_MODEL, D_IN), D_MODEL ** -0.5),
        "b_in": nrm(ks[6], (DEPTH, D_IN), 0.02),
        "g_q_lat": gain(ks[7], Q_LORA_RANK),
        "w_uq": nrm(ks[8], (DEPTH, Q_LORA_RANK, MLA_HEADS * QK_HEAD_DIM), Q_LORA_RANK ** -0.5),
        "g_kv_lat": gain(ks[9], KV_LORA_RANK),
        "w_ukv": nrm(ks[10], (DEPTH, KV_LORA_RANK, MLA_HEADS * (QK_NOPE_DIM + V_HEAD_DIM)), KV_LORA_RANK ** -0.5),
        "g_q_head": gain(ks[11], QK_HEAD_DIM),
        "g_k_head": gain(ks[12], QK_HEAD_DIM),
        "w_branch_mla": nrm(ks[13], (DEPTH, MLA_WIDTH, D_MODEL), MLA_WIDTH ** -0.5),
        "w_branch_sb": nrm(ks[14], (DEPTH, SB_WIDTH, D_MODEL), SB_WIDTH ** -0.5),
        "w_out": nrm(ks[15], (DEPTH, D_MODEL, D_MODEL), D_MODEL ** -0.5),
        "g_norm2": gain(ks[16], D_MODEL),
        "w_up": nrm(ks[17], (DEPTH, D_MODEL, 2 * D_FF), D_MODEL ** -0.5),
        "w_conv": nrm(ks[18], (DEPTH, CONV_WIDTH, 2 * D_FF), CONV_WIDTH ** -0.5),
        "b_conv": nrm(ks[19], (DEPTH, 2 * D_FF), 0.02),
        "w_down": nrm(ks[20], (DEPTH, D_FF, D_MODEL), D_FF ** -0.5),
    }


def reference(x, c, positions, w_ada, b_ada, g_norm1, w_in, b_in, g_q_lat, w_uq, g_kv_lat,
              w_ukv, g_q_head, g_k_head, w_branch_mla, w_branch_sb, w_out, g_norm2, w_up,
              w_conv, b_conv, w_down):
    b, s, _ = x.shape
    c_act = jax.nn.silu(c)
    for l in range(DEPTH):
        mod = c_act @ w_ada[l] + b_ada[l]
        sh1, sc1, gt1, sh2, sc2, gt2 = [m[:, None, :] for m in jnp.split(mod, N_MOD, axis=-1)]

        h = rms_norm(x, g_norm1[l]) * (1.0 + sc1) + sh1
        u = h @ w_in[l] + b_in[l]
        q_lat, kv_lat, k_rope, q_sb, k_sb, v_sb, gate_mla, gate_sb = jnp.split(u, IN_SPLIT_POINTS, axis=-1)

        q = (rms_norm(q_lat, g_q_lat[l]) @ w_uq[l]).reshape(b, s, MLA_HEADS, QK_HEAD_DIM)
        kv = (rms_norm(kv_lat, g_kv_lat[l]) @ w_ukv[l]).reshape(b, s, MLA_HEADS, QK_NOPE_DIM + V_HEAD_DIM)
        k_nope, v_mla = kv[..., :QK_NOPE_DIM], kv[..., QK_NOPE_DIM:]
        k_rope_h = jnp.broadcast_to(k_rope[:, :, None, :], (b, s, MLA_HEADS, QK_ROPE_DIM))
        k = jnp.concatenate([k_nope, k_rope_h], axis=-1)
        q = rms_norm(q, g_q_head[l])
        k = rms_norm(k, g_k_head[l])
        q = jnp.concatenate([q[..., :QK_NOPE_DIM], rope(q[..., QK_NOPE_DIM:], positions)], axis=-1)
        k = jnp.concatenate([k[..., :QK_NOPE_DIM], rope(k[..., QK_NOPE_DIM:], positions)], axis=-1)
        y_mla = causal_softmax_attention(q, k, v_mla)

        y_sb = stick_breaking_attention(
            q_sb.reshape(b, s, SB_HEADS, SB_HEAD_DIM),
            k_sb.reshape(b, s, SB_HEADS, SB_HEAD_DIM),
            v_sb.reshape(b, s, SB_HEADS, SB_HEAD_DIM))

        merged = (jax.nn.sigmoid(gate_mla) * (y_mla @ w_branch_mla[l])
                  + jax.nn.sigmoid(gate_sb) * (y_sb @ w_branch_sb[l]))
        x = x + gt1 * (merged @ w_out[l])

        h = rms_norm(x, g_norm2[l]) * (1.0 + sc2) + sh2
        up = causal_depthwise_conv(h @ w_up[l], w_conv[l], b_conv[l])
        a, v_ff = jnp.split(up, 2, axis=-1)
        x = x + gt2 * ((jax.nn.silu(a) * v_ff) @ w_down[l])
    return x
```

```python
import contextlib
import math
import numpy as np
import concourse.bass as bass
import concourse.mybir as mybir
from concourse.bass_utils import run_bass_kernel_spmd

F32 = mybir.dt.float32
BF16 = mybir.dt.bfloat16
I32 = mybir.dt.int32
AF = mybir.ActivationFunctionType
ALU = mybir.AluOpType

import os
SAFE = bool(int(os.environ.get("KSAFE", "0")))
D = 1024
DIN = 4032
DFF = 2816
EPS = 1e-6
SC_MLA = 192 ** -0.5
SC_SB = 128 ** -0.5
MAGIC = 12582912.0
TWO_PI = 2.0 * math.pi
C1 = 6.28125
C2 = float(np.float32(TWO_PI - C1))
C3 = float(TWO_PI - C1 - C2)
PI_LO = 3.1415925

GROUPS = [(0, 128), (128, 128), (256, 128), (384, 64), (4032, 64)]
GROUPS += [(448 + 128 * h, 128) for h in range(4)]
GROUPS += [(960 + 128 * h, 128) for h in range(4)]
GROUPS += [(1984 + 128 * j, 128) for j in range(8)]
GROUPS += [(3008 + 128 * j, 128) for j in range(8)]
NG = len(GROUPS)


NSLOT = 16


class StopBuild(Exception):
    pass


class Sem:
    def __init__(self, nc, name, inc):
        self.h = nc.alloc_semaphore(name=name)
        self.inc = inc
        self.count = 0


class Stream:
    def __init__(self, eng):
        self.eng = eng
        self.waited = {}


class Res:
    __slots__ = ("w", "r")

    def __init__(self):
        self.w = None
        self.r = {}


class Tl:
    def __init__(self, t, ex=False):
        self.t = t
        self.r = Res()
        self.ex = ex

    def __getitem__(self, k):
        return self.t[k]


class FW:
    def __init__(self, nc):
        self.nc = nc
        self.S_pe = Stream(nc.tensor)
        self.S_act = Stream(nc.scalar)
        self.S_dve = Stream(nc.vector)
        self.S_pool = Stream(nc.gpsimd)
        self.S_sp = Stream(nc.sync)
        self.sem_pe = Sem(nc, "s_pe", 1)
        self.sem_act = Sem(nc, "s_act", 1)
        self.sem_dve = Sem(nc, "s_dve", 1)
        self.sem_pool = Sem(nc, "s_pool", 1)
        self.sem_cc = Sem(nc, "s_cc", 1)
        self.dq_sp = [Sem(nc, f"s_dsp{i}", 16) for i in range(NSLOT)]
        self.dq_pl = [Sem(nc, f"s_dpl{i}", 16) for i in range(NSLOT)]
        self.n_sp = 0
        self.n_pl = 0
        self.n_ins = 0
        self.stopped = False

    def _issue(self, stream, sem, fn, reads, writes):
        if self.stopped:
            return None
        exr = [t for t in reads if t.ex]
        if exr:
            reads = [t for t in reads if not t.ex]
            writes = list(writes) + [t for t in exr if t not in writes]
        need = {}
        for t in reads:
            w = t.r.w
            if w is not None and need.get(w[0], 0) < w[1]:
                need[w[0]] = w[1]
        for t in writes:
            w = t.r.w
            if w is not None and need.get(w[0], 0) < w[1]:
                need[w[0]] = w[1]
            for sm, c in t.r.r.items():
                if need.get(sm, 0) < c:
                    need[sm] = c
        if SAFE:
            for sm in self.all_sems():
                if sm.count > 0:
                    need[sm] = sm.count
        for sm, c in need.items():
            if sm is self.sem_pe and stream is self.S_pe and not SAFE:
                continue
            if stream.waited.get(sm, 0) < c:
                stream.eng.wait_ge(sm.h, c * sm.inc)
                stream.waited[sm] = c
                self.n_ins += 1
        ins = fn()
        sem.count += 1
        ins.then_inc(sem.h, sem.inc)
        self.n_ins += 1
        for t in reads:
            if t.r.r.get(sem, 0) < sem.count:
                t.r.r[sem] = sem.count
        for t in writes:
            t.r.w = (sem, sem.count)
            t.r.r = {}
        return ins

    def barrier(self):
        if self.stopped:
            return
        for stream in (self.S_pe, self.S_act, self.S_dve, self.S_pool, self.S_sp):
            for sm in self.all_sems():
                if sm.count > 0 and stream.waited.get(sm, 0) < sm.count:
                    stream.eng.wait_ge(sm.h, sm.count * sm.inc)
                    stream.waited[sm] = sm.count
                    self.n_ins += 1

    def all_sems(self):
        return self.dq_sp + self.dq_pl + [self.sem_cc, self.sem_pe, self.sem_act, self.sem_dve, self.sem_pool]

    def pe(self, fn, reads, writes):
        return self._issue(self.S_pe, self.sem_pe, fn, reads, writes)

    def act(self, fn, reads, writes):
        return self._issue(self.S_act, self.sem_act, fn, reads, writes)

    def dve(self, fn, reads, writes):
        return self._issue(self.S_dve, self.sem_dve, fn, reads, writes)

    def pool(self, fn, reads, writes):
        return self._issue(self.S_pool, self.sem_pool, fn, reads, writes)

    def cc(self, fn, reads, writes):
        return self._issue(self.S_pool, self.sem_cc, fn, reads, writes)

    def dma(self, which, out, in_, reads, writes):
        nc = self.nc
        if which == "sp":
            stream, sem, eng = self.S_sp, self.dq_sp[self.n_sp % NSLOT], nc.sync
            self.n_sp += 1
        else:
            stream, sem, eng = self.S_pool, self.dq_pl[self.n_pl % NSLOT], nc.gpsimd
            self.n_pl += 1
        if self.stopped:
            return None
        if sem.count > 0 and stream.waited.get(sem, 0) < sem.count:
            stream.eng.wait_ge(sem.h, sem.count * 16)
            stream.waited[sem] = sem.count
        return self._issue(stream, sem, lambda: eng.dma_start(out=out, in_=in_), reads, writes)

    def finish(self):
        for sem in self.dq_sp + self.dq_pl + [self.sem_cc, self.sem_pe, self.sem_act, self.sem_dve, self.sem_pool]:
            if sem.count > 0:
                self.nc.sync.wait_ge(sem.h, sem.count * sem.inc)


def build(S, L, stop_after=None, dump=()):
    NB = S // 512
    TT = NB // 4
    NT = NB * 128
    nc = bass.Bass("TRN2", target_bir_lowering=False)
    fw = FW(nc)

    def din(name, shape, dt=F32):
        return Tl(nc.dram_tensor(name, list(shape), dt, kind="ExternalInput").ap())

    scr = {}

    def dscr(name, shape, dt):
        t = Tl(nc.dram_tensor(name, list(shape), dt).ap())
        scr[name] = (t, list(shape), dt)
        return t

    RG = [[0, 1, 2, 3], [4, 5, 6, 7]]
    gathers = []

    class WStack:
        def __init__(self, name, K, N, shard=False):
            self.shard, self.K, self.N = shard, K, N
            if shard:
                self.sh = [din(f"{name}_sh{l}", [K // 4, N]) for l in range(L)]
                self.bn = [Tl(nc.dram_tensor(f"{name}_bn{l}", [K // 4, N], F32).ap()) for l in range(L)]
                self.tl = [Tl(nc.dram_tensor(f"{name}_full{l}", [K, N], F32).ap()) for l in range(L)]
                gathers.append(self)
            else:
                self.tl = [din(f"{name}_l{l}", [K, N]) for l in range(L)]

        def __getitem__(self, key):
            if isinstance(key, tuple):
                return self.tl[key[0]].t[key[1:]]
            return self.tl[key].t

    x_in = din("x", [NT, D])
    pos = din("pos", [1, NT], I32)
    c_in = din("c", [128, 8])
    ident_in = din("ident", [128, 128])
    umat_in = din("umat", [128, 128])
    ubar_in = din("ubar", [128, 128])
    mcaus_in = din("mcaus", [128, 4, 128])
    mstr_in = din("mstrict", [128, 4, 128])
    halow_in = din("halow", [128, 5])
    invf_in = din("invf", [64, 1])
    w_ada_q = [din(f"w_ada_q{l}", [D, 1536]) for l in range(L)]
    b_ada_fq = din("b_ada_fq", [L, 128, 12])
    b_ada_q = din("b_ada_q", [L, 1536])
    g1_in = din("g1", [L, 128, 8])
    g2_in = din("g2", [L, 128, 8])
    w_in = WStack("w_in", D, DIN, shard=True)
    b_in_g = din("b_in_g", [L, 128, NG])
    b_in = din("b_in", [L, DIN])
    gql_in = din("gql", [L, 128, 2])
    gkvl_in = din("gkvl", [L, 128, 1])
    w_uq = WStack("w_uq", 256, 768)
    w_ukv = WStack("w_ukv", 128, 1024)
    gqh_in = din("gqh", [L, 128, 3])
    gkh_in = din("gkh", [L, 128, 3])
    w_bm = WStack("w_bm", 512, D)
    w_bs = WStack("w_bs", 512, D)
    w_out = WStack("w_out", D, D)
    w_up = WStack("w_up", D, 2 * DFF, shard=True)
    wconv_f = din("wconv_f", [L, 128, 44, 3])
    bconv_f = din("bconv_f", [L, 128, 44])
    w_down = WStack("w_down", DFF, D)
    y_out = Tl(nc.dram_tensor("y", [NT, D], F32, kind="ExternalOutput").ap())

    X1 = dscr("X1", [NT, D], F32)
    XA = dscr("XA", [NT, D], F32)
    MF_i = dscr("MF_i", [128, L * 12], F32)
    MF_o = dscr("MF_o", [4 * 128, L * 12], F32)
    GR_i = dscr("GR_i", [1, L * 1536], F32)
    GR_o = dscr("GR_o", [4, L * 1536], F32)
    CS = dscr("CS", [2, 64, NT], F32)
    QTn = dscr("QTn", [4, 128, NT], BF16)
    QTr = dscr("QTr", [4, 64, NT], BF16)
    QTs = dscr("QTs", [4, 128, NT], BF16)
    GA = dscr("GA", [TT, 128, 16 * 512], BF16)
    YTm = dscr("YTm", [4, 128, NT], BF16)
    YTs = dscr("YTs", [4, 128, NT], BF16)
    H2 = dscr("H2", [TT, 128, 8 * 512], BF16)
    KTn_i = [dscr(f"KTn_i{h}", [128, NT], BF16) for h in range(4)]
    KTn_o = [dscr(f"KTn_o{h}", [4 * 128, NT], BF16) for h in range(4)]
    KTr_i = [dscr(f"KTr_i{h}", [64, NT], BF16) for h in range(4)]
    KTr_o = [dscr(f"KTr_o{h}", [4 * 64, NT], BF16) for h in range(4)]
    KTs_i = [dscr(f"KTs_i{h}", [128, NT], BF16) for h in range(4)]
    KTs_o = [dscr(f"KTs_o{h}", [4 * 128, NT], BF16) for h in range(4)]
    VM_i = [dscr(f"VM_i{h}", [TT * 128, 512], BF16) for h in range(4)]
    VM_o = [dscr(f"VM_o{h}", [4 * TT * 128, 512], BF16) for h in range(4)]
    VS_i = [dscr(f"VS_i{h}", [TT * 128, 512], BF16) for h in range(4)]
    VS_o = [dscr(f"VS_o{h}", [4 * TT * 128, 512], BF16) for h in range(4)]
    HB_i = dscr("HB_i", [128, 8 * NB * 2], BF16)
    HB_o = dscr("HB_o", [4 * 128, 8 * NB * 2], BF16)
    RG = [[0, 1, 2, 3], [4, 5, 6, 7]]

    def allgather(src, dst):
        fw.cc(lambda: nc.gpsimd.collective_compute("AllGather", ALU.bypass, replica_groups=RG,
                                                   ins=[src.t.opt()], outs=[dst.t.opt()]),
              reads=[src], writes=[dst])

    with contextlib.ExitStack() as top:
        nm = [0]

        def SB(es, shape, dt):
            nm[0] += 1
            return Tl(es.enter_context(nc.sbuf_tensor(f"sb{nm[0]}", list(shape), dt)))

        def PS(es, shape, dt):
            nm[0] += 1
            return Tl(es.enter_context(nc.psum_tensor(f"ps{nm[0]}", list(shape), dt)), ex=True)

        ps = [PS(top, [128, 512], F32) for _ in range(8)]
        pt = []
        for i in (6, 7):
            v = Tl(ps[i].t[:].bitcast(BF16), ex=True)
            v.r = ps[i].r
            pt.append(v)
        rot = [0]

        def nextps(n=6):
            rot[0] = (rot[0] + 1) % n
            return ps[rot[0]]

        def mm(out, lhsT, rhs, start, stop, reads, writes):
            fw.pe(lambda: nc.tensor.matmul(out, lhsT=lhsT, rhs=rhs, start=start, stop=stop,
                                           skip_group_check=True), reads, writes)

        def actf(out, in_, func, reads, writes, scale=None, bias=None):
            kw = {}
            if scale is not None:
                kw["scale"] = scale
            if bias is not None:
                kw["bias"] = bias
            fw.act(lambda: nc.scalar.activation(out=out, in_=in_, func=func, **kw), reads, writes)

        def ts(out, in0, s1, s2, op0, op1, reads, writes, eng="dve"):
            if op1 is None:
                f = lambda: nc.vector.tensor_scalar(out=out, in0=in0, scalar1=s1, scalar2=None, op0=op0)
            else:
                f = lambda: nc.vector.tensor_scalar(out=out, in0=in0, scalar1=s1, scalar2=s2, op0=op0, op1=op1)
            fw.dve(f, reads, writes)

        def stt(out, in0, scalar, in1, op0, op1, reads, writes):
            fw.dve(lambda: nc.vector.scalar_tensor_tensor(out=out, in0=in0, scalar=scalar, in1=in1, op0=op0, op1=op1),
                   reads, writes)

        def tt(out, in0, in1, op, reads, writes):
            fw.dve(lambda: nc.vector.tensor_tensor(out=out, in0=in0, in1=in1, op=op), reads, writes)

        def ptt(out, in0, in1, op, reads, writes):
            fw.pool(lambda: nc.gpsimd.tensor_tensor(out=out, in0=in0, in1=in1, op=op), reads, writes)

        cst = SB(top, [128, 8], F32)
        fw.dve(lambda: nc.vector.memset(cst[:, 0:1], EPS), [], [cst])
        fw.dve(lambda: nc.vector.memset(cst[:, 1:2], 1.0), [], [cst])
        fw.dve(lambda: nc.vector.memset(cst[:, 2:3], math.pi / 2), [], [cst])
        fw.dve(lambda: nc.vector.memset(cst[:, 3:4], 0.0), [], [cst])
        ident_f = SB(top, [128, 128], F32)
        identb = SB(top, [128, 128], BF16)
        umat = SB(top, [128, 128], BF16)
        ubar = SB(top, [128, 128], BF16)
        onesb = SB(top, [128, 128], BF16)
        onesf = SB(top, [128, 128], F32)
        mcaus = SB(top, [128, 4, 128], BF16)
        mstr = SB(top, [128, 4, 128], F32)
        halow = SB(top, [128, 5], F32)
        invf = SB(top, [64, 1], F32)
        fw.dma("sp", ident_f[:], ident_in[:], [ident_in], [ident_f])
        fw.dve(lambda: nc.vector.tensor_copy(out=identb[:], in_=ident_f[:]), [ident_f], [identb])
        fw.dma("pl", umat[:], umat_in[:], [umat_in], [umat])
        fw.dma("pl", ubar[:], ubar_in[:], [ubar_in], [ubar])
        fw.dma("pl", mcaus[:], mcaus_in[:], [mcaus_in], [mcaus])
        fw.dma("sp", mstr[:], mstr_in[:], [mstr_in], [mstr])
        fw.dma("sp", halow[:], halow_in[:], [halow_in], [halow])
        fw.dma("sp", invf[:], invf_in[:], [invf_in], [invf])
        fw.dve(lambda: nc.vector.memset(onesb[:], 1.0), [], [onesb])
        fw.dve(lambda: nc.vector.memset(onesf[:], 1.0), [], [onesf])
        modF = SB(top, [128, L, 48], F32)
        A1 = SB(top, [128, L, 8], F32)
        A2 = SB(top, [128, L, 8], F32)

        def issue_weight_bounce(l):
            for ws in gathers:
                fw.dma("sp", ws.bn[l].t, ws.sh[l].t, [ws.sh[l]], [ws.bn[l]])

        def issue_weight_cc(l):
            for ws in gathers:
                for kc in range(ws.K // 128):
                    fw.cc(lambda: nc.gpsimd.collective_compute(
                        "AllGather", ALU.bypass, replica_groups=RG,
                        ins=[ws.bn[l].t[kc * 32:(kc + 1) * 32, :].opt()],
                        outs=[ws.tl[l].t[kc * 128:(kc + 1) * 128, :].opt()]),
                        reads=[ws.bn[l]], writes=[ws.tl[l]])

        def issue_weight_gathers(l):
            issue_weight_bounce(l)
            issue_weight_cc(l)

        issue_weight_gathers(0)
        with contextlib.ExitStack() as es:
            cact = SB(es, [128, 8], F32)
            wbuf = [SB(es, [128, 8, 1536], F32) for _ in range(2)]
            baf = SB(es, [128, L, 12], F32)
            brow = SB(es, [1, L, 1536], F32)
            grow = SB(es, [1, L, 1536], F32)
            modq = SB(es, [128, L, 12], F32)
            g12 = SB(es, [128, 2, 8], F32)
            fw.dma("sp", cact[:], c_in[:], [c_in], [cact])
            actf(cact[:], cact[:], AF.Silu, [cact], [cact])
            fw.dma("sp", baf[:], b_ada_fq[:].rearrange("l p j -> p l j"), [b_ada_fq], [baf])
            fw.dma("sp", brow[:], b_ada_q[:].rearrange("(o l) n -> o l n", o=1), [b_ada_q], [brow])
            for l in range(L):
                psF = nextps()
                wb = wbuf[l % 2]
                fw.dma("sp", wb[:], w_ada_q[l][:, :].rearrange("(kc p) n -> p kc n", p=128), [w_ada_q[l]], [wb])
                for j in range(12):
                    for kc in range(8):
                        mm(psF[:, j:j + 1], wb[:, kc, j * 128:(j + 1) * 128], cact[:, kc:kc + 1],
                           kc == 0, kc == 7, [wb, cact], [psF])
                tt(modq[:, l, :], psF[:, 0:12], baf[:, l, :], ALU.add, [psF, baf], [modq])
                for n3 in range(3):
                    pr = nextps()
                    for kc in range(8):
                        mm(pr[0:1, :], cact[:, kc:kc + 1], wb[:, kc, n3 * 512:(n3 + 1) * 512],
                           kc == 0, kc == 7, [wb, cact], [pr])
                    tt(grow[:, l, n3 * 512:(n3 + 1) * 512], pr[0:1, :], brow[:, l, n3 * 512:(n3 + 1) * 512],
                       ALU.add, [pr, brow], [grow])
            fw.dma("sp", MF_i[:, :].rearrange("p (l j) -> p l j", l=L), modq[:], [modq], [MF_i])
            fw.dma("sp", GR_i[:, :].rearrange("o (l n) -> o l n", l=L), grow[:], [grow], [GR_i])
            allgather(MF_i, MF_o)
            allgather(GR_i, GR_o)
            for r4 in range(4):
                fw.dma("sp", modF[:, :, r4 * 12:(r4 + 1) * 12],
                       MF_o[r4 * 128:(r4 + 1) * 128, :].rearrange("p (l j) -> p l j", l=L), [MF_o], [modF])
            for l in range(L):
                fw.dma("sp", g12[:, 0, :], g1_in[l], [g1_in], [g12])
                fw.dma("sp", g12[:, 1, :], g2_in[l], [g2_in], [g12])
                stt(A1[:, l, :], modF[:, l, 8:16], 1.0, g12[:, 0, :], ALU.add, ALU.mult, [modF, g12], [A1])
                stt(A2[:, l, :], modF[:, l, 32:40], 1.0, g12[:, 1, :], ALU.add, ALU.mult, [modF, g12], [A2])

        fw.barrier()
        with contextlib.ExitStack() as es:
            posi = SB(es, [64, 512], I32)
            a0 = SB(es, [64, 512], F32)
            a1 = SB(es, [64, 512], F32)
            a2 = SB(es, [64, 512], F32)
            sn = SB(es, [64, 512], F32)
            cs = SB(es, [64, 512], F32)
            for t in range(TT):
                sl = slice(t * 512, (t + 1) * 512)
                fw.dma("sp", posi[:], pos[0:1, sl].partition_broadcast(64), [pos], [posi])
                fw.dve(lambda: nc.vector.tensor_copy(out=a0[:], in_=posi[:]), [posi], [a0])
                ts(a0[:], a0[:], invf[:, 0:1], None, ALU.mult, None, [a0, invf], [a0])
                ts(a1[:], a0[:], 1.0 / TWO_PI, MAGIC, ALU.mult, ALU.add, [a0], [a1])
                ts(a1[:], a1[:], -MAGIC, None, ALU.add, None, [a1], [a1])
                stt(a2[:], a1[:], -C1, a0[:], ALU.mult, ALU.add, [a1, a0], [a2])
                stt(a2[:], a1[:], -C2, a2[:], ALU.mult, ALU.add, [a1, a2], [a2])
                stt(a2[:], a1[:], -C3, a2[:], ALU.mult, ALU.add, [a1, a2], [a2])
                ts(a2[:], a2[:], -PI_LO, PI_LO, ALU.max, ALU.min, [a2], [a2])
                actf(sn[:], a2[:], AF.Sin, [a2], [sn])
                actf(a1[:], a2[:], AF.Abs, [a2], [a1])
                actf(cs[:], a1[:], AF.Sin, [a1, cst], [cs], scale=-1.0, bias=cst[0:64, 2:3])
                fw.dma("sp", CS[0, :, sl], cs[:], [cs], [CS])
                fw.dma("sp", CS[1, :, sl], sn[:], [sn], [CS])

        fw.barrier()
        if stop_after == "p0":
            fw.finish()
            return nc

        def norm_hT(xt, sq, ss, rstd, xn, A_ap, sh_ap, hT, dep):
            actf(sq[:], xt[:], AF.Square, [xt], [sq])
            fw.dve(lambda: nc.vector.tensor_reduce(out=ss[:], in_=sq[:], axis=mybir.AxisListType.X, op=ALU.add),
                   [sq], [ss])
            actf(rstd[:], ss[:], AF.Sqrt, [ss, cst], [rstd], scale=1.0 / D, bias=cst[:, 0:1])
            fw.dve(lambda: nc.vector.reciprocal(out=rstd[:], in_=rstd[:]), [rstd], [rstd])
            for b in range(4):
                ts(xn[:, b, :], xt[:, b, :], rstd[:, b:b + 1], None, ALU.mult, None, [xt, rstd], [xn])
            for kc in range(8):
                p = pt[kc % 2]
                for b in range(4):
                    fw.pe(lambda: nc.tensor.transpose(out=p[:, b * 128:(b + 1) * 128],
                                                      in_=xn[:, b, kc * 128:(kc + 1) * 128], identity=identb[:]),
                          [xn, identb], [p])
                actf(hT[:, kc, :], p[:, 0:512], AF.Identity, [p] + dep, [hT], scale=A_ap[:, kc:kc + 1], bias=sh_ap[:, kc:kc + 1])

        def chk(name):
            if stop_after == name:
                fw.stopped = True

        try:
          for l in range(L):
            x_src = x_in if l == 0 else XA
            x_dst = y_out if l == L - 1 else XA

            with contextlib.ExitStack() as es:
                w1 = SB(es, [128, 8, DIN + 64], BF16)
                wq = SB(es, [128, 2, 768 + 256], BF16)
                wkv = SB(es, [128, 2, 512], BF16)
                bg = SB(es, [128, NG], F32)
                bvs = SB(es, [128, 512], F32)
                gql = SB(es, [128, 2], F32)
                gkvl = SB(es, [128, 1], F32)
                gqh = SB(es, [128, 3], F32)
                gkh = SB(es, [128, 3], F32)
                for kc in range(8):
                    fw.dma("pl", w1[:, kc, 0:DIN], w_in[l, kc * 128:(kc + 1) * 128, :], [w_in.tl[l]], [w1])
                for kc in range(2):
                    fw.dma("pl", wq[:, kc, 0:768], w_uq[l, kc * 128:(kc + 1) * 128, :], [w_uq.tl[l]], [wq])
                for t2 in range(2):
                    fw.dma("pl", wkv[:, t2, :].rearrange("p (h d) -> p h d", h=4),
                           w_ukv[l].rearrange("p (h t d) -> p t h d", h=4, t=2)[:, t2], [w_ukv.tl[l]], [wkv])
                fw.dma("sp", bg[:], b_in_g[l], [b_in_g], [bg])
                fw.dma("sp", bvs[:], b_in[l:l + 1, 1472:1984].partition_broadcast(128), [b_in], [bvs])
                fw.dma("sp", gql[:], gql_in[l], [gql_in], [gql])
                fw.dma("sp", gkvl[:], gkvl_in[l], [gkvl_in], [gkvl])
                fw.dma("sp", gqh[:], gqh_in[l], [gqh_in], [gqh])
                fw.dma("sp", gkh[:], gkh_in[l], [gkh_in], [gkh])
                actf(w1[:, :, DIN:DIN + 32], w1[:, :, 416:448], AF.Identity, [w1], [w1], scale=-1.0)
                fw.act(lambda: nc.scalar.copy(out=w1[:, :, DIN + 32:DIN + 64], in_=w1[:, :, 384:416]), [w1], [w1])
                fw.dve(lambda: nc.vector.tensor_scalar(out=bg[0:32, 4:5], in0=bg[0:32, 4:5], scalar1=-1.0, scalar2=None,
                                                       op0=ALU.mult), [bg], [bg])
                for h in range(4):
                    actf(wq[:, :, 768 + h * 64:768 + h * 64 + 32], wq[:, :, h * 192 + 160:h * 192 + 192], AF.Identity, [wq], [wq], scale=-1.0)
                    fw.act(lambda: nc.scalar.copy(out=wq[:, :, 768 + h * 64 + 32:768 + h * 64 + 64],
                                                  in_=wq[:, :, h * 192 + 128:h * 192 + 160]), [wq], [wq])

                chk("p1w")
                xt = SB(es, [128, 4, D], F32)
                sq = SB(es, [128, 4, D], F32)
                xn = SB(es, [128, 4, D], BF16)
                ss = SB(es, [128, 4], F32)
                rstd = SB(es, [128, 4], F32)
                hT = SB(es, [128, 8, 512], BF16)
                cos2 = SB(es, [64, 512], F32)
                sin2 = SB(es, [64, 512], F32)
                qg = SB(es, [128, 2, 512], BF16)
                sqq = SB(es, [128, 2, 512], BF16)
                epsq = SB(es, [128, 512], F32)
                kvg = SB(es, [128, 512], BF16)
                sqkv = SB(es, [128, 512], BF16)
                rk2 = SB(es, [128, 512], F32)
                rkb = SB(es, [128, 512], F32)
                rktok = SB(es, [128, 4], F32)
                kr = SB(es, [64, 512], F32)
                krot = SB(es, [64, 512], F32)
                sqkr = SB(es, [64, 512], BF16)
                s2sb = SB(es, [128, 512], F32)
                krr = SB(es, [64, 512], F32)
                tmpa_t = [SB(es, [128, 512], F32) for _ in range(2)]
                tmpb_t = [SB(es, [128, 512], F32) for _ in range(2)]
                rsh_t = [SB(es, [128, 512], F32) for _ in range(2)]
                sqn_t = [SB(es, [128, 512], BF16) for _ in range(2)]
                sqr_t = [SB(es, [64, 512], BF16) for _ in range(2)]
                tmpa, tmpb, rsh, sqn, sqr = tmpa_t[0], tmpb_t[0], rsh_t[0], sqn_t[0], sqr_t[0]
                ob = [SB(es, [128, 512], BF16) for _ in range(4)]
                obr = [SB(es, [64, 512], BF16) for _ in range(4)]
                gat = SB(es, [128, 16, 512], BF16)
                vt = SB(es, [128, 4, 512], BF16)
                obk = [0]

                def nob():
                    obk[0] += 1
                    return ob[obk[0] % 4]

                def nobr():
                    obk[0] += 1
                    return obr[obk[0] % 4]

                def inproj(gi):
                    col0, M = GROUPS[gi]
                    p = nextps()
                    for kc in range(8):
                        mm(p[0:M, :], w1[:, kc, col0:col0 + M], hT[:, kc, :], kc == 0, kc == 7, [w1, hT], [p])
                    return p

                for t in range(TT):
                    sl = slice(t * 512, (t + 1) * 512)
                    fw.dma("sp", xt[:], x_src[sl, :].rearrange("(b p) d -> p b d", p=128), [x_src], [xt])
                    fw.dma("sp", cos2[:], CS[0, :, sl], [CS], [cos2])
                    fw.dma("sp", sin2[:], CS[1, :, sl], [CS], [sin2])
                    norm_hT(xt, sq, ss, rstd, xn, A1[:, l, :], modF[:, l, 0:8], hT, [A1, modF])

                    chk("p1n")
                    for j in range(2):
                        p = inproj(j)
                        ts(qg[:, j, :], p[:], bg[:, j:j + 1], gql[:, j:j + 1], ALU.add, ALU.mult, [p, bg, gql], [qg])
                        actf(sqq[:, j, :], p[:], AF.Square, [p, bg], [sqq], bias=bg[:, j:j + 1])
                    p = nextps()
                    for j in range(2):
                        mm(p[:], onesb[:], sqq[:, j, :], j == 0, j == 1, [onesb, sqq], [p])
                    ts(epsq[:], p[:], EPS / 256.0, EPS * EPS, ALU.mult, ALU.add, [p], [epsq])
                    p = inproj(2)
                    ts(kvg[:], p[:], bg[:, 2:3], gkvl[:, 0:1], ALU.add, ALU.mult, [p, bg, gkvl], [kvg])
                    actf(sqkv[:], p[:], AF.Square, [p, bg], [sqkv], bias=bg[:, 2:3])
                    p = nextps()
                    mm(p[:], onesb[:], sqkv[:], True, True, [onesb, sqkv], [p])
                    ts(rk2[:], p[:], 1.0 / 128.0, EPS, ALU.mult, ALU.add, [p], [rk2])
                    fw.dve(lambda: nc.vector.reciprocal(out=rk2[:], in_=rk2[:]), [rk2], [rk2])
                    actf(rkb[:], rk2[:], AF.Sqrt, [rk2], [rkb])
                    p = nextps()
                    for b in range(4):
                        mm(p[:, b:b + 1], sqkv[:, b * 128:(b + 1) * 128], onesb[:, 0:1], True, True, [sqkv, onesb], [p])
                    actf(rktok[:], p[:, 0:4], AF.Sqrt, [p, cst], [rktok], scale=1.0 / 128.0, bias=cst[:, 0:1])
                    fw.dve(lambda: nc.vector.reciprocal(out=rktok[:], in_=rktok[:]), [rktok], [rktok])
                    p = inproj(3)
                    actf(kr[:], p[0:64, :], AF.Identity, [p, bg], [kr], bias=bg[0:64, 3:4])
                    actf(sqkr[:], p[0:64, :], AF.Square, [p, bg], [sqkr], bias=bg[0:64, 3:4])
                    p = inproj(4)
                    actf(krot[:], p[0:64, :], AF.Identity, [p, bg], [krot], bias=bg[0:64, 4:5])
                    p = nextps()
                    mm(p[:], onesb[0:64, :], sqkr[:], True, True, [onesb, sqkr], [p])
                    actf(s2sb[:], p[:], AF.Identity, [p], [s2sb])
                    stt(krr[:], kr[:], gkh[0:64, 1:2], cos2[:], ALU.mult, ALU.mult, [kr, gkh, cos2], [krr])
                    stt(tmpa[0:64, :], krot[:], gkh[0:64, 2:3], sin2[:], ALU.mult, ALU.mult, [krot, gkh, sin2], [tmpa])
                    tt(krr[:], krr[:], tmpa[0:64, :], ALU.add, [krr, tmpa], [krr])

                    chk("p1l")
                    for h in range(4):
                        tmpa, tmpb, rsh, sqn, sqr = tmpa_t[h % 2], tmpb_t[h % 2], rsh_t[h % 2], sqn_t[h % 2], sqr_t[h % 2]
                        pn = nextps()
                        for kc in range(2):
                            mm(pn[:], wq[:, kc, h * 192:h * 192 + 128], qg[:, kc, :], kc == 0, kc == 1, [wq, qg], [pn])
                        pr = nextps()
                        for kc in range(2):
                            mm(pr[0:64, :], wq[:, kc, h * 192 + 128:h * 192 + 192], qg[:, kc, :], kc == 0, kc == 1, [wq, qg], [pr])
                        pro = nextps()
                        for kc in range(2):
                            mm(pro[0:64, :], wq[:, kc, 768 + h * 64:768 + h * 64 + 64], qg[:, kc, :], kc == 0, kc == 1, [wq, qg], [pro])
                        actf(sqn[:], pn[:], AF.Square, [pn], [sqn])
                        actf(sqr[:], pr[0:64, :], AF.Square, [pr], [sqr])
                        pss = nextps()
                        mm(pss[:], onesb[:], sqn[:], True, False, [onesb, sqn], [pss])
                        mm(pss[:], onesb[0:64, :], sqr[:], False, True, [onesb, sqr], [pss])
                        stt(tmpa[:], pss[:], 1.0 / 192.0, epsq[:], ALU.mult, ALU.add, [pss, epsq], [tmpa])
                        actf(rsh[:], tmpa[:], AF.Sqrt, [tmpa], [rsh])
                        fw.dve(lambda: nc.vector.reciprocal(out=rsh[:], in_=rsh[:]), [rsh], [rsh])
                        o = nob()
                        stt(o[:], pn[:], gqh[:, 0:1], rsh[:], ALU.mult, ALU.mult, [pn, gqh, rsh], [o])
                        fw.dma("sp", QTn[h, :, sl], o[:], [o], [QTn])
                        stt(tmpa[0:64, :], pr[0:64, :], gqh[0:64, 1:2], cos2[:], ALU.mult, ALU.mult, [pr, gqh, cos2], [tmpa])
                        stt(tmpb[0:64, :], pro[0:64, :], gqh[0:64, 2:3], sin2[:], ALU.mult, ALU.mult, [pro, gqh, sin2], [tmpb])
                        tt(tmpa[0:64, :], tmpa[0:64, :], tmpb[0:64, :], ALU.add, [tmpa, tmpb], [tmpa])
                        orr = nobr()
                        tt(orr[:], tmpa[0:64, :], rsh[0:64, :], ALU.mult, [tmpa, rsh], [orr])
                        fw.dma("sp", QTr[h, :, sl], orr[:], [orr], [QTr])

                    chk("p1q")
                    for h in range(4):
                        tmpa, tmpb, rsh, sqn, sqr = tmpa_t[h % 2], tmpb_t[h % 2], rsh_t[h % 2], sqn_t[h % 2], sqr_t[h % 2]
                        pn = nextps()
                        mm(pn[:], wkv[:, 0, h * 128:(h + 1) * 128], kvg[:], True, True, [wkv, kvg], [pn])
                        actf(sqn[:], pn[:], AF.Square, [pn], [sqn])
                        pss = nextps()
                        mm(pss[:], onesb[:], sqn[:], True, True, [onesb, sqn], [pss])
                        tt(tmpa[:], pss[:], rk2[:], ALU.mult, [pss, rk2], [tmpa])
                        tt(tmpa[:], tmpa[:], s2sb[:], ALU.add, [tmpa, s2sb], [tmpa])
                        ts(tmpa[:], tmpa[:], 1.0 / 192.0, EPS, ALU.mult, ALU.add, [tmpa], [tmpa])
                        actf(rsh[:], tmpa[:], AF.Sqrt, [tmpa], [rsh])
                        fw.dve(lambda: nc.vector.reciprocal(out=rsh[:], in_=rsh[:]), [rsh], [rsh])
                        tt(tmpb[:], rsh[:], rkb[:], ALU.mult, [rsh, rkb], [tmpb])
                        o = nob()
                        stt(o[:], pn[:], gkh[:, 0:1], tmpb[:], ALU.mult, ALU.mult, [pn, gkh, tmpb], [o])
                        fw.dma("sp", KTn_i[h][:, sl], o[:], [o], [KTn_i[h]])
                        orr = nobr()
                        tt(orr[:], krr[:], rsh[0:64, :], ALU.mult, [krr, rsh], [orr])
                        fw.dma("sp", KTr_i[h][:, sl], orr[:], [orr], [KTr_i[h]])
                    for b in range(4):
                        p = nextps()
                        mm(p[:], kvg[:, b * 128:(b + 1) * 128], wkv[:, 1, :], True, True, [kvg, wkv], [p])
                        ts(vt[:, b, :], p[:], rktok[:, b:b + 1], None, ALU.mult, None, [p, rktok], [vt])
                    for h in range(4):
                        fw.dma("sp", VM_i[h][t * 128:(t + 1) * 128, :].rearrange("p (b d) -> p b d", b=4),
                               vt[:, :, h * 128:(h + 1) * 128], [vt], [VM_i[h]])
                    chk("p1k")
                    for h in range(4):
                        p = inproj(5 + h)
                        o = nob()
                        actf(o[:], p[:], AF.Identity, [p, bg], [o], bias=bg[:, 5 + h:6 + h])
                        fw.dma("sp", QTs[h, :, sl], o[:], [o], [QTs])
                    for h in range(4):
                        p = inproj(9 + h)
                        o = nob()
                        actf(o[:], p[:], AF.Identity, [p, bg], [o], bias=bg[:, 9 + h:10 + h])
                        fw.dma("sp", KTs_i[h][:, sl], o[:], [o], [KTs_i[h]])
                    for b in range(4):
                        p = nextps()
                        for kc in range(8):
                            mm(p[:], hT[:, kc, b * 128:(b + 1) * 128], w1[:, kc, 1472:1984], kc == 0, kc == 7, [hT, w1], [p])
                        tt(vt[:, b, :], p[:], bvs[:], ALU.add, [p, bvs], [vt])
                    for h in range(4):
                        fw.dma("sp", VS_i[h][t * 128:(t + 1) * 128, :].rearrange("p (b d) -> p b d", b=4),
                               vt[:, :, h * 128:(h + 1) * 128], [vt], [VS_i[h]])
                    chk("p1s")
                    for j in range(16):
                        p = inproj(13 + j)
                        actf(gat[:, j, :], p[:], AF.Sigmoid, [p, bg], [gat], bias=bg[:, 13 + j:14 + j])
                    fw.dma("sp", GA[t].rearrange("p (j n) -> p j n", j=16), gat[:], [gat], [GA])

            fw.barrier()
            chk("p1")
            for h in range(4):
                allgather(KTn_i[h], KTn_o[h])
                allgather(KTr_i[h], KTr_o[h])
                allgather(VM_i[h], VM_o[h])
            for h in range(4):
                allgather(KTs_i[h], KTs_o[h])
                allgather(VS_i[h], VS_o[h])

            chk("ag")
            if l + 1 < L:
                issue_weight_bounce(l + 1)
            with contextlib.ExitStack() as es:
                NM, NS = 6, 8
                qn_t = [SB(es, [128, 512], BF16) for _ in range(2)]
                qr_t = [SB(es, [64, 512], BF16) for _ in range(2)]
                mkn_t = [SB(es, [128, 512], BF16) for _ in range(NM)]
                mkr_t = [SB(es, [64, 512], BF16) for _ in range(NM)]
                mvv_t = [SB(es, [128, 512], BF16) for _ in range(NM)]
                NP = 3
                pT_t = [SB(es, [128, 512], BF16) for _ in range(NP)]
                pacc = SB(es, [128, 512], F32)
                rinv = SB(es, [128, 512], F32)
                ytm_t = [SB(es, [128, 512], BF16) for _ in range(2)]
                NE = 3
                sc_ps = [ps[0], ps[1], ps[2]]
                oacc_m = ps[7]
                cnt = {"qm": 0, "km": 0, "sm": 0, "ym": 0, "sc": 0}

                def next_sc():
                    cnt["sc"] += 1
                    return sc_ps[cnt["sc"] % 3]

                class SBStream:
                    def __init__(self, rps, oacc, zb):
                        self.rps, self.oacc, self.zb = rps, oacc, zb
                        self.qs_t = [SB(es, [128, 512], BF16) for _ in range(2)]
                        self.kn_t = [SB(es, [128, 512], BF16) for _ in range(NS)]
                        self.vv_t = [SB(es, [128, 512], BF16) for _ in range(NS)]
                        self.e_t = [SB(es, [128, 512], F32) for _ in range(NE)]
                        self.lp_t = [SB(es, [128, 512], F32) for _ in range(NE)]
                        self.hi_t = [SB(es, [128, 512], BF16) for _ in range(NE)]
                        self.lo_t = [SB(es, [128, 512], BF16) for _ in range(NE)]
                        self.ex_t = [SB(es, [128, 512], F32) for _ in range(NE)]
                        self.a_t = [SB(es, [128, 512], BF16) for _ in range(NE)]
                        self.yt_t = [SB(es, [128, 512], BF16) for _ in range(2)]
                        self.nq = self.nk = self.nsx = self.ny = 0

                stA = SBStream(ps[3], ps[5], ps[0])
                stB = SBStream(ps[4], ps[6], ps[1])

                def mla_sweep(g, h):
                    gsl = slice(g * 512, (g + 1) * 512)
                    qn = qn_t[cnt["qm"] % 2]
                    qr = qr_t[cnt["qm"] % 2]
                    cnt["qm"] += 1
                    fw.dma("sp", qn[:], QTn[h, :, gsl], [QTn], [qn])
                    fw.dma("sp", qr[:], QTr[h, :, gsl], [QTr], [qr])
                    fw.pool(lambda: nc.gpsimd.memset(pacc[:], 0.0), [], [pacc])
                    first = True
                    for r in range(4):
                        for ch in range(g + 1):
                            i = cnt["km"] % NM
                            cnt["km"] += 1
                            kn, krt, vv = mkn_t[i], mkr_t[i], mvv_t[i]
                            csl = slice(ch * 512, (ch + 1) * 512)
                            fw.dma("sp", kn[:], KTn_o[h][r * 128:(r + 1) * 128, csl], [KTn_o[h]], [kn])
                            fw.dma("sp", krt[:], KTr_o[h][r * 64:(r + 1) * 64, csl], [KTr_o[h]], [krt])
                            vrow = r * (TT * 128) + ch * 128
                            fw.dma("sp", vv[:], VM_o[h][vrow:vrow + 128, :], [VM_o[h]], [vv])
                            tail = (ch == g)
                            for jb in range(4):
                                c0 = jb * 128 if tail else 0
                                last = (r == 3 and ch == g and jb == 3)
                                s_ = ps[2]
                                pT = pT_t[cnt["sm"] % NP]
                                cnt["sm"] += 1
                                ksl = slice(jb * 128, (jb + 1) * 128)
                                mm(s_[:, c0:], kn[:, ksl], qn[:, c0:], True, False, [kn, qn], [s_])
                                mm(s_[:, c0:], krt[:, ksl], qr[:, c0:], False, True, [krt, qr], [s_])
                                yield
                                actf(pT[:, c0:], s_[:, c0:], AF.Exp, [s_], [pT], scale=SC_MLA)
                                if tail:
                                    ptt(pT[:, c0:c0 + 128], pT[:, c0:c0 + 128], mcaus[:, r, :], ALU.mult, [pT, mcaus], [pT])
                                ptt(pacc[:, c0:], pacc[:, c0:], pT[:, c0:], ALU.add, [pacc, pT], [pacc])
                                mm(oacc_m[:, c0:], vv[:, ksl], pT[:, c0:], first, last, [vv, pT], [oacc_m])
                                first = False
                                yield
                    prs = ps[2]
                    mm(prs[:], onesf[:], pacc[:], True, True, [onesf, pacc], [prs])
                    fw.dve(lambda: nc.vector.reciprocal(out=rinv[:], in_=prs[:]), [prs], [rinv])
                    yt = ytm_t[cnt["ym"] % 2]
                    cnt["ym"] += 1
                    tt(yt[:], oacc_m[:], rinv[:], ALU.mult, [oacc_m, rinv], [yt])
                    fw.dma("pl", YTm[h, :, gsl], yt[:], [yt], [YTm])

                def sb_sweep(g, h, st):
                    gsl = slice(g * 512, (g + 1) * 512)
                    rps, oacc_s = st.rps, st.oacc
                    qs = st.qs_t[st.nq % 2]
                    st.nq += 1
                    fw.dma("sp", qs[:], QTs[h, :, gsl], [QTs], [qs])
                    sfirst = True
                    for ch in range(g, -1, -1):
                        csl = slice(ch * 512, (ch + 1) * 512)
                        kss, vss = [], []
                        for r in range(4):
                            i = st.nk % NS
                            st.nk += 1
                            kn, vv = st.kn_t[i], st.vv_t[i]
                            fw.dma("sp", kn[:], KTs_o[h][r * 128:(r + 1) * 128, csl], [KTs_o[h]], [kn])
                            vrow = r * (TT * 128) + ch * 128
                            fw.dma("sp", vv[:], VS_o[h][vrow:vrow + 128, :], [VS_o[h]], [vv])
                            kss.append(kn)
                            vss.append(vv)
                        tail = (ch == g)
                        for jb in range(3, -1, -1):
                            c0 = jb * 128 if tail else 0
                            ksl = slice(jb * 128, (jb + 1) * 128)
                            for r in range(3, -1, -1):
                                last = (ch == 0 and jb == 0 and r == 0)
                                k3 = st.nsx % NE
                                st.nsx += 1
                                z = st.zb
                                e, lp, hi, lo, ex, aa = st.e_t[k3], st.lp_t[k3], st.hi_t[k3], st.lo_t[k3], st.ex_t[k3], st.a_t[k3]
                                mm(z[:, c0:], kss[r][:, ksl], qs[:, c0:], True, True, [kss[r], qs], [z])
                                yield
                                actf(e[:, c0:], z[:, c0:], AF.Exp, [z], [e], scale=SC_SB)
                                actf(lp[:, c0:], e[:, c0:], AF.Ln, [e, cst], [lp], bias=cst[:, 1:2])
                                if tail:
                                    tt(lp[:, c0:c0 + 128], lp[:, c0:c0 + 128], mstr[:, r, :], ALU.mult, [lp, mstr], [lp])
                                    ptt(e[:, c0:c0 + 128], e[:, c0:c0 + 128], mstr[:, r, :], ALU.mult, [e, mstr], [e])
                                fw.dve(lambda: nc.vector.tensor_copy(out=hi[:, c0:], in_=lp[:, c0:]), [lp], [hi])
                                tt(lo[:, c0:], lp[:, c0:], hi[:, c0:], ALU.subtract, [lp, hi], [lo])
                                yield
                                mm(rps[:, c0:], umat[:], hi[:, c0:], sfirst, False, [umat, hi], [rps])
                                mm(rps[:, c0:], umat[:], lo[:, c0:], False, True, [umat, lo], [rps])
                                yield
                                actf(ex[:, c0:], rps[:, c0:], AF.Exp, [rps], [ex], scale=-1.0)
                                mm(rps[:, c0:], ubar[:], hi[:, c0:], False, False, [ubar, hi], [rps])
                                mm(rps[:, c0:], ubar[:], lo[:, c0:], False, True, [ubar, lo], [rps])
                                tt(aa[:, c0:], e[:, c0:], ex[:, c0:], ALU.mult, [e, ex], [aa])
                                yield
                                mm(oacc_s[:, c0:], vss[r][:, ksl], aa[:, c0:], sfirst, last, [vss[r], aa], [oacc_s])
                                sfirst = False
                                yield
                    yt = st.yt_t[st.ny % 2]
                    st.ny += 1
                    actf(yt[:], oacc_s[:], AF.Identity, [oacc_s], [yt])
                    fw.dma("pl", YTs[h, :, gsl], yt[:], [yt], [YTs])

                def chain(*gens):
                    for gen in gens:
                        yield from gen

                for g in range(TT):
                    for hp in range(2):
                        if l + 1 < L and g == min(1, TT - 1) and hp == 1:
                            issue_weight_cc(l + 1)
                        h0, h1 = 2 * hp, 2 * hp + 1
                        gens = [sb_sweep(g, h0, stA), chain(mla_sweep(g, h0), mla_sweep(g, h1)), sb_sweep(g, h1, stB)]
                        done = [False, False, False]
                        next(gens[0])
                        next(gens[0])
                        kk = 0
                        while not all(done):
                            kk += 1
                            order = [0, 2] if kk % 5 == 0 else [0, 1, 2]
                            for gi in order:
                                if not done[gi]:
                                    try:
                                        next(gens[gi])
                                    except StopIteration:
                                        done[gi] = True
            fw.barrier()
            chk("p2")
            with contextlib.ExitStack() as es:
                wbm = SB(es, [128, 4, D], BF16)
                wbs = SB(es, [128, 4, D], BF16)
                wo = SB(es, [128, 8, D], BF16)
                gtb = SB(es, [128, D], F32)
                for kc in range(4):
                    fw.dma("pl", wbm[:, kc, :], w_bm[l, kc * 128:(kc + 1) * 128, :], [w_bm.tl[l]], [wbm])
                    fw.dma("pl", wbs[:, kc, :], w_bs[l, kc * 128:(kc + 1) * 128, :], [w_bs.tl[l]], [wbs])
                for kc in range(8):
                    fw.dma("pl", wo[:, kc, :], w_out[l, kc * 128:(kc + 1) * 128, :], [w_out.tl[l]], [wo])
                fw.dma("sp", gtb[:], GR_o[1:2, l * 1536 + 512:(l + 1) * 1536].partition_broadcast(128), [GR_o], [gtb])
                ym = SB(es, [128, 4, 512], BF16)
                ysb = SB(es, [128, 4, 512], BF16)
                gat = SB(es, [128, 16, 512], BF16)
                xt = SB(es, [128, 4, D], F32)
                sq = SB(es, [128, 4, D], F32)
                xn = SB(es, [128, 4, D], BF16)
                ss = SB(es, [128, 4], F32)
                rstd = SB(es, [128, 4], F32)
                mg = SB(es, [128, 8, 512], BF16)
                m1 = SB(es, [128, 512], F32)
                m2 = SB(es, [128, 512], F32)
                hT = SB(es, [128, 8, 512], BF16)
                hb = SB(es, [128, 8, NB, 2], BF16)
                for t in range(TT):
                    sl = slice(t * 512, (t + 1) * 512)
                    fw.dma("sp", ym[:], YTm[:, :, sl].rearrange("h p n -> p h n"), [YTm], [ym])
                    fw.dma("sp", ysb[:], YTs[:, :, sl].rearrange("h p n -> p h n"), [YTs], [ysb])
                    fw.dma("sp", gat[:], GA[t].rearrange("p (j n) -> p j n", j=16), [GA], [gat])
                    fw.dma("sp", xt[:], x_src[sl, :].rearrange("(b p) d -> p b d", p=128), [x_src], [xt])
                    for oc in range(8):
                        pa = nextps()
                        for kc in range(4):
                            mm(pa[:], wbm[:, kc, oc * 128:(oc + 1) * 128], ym[:, kc, :], kc == 0, kc == 3, [wbm, ym], [pa])
                        pb = nextps()
                        for kc in range(4):
                            mm(pb[:], wbs[:, kc, oc * 128:(oc + 1) * 128], ysb[:, kc, :], kc == 0, kc == 3, [wbs, ysb], [pb])
                        tt(m1[:], pa[:], gat[:, oc, :], ALU.mult, [pa, gat], [m1])
                        tt(m2[:], pb[:], gat[:, 8 + oc, :], ALU.mult, [pb, gat], [m2])
                        tt(mg[:, oc, :], m1[:], m2[:], ALU.add, [m1, m2], [mg])
                    for b in range(4):
                        for half in range(2):
                            p = nextps()
                            for kc in range(8):
                                mm(p[:], mg[:, kc, b * 128:(b + 1) * 128], wo[:, kc, half * 512:(half + 1) * 512],
                                   kc == 0, kc == 7, [mg, wo], [p])
                            hs = slice(half * 512, (half + 1) * 512)
                            tt(m1[:], p[:], gtb[:, hs], ALU.mult, [p, gtb], [m1])
                            tt(xt[:, b, hs], xt[:, b, hs], m1[:], ALU.add, [xt, m1], [xt])
                    fw.dma("sp", X1[sl, :].rearrange("(b p) d -> p b d", p=128), xt[:], [xt], [X1])
                    norm_hT(xt, sq, ss, rstd, xn, A2[:, l, :], modF[:, l, 24:32], hT, [A2, modF])
                    fw.dma("sp", H2[t].rearrange("p (k n) -> p k n", k=8), hT[:], [hT], [H2])
                    for b in range(4):
                        fw.act(lambda: nc.scalar.copy(out=hb[:, :, t * 4 + b, :], in_=hT[:, :, b * 128 + 126:b * 128 + 128]),
                               [hT], [hb])
                fw.dma("sp", HB_i[:, :].rearrange("p (k m e) -> p k m e", k=8, m=NB), hb[:], [hb], [HB_i])
            fw.barrier()
            chk("p3")
            allgather(HB_i, HB_o)

            with contextlib.ExitStack() as es:
                wu = SB(es, [128, 8, 2 * DFF], BF16)
                wd = SB(es, [128, 22, D], BF16)
                wcv = SB(es, [128, 44, 3], F32)
                bcv = SB(es, [128, 44], F32)
                gtb = SB(es, [128, D], F32)
                for kc in range(8):
                    fw.dma("pl", wu[:, kc, :], w_up[l, kc * 128:(kc + 1) * 128, :], [w_up.tl[l]], [wu])
                for kc in range(22):
                    fw.dma("pl", wd[:, kc, :], w_down[l, kc * 128:(kc + 1) * 128, :], [w_down.tl[l]], [wd])
                fw.dma("sp", wcv[:], wconv_f[l], [wconv_f], [wcv])
                fw.dma("sp", bcv[:], bconv_f[l], [bconv_f], [bcv])
                fw.dma("sp", gtb[:], GR_o[3:4, l * 1536 + 512:(l + 1) * 1536].partition_broadcast(128), [GR_o], [gtb])
                hT = SB(es, [128, 8, 512], BF16)
                hselb = SB(es, [128, 8, NB, 2], BF16)
                uek = [0]
                cva = [SB(es, [128, 4, 128], F32) for _ in range(2)]
                sil = SB(es, [128, 4, 128], F32)
                ff = SB(es, [128, 22, 512], BF16)
                xt = SB(es, [128, 4, D], F32)
                m1 = SB(es, [128, 512], F32)
                es2 = contextlib.ExitStack()
                hcand = SB(es2, [128, 4, 8, NB, 2], BF16)
                hsel = SB(es2, [128, 8, NB, 2], F32)
                fw.dma("sp", hcand[:].rearrange("p r k m e -> p r (k m e)"),
                       HB_o[:, :].rearrange("(r p) n -> p r n", p=128), [HB_o], [hcand])
                ts(hsel[:], hcand[:, 0], halow[:, 0:1], None, ALU.mult, None, [hcand, halow], [hsel])
                for r in range(1, 4):
                    stt(hsel[:], hcand[:, r], halow[:, r:r + 1], hsel[:], ALU.mult, ALU.add, [hcand, halow, hsel], [hsel])
                if NB > 1:
                    stt(hsel[:, :, 1:NB, :], hcand[:, 3, :, 0:NB - 1, :], halow[:, 4:5], hsel[:, :, 1:NB, :],
                        ALU.mult, ALU.add, [hcand, halow, hsel], [hsel])
                fw.dve(lambda: nc.vector.tensor_copy(out=hselb[:], in_=hsel[:]), [hsel], [hselb])
                fw.barrier()
                es2.close()
                ue_t = [SB(es, [128, 4, 130], F32) for _ in range(2)]
                for t in range(TT):
                    sl = slice(t * 512, (t + 1) * 512)
                    fw.dma("sp", hT[:], H2[t].rearrange("p (k n) -> p k n", k=8), [H2], [hT])
                    fw.dma("sp", xt[:], X1[sl, :].rearrange("(b p) d -> p b d", p=128), [X1], [xt])
                    for j in range(22):
                        for which, oc in ((0, j), (1, 22 + j)):
                            p = nextps()
                            for kc in range(8):
                                mm(p[:], wu[:, kc, oc * 128:(oc + 1) * 128], hT[:, kc, :], kc == 0, kc == 7, [wu, hT], [p])
                            ph = nextps()
                            for kc in range(8):
                                mm(ph[:, 0:8], wu[:, kc, oc * 128:(oc + 1) * 128], hselb[:, kc, t * 4:(t + 1) * 4, :],
                                   kc == 0, kc == 7, [wu, hselb], [ph])
                            ue = ue_t[uek[0] % 2]
                            uek[0] += 1
                            cv = cva[which]
                            actf(ue[:, :, 2:130], p[:].rearrange("p (b n) -> p b n", b=4), AF.Identity, [p], [ue])
                            actf(ue[:, :, 0:2], ph[:, 0:8].rearrange("p (b e) -> p b e", b=4), AF.Identity, [ph], [ue])
                            actf(cv[:], p[:].rearrange("p (b n) -> p b n", b=4), AF.Identity, [p, wcv, bcv], [cv],
                                 scale=wcv[:, oc, 2:3], bias=bcv[:, oc:oc + 1])
                            stt(cv[:], ue[:, :, 1:129], wcv[:, oc, 1:2], cv[:], ALU.mult, ALU.add, [ue, wcv, cv], [cv])
                            stt(cv[:], ue[:, :, 0:128], wcv[:, oc, 0:1], cv[:], ALU.mult, ALU.add, [ue, wcv, cv], [cv])
                        actf(sil[:], cva[0][:], AF.Silu, [cva[0]], [sil])
                        tt(ff[:, j, :].rearrange("p (b n) -> p b n", b=4), sil[:], cva[1][:], ALU.mult, [sil, cva[1]], [ff])
                    for b in range(4):
                        for half in range(2):
                            p = nextps()
                            for kc in range(22):
                                mm(p[:], ff[:, kc, b * 128:(b + 1) * 128], wd[:, kc, half * 512:(half + 1) * 512],
                                   kc == 0, kc == 21, [ff, wd], [p])
                            hs = slice(half * 512, (half + 1) * 512)
                            tt(m1[:], p[:], gtb[:, hs], ALU.mult, [p, gtb], [m1])
                            tt(xt[:, b, hs], xt[:, b, hs], m1[:], ALU.add, [xt, m1], [xt])
                    fw.dma("sp", x_dst[sl, :].rearrange("(b p) d -> p b d", p=128), xt[:], [xt], [x_dst])
            fw.barrier()
        except StopBuild:
            pass
        fw.stopped = False
        for name in dump:
            t, shape, dt = scr[name]
            o = Tl(nc.dram_tensor("dbg_" + name, shape, dt, kind="ExternalOutput").ap())
            fw.dma("sp", o.t, t.t, [t], [o])
        fw.finish()
    if os.environ.get("KVERBOSE"):
        print("instr counts: pe", fw.sem_pe.count, "act", fw.sem_act.count, "dve", fw.sem_dve.count, "pool", fw.sem_pool.count,
              "cc", fw.sem_cc.count, "dma_sp", fw.n_sp, "dma_pl", fw.n_pl, "total(with waits)", fw.n_ins, flush=True)
    return nc


def _prep_inputs(inp, S, L):
    B = inp["x"].shape[0]
    NB = S // 512
    f32 = np.float32
    ident = np.eye(128, dtype=f32)
    kk = np.arange(128)
    umat = (kk[:, None] >= kk[None, :]).astype(f32)
    ubar = (kk[:, None] < kk[None, :]).astype(f32)
    tri_le = (kk[:, None] <= kk[None, :]).astype(f32)
    tri_lt = (kk[:, None] < kk[None, :]).astype(f32)
    invf = (1.0 / (10000.0 ** (np.arange(32, dtype=f32) * f32(2.0 / 64)))).astype(f32)
    invf2 = np.concatenate([invf, invf]).reshape(64, 1).astype(f32)

    def fm(v, n):
        return np.ascontiguousarray(v.reshape(L, n, 128).transpose(0, 2, 1))

    b_in = np.asarray(inp["b_in"], f32)
    big = np.zeros((L, 128, NG), f32)
    for gi, (c0, M) in enumerate(GROUPS):
        if gi == 4:
            big[:, 0:32, gi] = b_in[:, 416:448]
            big[:, 32:64, gi] = b_in[:, 384:416]
        else:
            big[:, 0:M, gi] = b_in[:, c0:c0 + M]

    def headg(g):
        o = np.zeros((L, 128, 3), f32)
        o[:, :, 0] = g[:, 0:128]
        o[:, 0:64, 1] = g[:, 128:192]
        o[:, 0:32, 2] = g[:, 160:192]
        o[:, 32:64, 2] = g[:, 128:160]
        return o

    shared = dict(
        ident=ident, umat=umat, ubar=ubar, invf=invf2,
        g1=fm(np.asarray(inp["g_norm1"], f32), 8), g2=fm(np.asarray(inp["g_norm2"], f32), 8),
        b_in_g=big, b_in=b_in,
        gql=fm(np.asarray(inp["g_q_lat"], f32), 2), gkvl=fm(np.asarray(inp["g_kv_lat"], f32), 1),
        gqh=headg(np.asarray(inp["g_q_head"], f32)), gkh=headg(np.asarray(inp["g_k_head"], f32)),
        wconv_f=np.ascontiguousarray(np.asarray(inp["w_conv"], f32).reshape(L, 3, 44, 128).transpose(0, 3, 2, 1)),
        bconv_f=fm(np.asarray(inp["b_conv"], f32), 44),
    )
    wrep = dict(w_uq="w_uq", w_ukv="w_ukv", w_bm="w_branch_mla", w_bs="w_branch_sb", w_out="w_out", w_down="w_down")
    for k, v in wrep.items():
        w = np.asarray(inp[v], f32)
        for l in range(L):
            shared[f"{k}_l{l}"] = w[l]
    wshard = {k: np.asarray(inp[k], f32) for k in ("w_in", "w_up")}
    w_ada_np = np.asarray(inp["w_ada"], f32)
    b_ada_np = np.asarray(inp["b_ada"], f32)
    x = np.asarray(inp["x"], f32)
    c = np.asarray(inp["c"], f32)
    positions = np.asarray(inp["positions"], np.int32)
    in_maps = []
    for r in range(8):
        b, cc = r // 4, r % 4
        xb = x[b].reshape(NB, 4, 128, D)[:, cc].reshape(NB * 128, D)
        pb = positions[b].reshape(NB, 4, 128)[:, cc].reshape(1, NB * 128)
        mca = np.zeros((128, 4, 128), f32)
        mst = np.zeros((128, 4, 128), f32)
        for rr in range(4):
            if rr < cc:
                mca[:, rr, :] = 1.0
                mst[:, rr, :] = 1.0
            elif rr == cc:
                mca[:, rr, :] = tri_le
                mst[:, rr, :] = tri_lt
        hw = np.zeros((128, 5), f32)
        if cc >= 1:
            hw[:, cc - 1] = 1.0
        else:
            hw[:, 4] = 1.0
        m = dict(shared)
        for k, w in wshard.items():
            K, N = w.shape[1], w.shape[2]
            for l in range(L):
                m[f"{k}_sh{l}"] = np.ascontiguousarray(w[l].reshape(K // 128, 4, 32, N)[:, cc].reshape(K // 4, N))
        for l in range(L):
            m[f"w_ada_q{l}"] = np.ascontiguousarray(w_ada_np[l][:, cc * 1536:(cc + 1) * 1536])
        m["b_ada_fq"] = np.ascontiguousarray(b_ada_np[:, cc * 1536:(cc + 1) * 1536].reshape(L, 12, 128).transpose(0, 2, 1))
        m["b_ada_q"] = np.ascontiguousarray(b_ada_np[:, cc * 1536:(cc + 1) * 1536])
        m.update(x=np.ascontiguousarray(xb), pos=np.ascontiguousarray(pb),
                 c=np.ascontiguousarray(c[b].reshape(8, 128).T), mcaus=mca, mstrict=mst, halow=hw)
        in_maps.append(m)
    return in_maps


def run(inp, S, L, stop_after=None, trace=False, dump=()):
    nc = build(S, L, stop_after, dump)
    in_maps = _prep_inputs(inp, S, L)
    res = run_bass_kernel_spmd(nc, in_maps, core_ids=list(range(8)), trace=trace)
    NB = S // 512
    B = 2
    out = np.zeros((B, S, D), np.float32)
    ov = out.reshape(B, NB, 4, 128, D)
    for r in range(8):
        b, cc = r // 4, r % 4
        ov[b, :, cc] = np.asarray(res.results[r]["y"]).reshape(NB, 128, D)
    return out, res


def kernel(**inputs):
    S = inputs["x"].shape[1]
    L = inputs["w_in"].shape[0]
    out, _ = run(inputs, S, L)
    return out
```

```python
import contextlib
import math
import numpy as np
import concourse.bass as bass
import concourse.mybir as mybir
from concourse.bass_utils import run_bass_kernel_spmd

F32 = mybir.dt.float32
BF16 = mybir.dt.bfloat16
I32 = mybir.dt.int32
AF = mybir.ActivationFunctionType
ALU = mybir.AluOpType

import os
SAFE = bool(int(os.environ.get("KSAFE", "0")))
D = 1024
DIN = 4032
DFF = 2816
EPS = 1e-6
SC_MLA = 192 ** -0.5
SC_SB = 128 ** -0.5
MAGIC = 12582912.0
TWO_PI = 2.0 * math.pi
C1 = 6.28125
C2 = float(np.float32(TWO_PI - C1))
C3 = float(TWO_PI - C1 - C2)
PI_LO = 3.1415925

GROUPS = [(0, 128), (128, 128), (256, 128), (384, 64), (4032, 64)]
GROUPS += [(448 + 128 * h, 128) for h in range(4)]
GROUPS += [(960 + 128 * h, 128) for h in range(4)]
GROUPS += [(1984 + 128 * j, 128) for j in range(8)]
GROUPS += [(3008 + 128 * j, 128) for j in range(8)]
NG = len(GROUPS)


NSLOT = 16


class StopBuild(Exception):
    pass


class Sem:
    def __init__(self, nc, name, inc):
        self.h = nc.alloc_semaphore(name=name)
        self.inc = inc
        self.count = 0


class Stream:
    def __init__(self, eng):
        self.eng = eng
        self.waited = {}


class Res:
    __slots__ = ("w", "r")

    def __init__(self):
        self.w = None
        self.r = {}


class Tl:
    def __init__(self, t, ex=False):
        self.t = t
        self.r = Res()
        self.ex = ex

    def __getitem__(self, k):
        return self.t[k]


class FW:
    def __init__(self, nc):
        self.nc = nc
        self.S_pe = Stream(nc.tensor)
        self.S_act = Stream(nc.scalar)
        self.S_dve = Stream(nc.vector)
        self.S_pool = Stream(nc.gpsimd)
        self.S_sp = Stream(nc.sync)
        self.sem_pe = Sem(nc, "s_pe", 1)
        self.sem_act = Sem(nc, "s_act", 1)
        self.sem_dve = Sem(nc, "s_dve", 1)
        self.sem_pool = Sem(nc, "s_pool", 1)
        self.sem_cc = Sem(nc, "s_cc", 1)
        self.dq_sp = [Sem(nc, f"s_dsp{i}", 16) for i in range(NSLOT)]
        self.dq_pl = [Sem(nc, f"s_dpl{i}", 16) for i in range(NSLOT)]
        self.n_sp = 0
        self.n_pl = 0
        self.n_ins = 0
        self.stopped = False

    def _issue(self, stream, sem, fn, reads, writes):
        if self.stopped:
            return None
        exr = [t for t in reads if t.ex]
        if exr:
            reads = [t for t in reads if not t.ex]
            writes = list(writes) + [t for t in exr if t not in writes]
        need = {}
        for t in reads:
            w = t.r.w
            if w is not None and need.get(w[0], 0) < w[1]:
                need[w[0]] = w[1]
        for t in writes:
            w = t.r.w
            if w is not None and need.get(w[0], 0) < w[1]:
                need[w[0]] = w[1]
            for sm, c in t.r.r.items():
                if need.get(sm, 0) < c:
                    need[sm] = c
        if SAFE:
            for sm in self.all_sems():
                if sm.count > 0:
                    need[sm] = sm.count
        for sm, c in need.items():
            if sm is self.sem_pe and stream is self.S_pe and not SAFE:
                continue
            if stream.waited.get(sm, 0) < c:
                stream.eng.wait_ge(sm.h, c * sm.inc)
                stream.waited[sm] = c
                self.n_ins += 1
        ins = fn()
        sem.count += 1
        ins.then_inc(sem.h, sem.inc)
        self.n_ins += 1
        for t in reads:
            if t.r.r.get(sem, 0) < sem.count:
                t.r.r[sem] = sem.count
        for t in writes:
            t.r.w = (sem, sem.count)
            t.r.r = {}
        return ins

    def barrier(self):
        if self.stopped:
            return
        for stream in (self.S_pe, self.S_act, self.S_dve, self.S_pool, self.S_sp):
            for sm in self.all_sems():
                if sm.count > 0 and stream.waited.get(sm, 0) < sm.count:
                    stream.eng.wait_ge(sm.h, sm.count * sm.inc)
                    stream.waited[sm] = sm.count
                    self.n_ins += 1

    def all_sems(self):
        return self.dq_sp + self.dq_pl + [self.sem_cc, self.sem_pe, self.sem_act, self.sem_dve, self.sem_pool]

    def pe(self, fn, reads, writes):
        return self._issue(self.S_pe, self.sem_pe, fn, reads, writes)

    def act(self, fn, reads, writes):
        return self._issue(self.S_act, self.sem_act, fn, reads, writes)

    def dve(self, fn, reads, writes):
        return self._issue(self.S_dve, self.sem_dve, fn, reads, writes)

    def pool(self, fn, reads, writes):
        return self._issue(self.S_pool, self.sem_pool, fn, reads, writes)

    def cc(self, fn, reads, writes):
        return self._issue(self.S_pool, self.sem_cc, fn, reads, writes)

    def dma(self, which, out, in_, reads, writes):
        nc = self.nc
        if which == "sp":
            stream, sem, eng = self.S_sp, self.dq_sp[self.n_sp % NSLOT], nc.sync
            self.n_sp += 1
        else:
            stream, sem, eng = self.S_pool, self.dq_pl[self.n_pl % NSLOT], nc.gpsimd
            self.n_pl += 1
        if self.stopped:
            return None
        if sem.count > 0 and stream.waited.get(sem, 0) < sem.count:
            stream.eng.wait_ge(sem.h, sem.count * 16)
            stream.waited[sem] = sem.count
        return self._issue(stream, sem, lambda: eng.dma_start(out=out, in_=in_), reads, writes)

    def finish(self):
        for sem in self.dq_sp + self.dq_pl + [self.sem_cc, self.sem_pe, self.sem_act, self.sem_dve, self.sem_pool]:
            if sem.count > 0:
                self.nc.sync.wait_ge(sem.h, sem.count * sem.inc)


def build(S, L, stop_after=None, dump=()):
    NB = S // 512
    TT = NB // 4
    NT = NB * 128
    nc = bass.Bass("TRN2", target_bir_lowering=False)
    fw = FW(nc)

    def din(name, shape, dt=F32):
        return Tl(nc.dram_tensor(name, list(shape), dt, kind="ExternalInput").ap())

    scr = {}

    def dscr(name, shape, dt):
        t = Tl(nc.dram_tensor(name, list(shape), dt).ap())
        scr[name] = (t, list(shape), dt)
        return t

    RG = [[0, 1, 2, 3], [4, 5, 6, 7]]
    gathers = []

    class WStack:
        def __init__(self, name, K, N, shard=False):
            self.shard, self.K, self.N = shard, K, N
            if shard:
                self.sh = [din(f"{name}_sh{l}", [K // 4, N]) for l in range(L)]
                self.bn = [Tl(nc.dram_tensor(f"{name}_bn{l}", [K // 4, N], F32).ap()) for l in range(L)]
                self.tl = [Tl(nc.dram_tensor(f"{name}_full{l}", [K, N], F32).ap()) for l in range(L)]
                gathers.append(self)
            else:
                self.tl = [din(f"{name}_l{l}", [K, N]) for l in range(L)]

        def __getitem__(self, key):
            if isinstance(key, tuple):
                return self.tl[key[0]].t[key[1:]]
            return self.tl[key].t

    x_in = din("x", [NT, D])
    pos = din("pos", [1, NT], I32)
    c_in = din("c", [128, 8])
    ident_in = din("ident", [128, 128])
    umat_in = din("umat", [128, 128])
    ubar_in = din("ubar", [128, 128])
    mcaus_in = din("mcaus", [128, 4, 128])
    mstr_in = din("mstrict", [128, 4, 128])
    halow_in = din("halow", [128, 5])
    invf_in = din("invf", [64, 1])
    w_ada_q = [din(f"w_ada_q{l}", [D, 1536]) for l in range(L)]
    b_ada_fq = din("b_ada_fq", [L, 128, 12])
    b_ada_q = din("b_ada_q", [L, 1536])
    g1_in = din("g1", [L, 128, 8])
    g2_in = din("g2", [L, 128, 8])
    w_in = WStack("w_in", D, DIN, shard=True)
    b_in_g = din("b_in_g", [L, 128, NG])
    b_in = din("b_in", [L, DIN])
    gql_in = din("gql", [L, 128, 2])
    gkvl_in = din("gkvl", [L, 128, 1])
    w_uq = WStack("w_uq", 256, 768)
    w_ukv = WStack("w_ukv", 128, 1024)
    gqh_in = din("gqh", [L, 128, 3])
    gkh_in = din("gkh", [L, 128, 3])
    w_bm = WStack("w_bm", 512, D)
    w_bs = WStack("w_bs", 512, D)
    w_out = WStack("w_out", D, D)
    w_up = WStack("w_up", D, 2 * DFF, shard=True)
    wconv_f = din("wconv_f", [L, 128, 44, 3])
    bconv_f = din("bconv_f", [L, 128, 44])
    w_down = WStack("w_down", DFF, D)
    y_out = Tl(nc.dram_tensor("y", [NT, D], F32, kind="ExternalOutput").ap())

    X1 = dscr("X1", [NT, D], F32)
    XA = dscr("XA", [NT, D], F32)
    MF_i = dscr("MF_i", [128, L * 12], F32)
    MF_o = dscr("MF_o", [4 * 128, L * 12], F32)
    GR_i = dscr("GR_i", [1, L * 1536], F32)
    GR_o = dscr("GR_o", [4, L * 1536], F32)
    CS = dscr("CS", [2, 64, NT], F32)
    QTn = dscr("QTn", [4, 128, NT], BF16)
    QTr = dscr("QTr", [4, 64, NT], BF16)
    QTs = dscr("QTs", [4, 128, NT], BF16)
    GA = dscr("GA", [TT, 128, 16 * 512], BF16)
    YTm = dscr("YTm", [4, 128, NT], BF16)
    YTs = dscr("YTs", [4, 128, NT], BF16)
    H2 = dscr("H2", [TT, 128, 8 * 512], BF16)
    KTn_i = [dscr(f"KTn_i{h}", [128, NT], BF16) for h in range(4)]
    KTn_o = [dscr(f"KTn_o{h}", [4 * 128, NT], BF16) for h in range(4)]
    KTr_i = [dscr(f"KTr_i{h}", [64, NT], BF16) for h in range(4)]
    KTr_o = [dscr(f"KTr_o{h}", [4 * 64, NT], BF16) for h in range(4)]
    KTs_i = [dscr(f"KTs_i{h}", [128, NT], BF16) for h in range(4)]
    KTs_o = [dscr(f"KTs_o{h}", [4 * 128, NT], BF16) for h in range(4)]
    VM_i = [dscr(f"VM_i{h}", [TT * 128, 512], BF16) for h in range(4)]
    VM_o = [dscr(f"VM_o{h}", [4 * TT * 128, 512], BF16) for h in range(4)]
    VS_i = [dscr(f"VS_i{h}", [TT * 128, 512], BF16) for h in range(4)]
    VS_o = [dscr(f"VS_o{h}", [4 * TT * 128, 512], BF16) for h in range(4)]
    HB_i = dscr("HB_i", [128, 8 * NB * 2], BF16)
    HB_o = dscr("HB_o", [4 * 128, 8 * NB * 2], BF16)
    RG = [[0, 1, 2, 3], [4, 5, 6, 7]]

    def allgather(src, dst):
        fw.cc(lambda: nc.gpsimd.collective_compute("AllGather", ALU.bypass, replica_groups=RG,
                                                   ins=[src.t.opt()], outs=[dst.t.opt()]),
              reads=[src], writes=[dst])

    with contextlib.ExitStack() as top:
        nm = [0]

        def SB(es, shape, dt):
            nm[0] += 1
            return Tl(es.enter_context(nc.sbuf_tensor(f"sb{nm[0]}", list(shape), dt)))

        def PS(es, shape, dt):
            nm[0] += 1
            return Tl(es.enter_context(nc.psum_tensor(f"ps{nm[0]}", list(shape), dt)), ex=True)

        ps = [PS(top, [128, 512], F32) for _ in range(8)]
        pt = []
        for i in (6, 7):
            v = Tl(ps[i].t[:].bitcast(BF16), ex=True)
            v.r = ps[i].r
            pt.append(v)
        rot = [0]

        def nextps(n=6):
            rot[0] = (rot[0] + 1) % n
            return ps[rot[0]]

        def mm(out, lhsT, rhs, start, stop, reads, writes):
            fw.pe(lambda: nc.tensor.matmul(out, lhsT=lhsT, rhs=rhs, start=start, stop=stop,
                                           skip_group_check=True), reads, writes)

        def actf(out, in_, func, reads, writes, scale=None, bias=None):
            kw = {}
            if scale is not None:
                kw["scale"] = scale
            if bias is not None:
                kw["bias"] = bias
            fw.act(lambda: nc.scalar.activation(out=out, in_=in_, func=func, **kw), reads, writes)

        def ts(out, in0, s1, s2, op0, op1, reads, writes, eng="dve"):
            if op1 is None:
                f = lambda: nc.vector.tensor_scalar(out=out, in0=in0, scalar1=s1, scalar2=None, op0=op0)
            else:
                f = lambda: nc.vector.tensor_scalar(out=out, in0=in0, scalar1=s1, scalar2=s2, op0=op0, op1=op1)
            fw.dve(f, reads, writes)

        def stt(out, in0, scalar, in1, op0, op1, reads, writes):
            fw.dve(lambda: nc.vector.scalar_tensor_tensor(out=out, in0=in0, scalar=scalar, in1=in1, op0=op0, op1=op1),
                   reads, writes)

        def tt(out, in0, in1, op, reads, writes):
            fw.dve(lambda: nc.vector.tensor_tensor(out=out, in0=in0, in1=in1, op=op), reads, writes)

        def ptt(out, in0, in1, op, reads, writes):
            fw.pool(lambda: nc.gpsimd.tensor_tensor(out=out, in0=in0, in1=in1, op=op), reads, writes)

        cst = SB(top, [128, 8], F32)
        fw.dve(lambda: nc.vector.memset(cst[:, 0:1], EPS), [], [cst])
        fw.dve(lambda: nc.vector.memset(cst[:, 1:2], 1.0), [], [cst])
        fw.dve(lambda: nc.vector.memset(cst[:, 2:3], math.pi / 2), [], [cst])
        fw.dve(lambda: nc.vector.memset(cst[:, 3:4], 0.0), [], [cst])
        ident_f = SB(top, [128, 128], F32)
        identb = SB(top, [128, 128], BF16)
        umat = SB(top, [128, 128], BF16)
        ubar = SB(top, [128, 128], BF16)
        onesb = SB(top, [128, 128], BF16)
        onesf = SB(top, [128, 128], F32)
        mcaus = SB(top, [128, 4, 128], BF16)
        mstr = SB(top, [128, 4, 128], F32)
        halow = SB(top, [128, 5], F32)
        invf = SB(top, [64, 1], F32)
        fw.dma("sp", ident_f[:], ident_in[:], [ident_in], [ident_f])
        fw.dve(lambda: nc.vector.tensor_copy(out=identb[:], in_=ident_f[:]), [ident_f], [identb])
        fw.dma("pl", umat[:], umat_in[:], [umat_in], [umat])
        fw.dma("pl", ubar[:], ubar_in[:], [ubar_in], [ubar])
        fw.dma("pl", mcaus[:], mcaus_in[:], [mcaus_in], [mcaus])
        fw.dma("sp", mstr[:], mstr_in[:], [mstr_in], [mstr])
        fw.dma("sp", halow[:], halow_in[:], [halow_in], [halow])
        fw.dma("sp", invf[:], invf_in[:], [invf_in], [invf])
        fw.dve(lambda: nc.vector.memset(onesb[:], 1.0), [], [onesb])
        fw.dve(lambda: nc.vector.memset(onesf[:], 1.0), [], [onesf])
        modF = SB(top, [128, L, 48], F32)
        A1 = SB(top, [128, L, 8], F32)
        A2 = SB(top, [128, L, 8], F32)

        def issue_weight_bounce(l):
            for ws in gathers:
                fw.dma("sp", ws.bn[l].t, ws.sh[l].t, [ws.sh[l]], [ws.bn[l]])

        def issue_weight_cc(l):
            for ws in gathers:
                for kc in range(ws.K // 128):
                    fw.cc(lambda: nc.gpsimd.collective_compute(
                        "AllGather", ALU.bypass, replica_groups=RG,
                        ins=[ws.bn[l].t[kc * 32:(kc + 1) * 32, :].opt()],
                        outs=[ws.tl[l].t[kc * 128:(kc + 1) * 128, :].opt()]),
                        reads=[ws.bn[l]], writes=[ws.tl[l]])

        def issue_weight_gathers(l):
            issue_weight_bounce(l)
            issue_weight_cc(l)

        issue_weight_gathers(0)
        with contextlib.ExitStack() as es:
            cact = SB(es, [128, 8], F32)
            wbuf = [SB(es, [128, 8, 1536], F32) for _ in range(2)]
            baf = SB(es, [128, L, 12], F32)
            brow = SB(es, [1, L, 1536], F32)
            grow = SB(es, [1, L, 1536], F32)
            modq = SB(es, [128, L, 12], F32)
            g12 = SB(es, [128, 2, 8], F32)
            fw.dma("sp", cact[:], c_in[:], [c_in], [cact])
            actf(cact[:], cact[:], AF.Silu, [cact], [cact])
            fw.dma("sp", baf[:], b_ada_fq[:].rearrange("l p j -> p l j"), [b_ada_fq], [baf])
            fw.dma("sp", brow[:], b_ada_q[:].rearrange("(o l) n -> o l n", o=1), [b_ada_q], [brow])
            for l in range(L):
                psF = nextps()
                wb = wbuf[l % 2]
                fw.dma("sp", wb[:], w_ada_q[l][:, :].rearrange("(kc p) n -> p kc n", p=128), [w_ada_q[l]], [wb])
                for j in range(12):
                    for kc in range(8):
                        mm(psF[:, j:j + 1], wb[:, kc, j * 128:(j + 1) * 128], cact[:, kc:kc + 1],
                           kc == 0, kc == 7, [wb, cact], [psF])
                tt(modq[:, l, :], psF[:, 0:12], baf[:, l, :], ALU.add, [psF, baf], [modq])
                for n3 in range(3):
                    pr = nextps()
                    for kc in range(8):
                        mm(pr[0:1, :], cact[:, kc:kc + 1], wb[:, kc, n3 * 512:(n3 + 1) * 512],
                           kc == 0, kc == 7, [wb, cact], [pr])
                    tt(grow[:, l, n3 * 512:(n3 + 1) * 512], pr[0:1, :], brow[:, l, n3 * 512:(n3 + 1) * 512],
                       ALU.add, [pr, brow], [grow])
            fw.dma("sp", MF_i[:, :].rearrange("p (l j) -> p l j", l=L), modq[:], [modq], [MF_i])
            fw.dma("sp", GR_i[:, :].rearrange("o (l n) -> o l n", l=L), grow[:], [grow], [GR_i])
            allgather(MF_i, MF_o)
            allgather(GR_i, GR_o)
            for r4 in range(4):
                fw.dma("sp", modF[:, :, r4 * 12:(r4 + 1) * 12],
                       MF_o[r4 * 128:(r4 + 1) * 128, :].rearrange("p (l j) -> p l j", l=L), [MF_o], [modF])
            for l in range(L):
                fw.dma("sp", g12[:, 0, :], g1_in[l], [g1_in], [g12])
                fw.dma("sp", g12[:, 1, :], g2_in[l], [g2_in], [g12])
                stt(A1[:, l, :], modF[:, l, 8:16], 1.0, g12[:, 0, :], ALU.add, ALU.mult, [modF, g12], [A1])
                stt(A2[:, l, :], modF[:, l, 32:40], 1.0, g12[:, 1, :], ALU.add, ALU.mult, [modF, g12], [A2])

        fw.barrier()
        with contextlib.ExitStack() as es:
            posi = SB(es, [64, 512], I32)
            a0 = SB(es, [64, 512], F32)
            a1 = SB(es, [64, 512], F32)
            a2 = SB(es, [64, 512], F32)
            sn = SB(es, [64, 512], F32)
            cs = SB(es, [64, 512], F32)
            for t in range(TT):
                sl = slice(t * 512, (t + 1) * 512)
                fw.dma("sp", posi[:], pos[0:1, sl].partition_broadcast(64), [pos], [posi])
                fw.dve(lambda: nc.vector.tensor_copy(out=a0[:], in_=posi[:]), [posi], [a0])
                ts(a0[:], a0[:], invf[:, 0:1], None, ALU.mult, None, [a0, invf], [a0])
                ts(a1[:], a0[:], 1.0 / TWO_PI, MAGIC, ALU.mult, ALU.add, [a0], [a1])
                ts(a1[:], a1[:], -MAGIC, None, ALU.add, None, [a1], [a1])
                stt(a2[:], a1[:], -C1, a0[:], ALU.mult, ALU.add, [a1, a0], [a2])
                stt(a2[:], a1[:], -C2, a2[:], ALU.mult, ALU.add, [a1, a2], [a2])
                stt(a2[:], a1[:], -C3, a2[:], ALU.mult, ALU.add, [a1, a2], [a2])
                ts(a2[:], a2[:], -PI_LO, PI_LO, ALU.max, ALU.min, [a2], [a2])
                actf(sn[:], a2[:], AF.Sin, [a2], [sn])
                actf(a1[:], a2[:], AF.Abs, [a2], [a1])
                actf(cs[:], a1[:], AF.Sin, [a1, cst], [cs], scale=-1.0, bias=cst[0:64, 2:3])
                fw.dma("sp", CS[0, :, sl], cs[:], [cs], [CS])
                fw.dma("sp", CS[1, :, sl], sn[:], [sn], [CS])

        fw.barrier()
        if stop_after == "p0":
            fw.finish()
            return nc

        def norm_hT(xt, sq, ss, rstd, xn, A_ap, sh_ap, hT, dep):
            actf(sq[:], xt[:], AF.Square, [xt], [sq])
            fw.dve(lambda: nc.vector.tensor_reduce(out=ss[:], in_=sq[:], axis=mybir.AxisListType.X, op=ALU.add),
                   [sq], [ss])
            actf(rstd[:], ss[:], AF.Sqrt, [ss, cst], [rstd], scale=1.0 / D, bias=cst[:, 0:1])
            fw.dve(lambda: nc.vector.reciprocal(out=rstd[:], in_=rstd[:]), [rstd], [rstd])
            for b in range(4):
                ts(xn[:, b, :], xt[:, b, :], rstd[:, b:b + 1], None, ALU.mult, None, [xt, rstd], [xn])
            for kc in range(8):
                p = pt[kc % 2]
                for b in range(4):
                    fw.pe(lambda: nc.tensor.transpose(out=p[:, b * 128:(b + 1) * 128],
                                                      in_=xn[:, b, kc * 128:(kc + 1) * 128], identity=identb[:]),
                          [xn, identb], [p])
                actf(hT[:, kc, :], p[:, 0:512], AF.Identity, [p] + dep, [hT], scale=A_ap[:, kc:kc + 1], bias=sh_ap[:, kc:kc + 1])

        def chk(name):
            if stop_after == name:
                fw.stopped = True

        try:
          for l in range(L):
            x_src = x_in if l == 0 else XA
            x_dst = y_out if l == L - 1 else XA

            with contextlib.ExitStack() as es:
                w1 = SB(es, [128, 8, DIN + 64], BF16)
                wq = SB(es, [128, 2, 768 + 256], BF16)
                wkv = SB(es, [128, 2, 512], BF16)
                bg = SB(es, [128, NG], F32)
                bvs = SB(es, [128, 512], F32)
                gql = SB(es, [128, 2], F32)
                gkvl = SB(es, [128, 1], F32)
                gqh = SB(es, [128, 3], F32)
                gkh = SB(es, [128, 3], F32)
                for kc in range(8):
                    fw.dma("pl", w1[:, kc, 0:DIN], w_in[l, kc * 128:(kc + 1) * 128, :], [w_in.tl[l]], [w1])
                for kc in range(2):
                    fw.dma("pl", wq[:, kc, 0:768], w_uq[l, kc * 128:(kc + 1) * 128, :], [w_uq.tl[l]], [wq])
                for t2 in range(2):
                    fw.dma("pl", wkv[:, t2, :].rearrange("p (h d) -> p h d", h=4),
                           w_ukv[l].rearrange("p (h t d) -> p t h d", h=4, t=2)[:, t2], [w_ukv.tl[l]], [wkv])
                fw.dma("sp", bg[:], b_in_g[l], [b_in_g], [bg])
                fw.dma("sp", bvs[:], b_in[l:l + 1, 1472:1984].partition_broadcast(128), [b_in], [bvs])
                fw.dma("sp", gql[:], gql_in[l], [gql_in], [gql])
                fw.dma("sp", gkvl[:], gkvl_in[l], [gkvl_in], [gkvl])
                fw.dma("sp", gqh[:], gqh_in[l], [gqh_in], [gqh])
                fw.dma("sp", gkh[:], gkh_in[l], [gkh_in], [gkh])
                actf(w1[:, :, DIN:DIN + 32], w1[:, :, 416:448], AF.Identity, [w1], [w1], scale=-1.0)
                fw.act(lambda: nc.scalar.copy(out=w1[:, :, DIN + 32:DIN + 64], in_=w1[:, :, 384:416]), [w1], [w1])
                fw.dve(lambda: nc.vector.tensor_scalar(out=bg[0:32, 4:5], in0=bg[0:32, 4:5], scalar1=-1.0, scalar2=None,
                                                       op0=ALU.mult), [bg], [bg])
                for h in range(4):
                    actf(wq[:, :, 768 + h * 64:768 + h * 64 + 32], wq[:, :, h * 192 + 160:h * 192 + 192], AF.Identity, [wq], [wq], scale=-1.0)
                    fw.act(lambda: nc.scalar.copy(out=wq[:, :, 768 + h * 64 + 32:768 + h * 64 + 64],
                                                  in_=wq[:, :, h * 192 + 128:h * 192 + 160]), [wq], [wq])

                chk("p1w")
                xt = SB(es, [128, 4, D], F32)
                sq = SB(es, [128, 4, D], F32)
                xn = SB(es, [128, 4, D], BF16)
                ss = SB(es, [128, 4], F32)
                rstd = SB(es, [128, 4], F32)
                hT = SB(es, [128, 8, 512], BF16)
                cos2 = SB(es, [64, 512], F32)
                sin2 = SB(es, [64, 512], F32)
                qg = SB(es, [128, 2, 512], BF16)
                sqq = SB(es, [128, 2, 512], BF16)
                epsq = SB(es, [128, 512], F32)
                kvg = SB(es, [128, 512], BF16)
                sqkv = SB(es, [128, 512], BF16)
                rk2 = SB(es, [128, 512], F32)
                rkb = SB(es, [128, 512], F32)
                rktok = SB(es, [128, 4], F32)
                kr = SB(es, [64, 512], F32)
                krot = SB(es, [64, 512], F32)
                sqkr = SB(es, [64, 512], BF16)
                s2sb = SB(es, [128, 512], F32)
                krr = SB(es, [64, 512], F32)
                tmpa = SB(es, [128, 512], F32)
                tmpb = SB(es, [128, 512], F32)
                rsh = SB(es, [128, 512], F32)
                sqn = SB(es, [128, 512], BF16)
                sqr = SB(es, [64, 512], BF16)
                ob = [SB(es, [128, 512], BF16) for _ in range(4)]
                obr = [SB(es, [64, 512], BF16) for _ in range(4)]
                gat = SB(es, [128, 16, 512], BF16)
                vt = SB(es, [128, 4, 512], BF16)
                obk = [0]

                def nob():
                    obk[0] += 1
                    return ob[obk[0] % 4]

                def nobr():
                    obk[0] += 1
                    return obr[obk[0] % 4]

                def inproj(gi):
                    col0, M = GROUPS[gi]
                    p = nextps()
                    for kc in range(8):
                        mm(p[0:M, :], w1[:, kc, col0:col0 + M], hT[:, kc, :], kc == 0, kc == 7, [w1, hT], [p])
                    return p

                for t in range(TT):
                    sl = slice(t * 512, (t + 1) * 512)
                    fw.dma("sp", xt[:], x_src[sl, :].rearrange("(b p) d -> p b d", p=128), [x_src], [xt])
                    fw.dma("sp", cos2[:], CS[0, :, sl], [CS], [cos2])
                    fw.dma("sp", sin2[:], CS[1, :, sl], [CS], [sin2])
                    norm_hT(xt, sq, ss, rstd, xn, A1[:, l, :], modF[:, l, 0:8], hT, [A1, modF])

                    chk("p1n")
                    for j in range(2):
                        p = inproj(j)
                        ts(qg[:, j, :], p[:], bg[:, j:j + 1], gql[:, j:j + 1], ALU.add, ALU.mult, [p, bg, gql], [qg])
                        actf(sqq[:, j, :], p[:], AF.Square, [p, bg], [sqq], bias=bg[:, j:j + 1])
                    p = nextps()
                    for j in range(2):
                        mm(p[:], onesb[:], sqq[:, j, :], j == 0, j == 1, [onesb, sqq], [p])
                    ts(epsq[:], p[:], EPS / 256.0, EPS * EPS, ALU.mult, ALU.add, [p], [epsq])
                    p = inproj(2)
                    ts(kvg[:], p[:], bg[:, 2:3], gkvl[:, 0:1], ALU.add, ALU.mult, [p, bg, gkvl], [kvg])
                    actf(sqkv[:], p[:], AF.Square, [p, bg], [sqkv], bias=bg[:, 2:3])
                    p = nextps()
                    mm(p[:], onesb[:], sqkv[:], True, True, [onesb, sqkv], [p])
                    ts(rk2[:], p[:], 1.0 / 128.0, EPS, ALU.mult, ALU.add, [p], [rk2])
                    fw.dve(lambda: nc.vector.reciprocal(out=rk2[:], in_=rk2[:]), [rk2], [rk2])
                    actf(rkb[:], rk2[:], AF.Sqrt, [rk2], [rkb])
                    p = nextps()
                    for b in range(4):
                        mm(p[:, b:b + 1], sqkv[:, b * 128:(b + 1) * 128], onesb[:, 0:1], True, True, [sqkv, onesb], [p])
                    actf(rktok[:], p[:, 0:4], AF.Sqrt, [p, cst], [rktok], scale=1.0 / 128.0, bias=cst[:, 0:1])
                    fw.dve(lambda: nc.vector.reciprocal(out=rktok[:], in_=rktok[:]), [rktok], [rktok])
                    p = inproj(3)
                    actf(kr[:], p[0:64, :], AF.Identity, [p, bg], [kr], bias=bg[0:64, 3:4])
                    actf(sqkr[:], p[0:64, :], AF.Square, [p, bg], [sqkr], bias=bg[0:64, 3:4])
                    p = inproj(4)
                    actf(krot[:], p[0:64, :], AF.Identity, [p, bg], [krot], bias=bg[0:64, 4:5])
                    p = nextps()
                    mm(p[:], onesb[0:64, :], sqkr[:], True, True, [onesb, sqkr], [p])
                    actf(s2sb[:], p[:], AF.Identity, [p], [s2sb])
                    stt(krr[:], kr[:], gkh[0:64, 1:2], cos2[:], ALU.mult, ALU.mult, [kr, gkh, cos2], [krr])
                    stt(tmpa[0:64, :], krot[:], gkh[0:64, 2:3], sin2[:], ALU.mult, ALU.mult, [krot, gkh, sin2], [tmpa])
                    tt(krr[:], krr[:], tmpa[0:64, :], ALU.add, [krr, tmpa], [krr])

                    chk("p1l")
                    for h in range(4):
                        pn = nextps()
                        for kc in range(2):
                            mm(pn[:], wq[:, kc, h * 192:h * 192 + 128], qg[:, kc, :], kc == 0, kc == 1, [wq, qg], [pn])
                        pr = nextps()
                        for kc in range(2):
                            mm(pr[0:64, :], wq[:, kc, h * 192 + 128:h * 192 + 192], qg[:, kc, :], kc == 0, kc == 1, [wq, qg], [pr])
                        pro = nextps()
                        for kc in range(2):
                            mm(pro[0:64, :], wq[:, kc, 768 + h * 64:768 + h * 64 + 64], qg[:, kc, :], kc == 0, kc == 1, [wq, qg], [pro])
                        actf(sqn[:], pn[:], AF.Square, [pn], [sqn])
                        actf(sqr[:], pr[0:64, :], AF.Square, [pr], [sqr])
                        pss = nextps()
                        mm(pss[:], onesb[:], sqn[:], True, False, [onesb, sqn], [pss])
                        mm(pss[:], onesb[0:64, :], sqr[:], False, True, [onesb, sqr], [pss])
                        stt(tmpa[:], pss[:], 1.0 / 192.0, epsq[:], ALU.mult, ALU.add, [pss, epsq], [tmpa])
                        actf(rsh[:], tmpa[:], AF.Sqrt, [tmpa], [rsh])
                        fw.dve(lambda: nc.vector.reciprocal(out=rsh[:], in_=rsh[:]), [rsh], [rsh])
                        o = nob()
                        stt(o[:], pn[:], gqh[:, 0:1], rsh[:], ALU.mult, ALU.mult, [pn, gqh, rsh], [o])
                        fw.dma("sp", QTn[h, :, sl], o[:], [o], [QTn])
                        stt(tmpa[0:64, :], pr[0:64, :], gqh[0:64, 1:2], cos2[:], ALU.mult, ALU.mult, [pr, gqh, cos2], [tmpa])
                        stt(tmpb[0:64, :], pro[0:64, :], gqh[0:64, 2:3], sin2[:], ALU.mult, ALU.mult, [pro, gqh, sin2], [tmpb])
                        tt(tmpa[0:64, :], tmpa[0:64, :], tmpb[0:64, :], ALU.add, [tmpa, tmpb], [tmpa])
                        orr = nobr()
                        tt(orr[:], tmpa[0:64, :], rsh[0:64, :], ALU.mult, [tmpa, rsh], [orr])
                        fw.dma("sp", QTr[h, :, sl], orr[:], [orr], [QTr])

                    chk("p1q")
                    for h in range(4):
                        pn = nextps()
                        mm(pn[:], wkv[:, 0, h * 128:(h + 1) * 128], kvg[:], True, True, [wkv, kvg], [pn])
                        actf(sqn[:], pn[:], AF.Square, [pn], [sqn])
                        pss = nextps()
                        mm(pss[:], onesb[:], sqn[:], True, True, [onesb, sqn], [pss])
                        tt(tmpa[:], pss[:], rk2[:], ALU.mult, [pss, rk2], [tmpa])
                        tt(tmpa[:], tmpa[:], s2sb[:], ALU.add, [tmpa, s2sb], [tmpa])
                        ts(tmpa[:], tmpa[:], 1.0 / 192.0, EPS, ALU.mult, ALU.add, [tmpa], [tmpa])
                        actf(rsh[:], tmpa[:], AF.Sqrt, [tmpa], [rsh])
                        fw.dve(lambda: nc.vector.reciprocal(out=rsh[:], in_=rsh[:]), [rsh], [rsh])
                        tt(tmpb[:], rsh[:], rkb[:], ALU.mult, [rsh, rkb], [tmpb])
                        o = nob()
                        stt(o[:], pn[:], gkh[:, 0:1], tmpb[:], ALU.mult, ALU.mult, [pn, gkh, tmpb], [o])
                        fw.dma("sp", KTn_i[h][:, sl], o[:], [o], [KTn_i[h]])
                        orr = nobr()
                        tt(orr[:], krr[:], rsh[0:64, :], ALU.mult, [krr, rsh], [orr])
                        fw.dma("sp", KTr_i[h][:, sl], orr[:], [orr], [KTr_i[h]])
                    for b in range(4):
                        p = nextps()
                        mm(p[:], kvg[:, b * 128:(b + 1) * 128], wkv[:, 1, :], True, True, [kvg, wkv], [p])
                        ts(vt[:, b, :], p[:], rktok[:, b:b + 1], None, ALU.mult, None, [p, rktok], [vt])
                    for h in range(4):
                        fw.dma("sp", VM_i[h][t * 128:(t + 1) * 128, :].rearrange("p (b d) -> p b d", b=4),
                               vt[:, :, h * 128:(h + 1) * 128], [vt], [VM_i[h]])
                    chk("p1k")
                    for h in range(4):
                        p = inproj(5 + h)
                        o = nob()
                        actf(o[:], p[:], AF.Identity, [p, bg], [o], bias=bg[:, 5 + h:6 + h])
                        fw.dma("sp", QTs[h, :, sl], o[:], [o], [QTs])
                    for h in range(4):
                        p = inproj(9 + h)
                        o = nob()
                        actf(o[:], p[:], AF.Identity, [p, bg], [o], bias=bg[:, 9 + h:10 + h])
                        fw.dma("sp", KTs_i[h][:, sl], o[:], [o], [KTs_i[h]])
                    for b in range(4):
                        p = nextps()
                        for kc in range(8):
                            mm(p[:], hT[:, kc, b * 128:(b + 1) * 128], w1[:, kc, 1472:1984], kc == 0, kc == 7, [hT, w1], [p])
                        tt(vt[:, b, :], p[:], bvs[:], ALU.add, [p, bvs], [vt])
                    for h in range(4):
                        fw.dma("sp", VS_i[h][t * 128:(t + 1) * 128, :].rearrange("p (b d) -> p b d", b=4),
                               vt[:, :, h * 128:(h + 1) * 128], [vt], [VS_i[h]])
                    chk("p1s")
                    for j in range(16):
                        p = inproj(13 + j)
                        actf(gat[:, j, :], p[:], AF.Sigmoid, [p, bg], [gat], bias=bg[:, 13 + j:14 + j])
                    fw.dma("sp", GA[t].rearrange("p (j n) -> p j n", j=16), gat[:], [gat], [GA])

            fw.barrier()
            chk("p1")
            for h in range(4):
                allgather(KTn_i[h], KTn_o[h])
                allgather(KTr_i[h], KTr_o[h])
                allgather(VM_i[h], VM_o[h])
            for h in range(4):
                allgather(KTs_i[h], KTs_o[h])
                allgather(VS_i[h], VS_o[h])

            chk("ag")
            if l + 1 < L:
                issue_weight_bounce(l + 1)
            with contextlib.ExitStack() as es:
                NM, NS = 6, 8
                qn_t = [SB(es, [128, 512], BF16) for _ in range(2)]
                qr_t = [SB(es, [64, 512], BF16) for _ in range(2)]
                mkn_t = [SB(es, [128, 512], BF16) for _ in range(NM)]
                mkr_t = [SB(es, [64, 512], BF16) for _ in range(NM)]
                mvv_t = [SB(es, [128, 512], BF16) for _ in range(NM)]
                NP = 3
                pT_t = [SB(es, [128, 512], BF16) for _ in range(NP)]
                pacc = SB(es, [128, 512], F32)
                rinv = SB(es, [128, 512], F32)
                ytm_t = [SB(es, [128, 512], BF16) for _ in range(2)]
                NE = 3
                sc_ps = [ps[0], ps[1], ps[2]]
                oacc_m = ps[7]
                cnt = {"qm": 0, "km": 0, "sm": 0, "ym": 0, "sc": 0}

                def next_sc():
                    cnt["sc"] += 1
                    return sc_ps[cnt["sc"] % 3]

                class SBStream:
                    def __init__(self, rps, oacc, zb):
                        self.rps, self.oacc, self.zb = rps, oacc, zb
                        self.qs_t = [SB(es, [128, 512], BF16) for _ in range(2)]
                        self.kn_t = [SB(es, [128, 512], BF16) for _ in range(NS)]
                        self.vv_t = [SB(es, [128, 512], BF16) for _ in range(NS)]
                        self.e_t = [SB(es, [128, 512], F32) for _ in range(NE)]
                        self.lp_t = [SB(es, [128, 512], F32) for _ in range(NE)]
                        self.hi_t = [SB(es, [128, 512], BF16) for _ in range(NE)]
                        self.lo_t = [SB(es, [128, 512], BF16) for _ in range(NE)]
                        self.ex_t = [SB(es, [128, 512], F32) for _ in range(NE)]
                        self.a_t = [SB(es, [128, 512], BF16) for _ in range(NE)]
                        self.yt_t = [SB(es, [128, 512], BF16) for _ in range(2)]
                        self.nq = self.nk = self.nsx = self.ny = 0

                stA = SBStream(ps[3], ps[5], ps[0])
                stB = SBStream(ps[4], ps[6], ps[1])

                def mla_sweep(g, h):
                    gsl = slice(g * 512, (g + 1) * 512)
                    qn = qn_t[cnt["qm"] % 2]
                    qr = qr_t[cnt["qm"] % 2]
                    cnt["qm"] += 1
                    fw.dma("sp", qn[:], QTn[h, :, gsl], [QTn], [qn])
                    fw.dma("sp", qr[:], QTr[h, :, gsl], [QTr], [qr])
                    fw.pool(lambda: nc.gpsimd.memset(pacc[:], 0.0), [], [pacc])
                    first = True
                    for r in range(4):
                        for ch in range(g + 1):
                            i = cnt["km"] % NM
                            cnt["km"] += 1
                            kn, krt, vv = mkn_t[i], mkr_t[i], mvv_t[i]
                            csl = slice(ch * 512, (ch + 1) * 512)
                            fw.dma("sp", kn[:], KTn_o[h][r * 128:(r + 1) * 128, csl], [KTn_o[h]], [kn])
                            fw.dma("sp", krt[:], KTr_o[h][r * 64:(r + 1) * 64, csl], [KTr_o[h]], [krt])
                            vrow = r * (TT * 128) + ch * 128
                            fw.dma("sp", vv[:], VM_o[h][vrow:vrow + 128, :], [VM_o[h]], [vv])
                            tail = (ch == g)
                            for jb in range(4):
                                c0 = jb * 128 if tail else 0
                                last = (r == 3 and ch == g and jb == 3)
                                s_ = ps[2]
                                pT = pT_t[cnt["sm"] % NP]
                                cnt["sm"] += 1
                                ksl = slice(jb * 128, (jb + 1) * 128)
                                mm(s_[:, c0:], kn[:, ksl], qn[:, c0:], True, False, [kn, qn], [s_])
                                mm(s_[:, c0:], krt[:, ksl], qr[:, c0:], False, True, [krt, qr], [s_])
                                yield
                                actf(pT[:, c0:], s_[:, c0:], AF.Exp, [s_], [pT], scale=SC_MLA)
                                if tail:
                                    ptt(pT[:, c0:c0 + 128], pT[:, c0:c0 + 128], mcaus[:, r, :], ALU.mult, [pT, mcaus], [pT])
                                ptt(pacc[:, c0:], pacc[:, c0:], pT[:, c0:], ALU.add, [pacc, pT], [pacc])
                                mm(oacc_m[:, c0:], vv[:, ksl], pT[:, c0:], first, last, [vv, pT], [oacc_m])
                                first = False
                                yield
                    prs = ps[2]
                    mm(prs[:], onesf[:], pacc[:], True, True, [onesf, pacc], [prs])
                    fw.dve(lambda: nc.vector.reciprocal(out=rinv[:], in_=prs[:]), [prs], [rinv])
                    yt = ytm_t[cnt["ym"] % 2]
                    cnt["ym"] += 1
                    tt(yt[:], oacc_m[:], rinv[:], ALU.mult, [oacc_m, rinv], [yt])
                    fw.dma("pl", YTm[h, :, gsl], yt[:], [yt], [YTm])

                def sb_sweep(g, h, st):
                    gsl = slice(g * 512, (g + 1) * 512)
                    rps, oacc_s = st.rps, st.oacc
                    qs = st.qs_t[st.nq % 2]
                    st.nq += 1
                    fw.dma("sp", qs[:], QTs[h, :, gsl], [QTs], [qs])
                    sfirst = True
                    pend = []

                    def flush_pv():
                        while pend:
                            pc0, pv, pk, pa, pf, pl = pend.pop(0)
                            mm(oacc_s[:, pc0:], pv[:, pk], pa[:, pc0:], pf, pl, [pv, pa], [oacc_s])

                    for ch in range(g, -1, -1):
                        csl = slice(ch * 512, (ch + 1) * 512)
                        kss, vss = [], []
                        for r in range(4):
                            i = st.nk % NS
                            st.nk += 1
                            kn, vv = st.kn_t[i], st.vv_t[i]
                            fw.dma("sp", kn[:], KTs_o[h][r * 128:(r + 1) * 128, csl], [KTs_o[h]], [kn])
                            vrow = r * (TT * 128) + ch * 128
                            fw.dma("sp", vv[:], VS_o[h][vrow:vrow + 128, :], [VS_o[h]], [vv])
                            kss.append(kn)
                            vss.append(vv)
                        tail = (ch == g)
                        for jb in range(3, -1, -1):
                            c0 = jb * 128 if tail else 0
                            ksl = slice(jb * 128, (jb + 1) * 128)
                            for r in range(3, -1, -1):
                                last = (ch == 0 and jb == 0 and r == 0)
                                k3 = st.nsx % NE
                                st.nsx += 1
                                z = st.zb
                                e, lp, hi, lo, ex, aa = st.e_t[k3], st.lp_t[k3], st.hi_t[k3], st.lo_t[k3], st.ex_t[k3], st.a_t[k3]
                                mm(z[:, c0:], kss[r][:, ksl], qs[:, c0:], True, True, [kss[r], qs], [z])
                                flush_pv()
                                yield
                                actf(e[:, c0:], z[:, c0:], AF.Exp, [z], [e], scale=SC_SB)
                                actf(lp[:, c0:], e[:, c0:], AF.Ln, [e, cst], [lp], bias=cst[:, 1:2])
                                if tail:
                                    tt(lp[:, c0:c0 + 128], lp[:, c0:c0 + 128], mstr[:, r, :], ALU.mult, [lp, mstr], [lp])
                                    ptt(e[:, c0:c0 + 128], e[:, c0:c0 + 128], mstr[:, r, :], ALU.mult, [e, mstr], [e])
                                fw.dve(lambda: nc.vector.tensor_copy(out=hi[:, c0:], in_=lp[:, c0:]), [lp], [hi])
                                tt(lo[:, c0:], lp[:, c0:], hi[:, c0:], ALU.subtract, [lp, hi], [lo])
                                yield
                                mm(rps[:, c0:], umat[:], hi[:, c0:], sfirst, False, [umat, hi], [rps])
                                mm(rps[:, c0:], umat[:], lo[:, c0:], False, True, [umat, lo], [rps])
                                yield
                                actf(ex[:, c0:], rps[:, c0:], AF.Exp, [rps], [ex], scale=-1.0)
                                mm(rps[:, c0:], ubar[:], hi[:, c0:], False, False, [ubar, hi], [rps])
                                mm(rps[:, c0:], ubar[:], lo[:, c0:], False, True, [ubar, lo], [rps])
                                tt(aa[:, c0:], e[:, c0:], ex[:, c0:], ALU.mult, [e, ex], [aa])
                                pend.append((c0, vss[r], ksl, aa, sfirst, last))
                                sfirst = False
                                yield
                    flush_pv()
                    yt = st.yt_t[st.ny % 2]
                    st.ny += 1
                    actf(yt[:], oacc_s[:], AF.Identity, [oacc_s], [yt])
                    fw.dma("pl", YTs[h, :, gsl], yt[:], [yt], [YTs])

                def chain(*gens):
                    for gen in gens:
                        yield from gen

                for g in range(TT):
                    for hp in range(2):
                        if l + 1 < L and g == min(1, TT - 1) and hp == 1:
                            issue_weight_cc(l + 1)
                        h0, h1 = 2 * hp, 2 * hp + 1
                        gens = [sb_sweep(g, h0, stA), chain(mla_sweep(g, h0), mla_sweep(g, h1)), sb_sweep(g, h1, stB)]
                        done = [False, False, False]
                        next(gens[0])
                        next(gens[0])
                        kk = 0
                        while not all(done):
                            kk += 1
                            order = [0, 1, 2]
                            for gi in order:
                                if not done[gi]:
                                    try:
                                        next(gens[gi])
                                    except StopIteration:
                                        done[gi] = True
            fw.barrier()
            chk("p2")
            with contextlib.ExitStack() as es:
                wbm = SB(es, [128, 4, D], BF16)
                wbs = SB(es, [128, 4, D], BF16)
                wo = SB(es, [128, 8, D], BF16)
                gtb = SB(es, [128, D], F32)
                for kc in range(4):
                    fw.dma("pl", wbm[:, kc, :], w_bm[l, kc * 128:(kc + 1) * 128, :], [w_bm.tl[l]], [wbm])
                    fw.dma("pl", wbs[:, kc, :], w_bs[l, kc * 128:(kc + 1) * 128, :], [w_bs.tl[l]], [wbs])
                for kc in range(8):
                    fw.dma("pl", wo[:, kc, :], w_out[l, kc * 128:(kc + 1) * 128, :], [w_out.tl[l]], [wo])
                fw.dma("sp", gtb[:], GR_o[1:2, l * 1536 + 512:(l + 1) * 1536].partition_broadcast(128), [GR_o], [gtb])
                ym = SB(es, [128, 4, 512], BF16)
                ysb = SB(es, [128, 4, 512], BF16)
                gat = SB(es, [128, 16, 512], BF16)
                xt = SB(es, [128, 4, D], F32)
                sq = SB(es, [128, 4, D], F32)
                xn = SB(es, [128, 4, D], BF16)
                ss = SB(es, [128, 4], F32)
                rstd = SB(es, [128, 4], F32)
                mg = SB(es, [128, 8, 512], BF16)
                m1 = SB(es, [128, 512], F32)
                m2 = SB(es, [128, 512], F32)
                hT = SB(es, [128, 8, 512], BF16)
                hb = SB(es, [128, 8, NB, 2], BF16)
                for t in range(TT):
                    sl = slice(t * 512, (t + 1) * 512)
                    fw.dma("sp", ym[:], YTm[:, :, sl].rearrange("h p n -> p h n"), [YTm], [ym])
                    fw.dma("sp", ysb[:], YTs[:, :, sl].rearrange("h p n -> p h n"), [YTs], [ysb])
                    fw.dma("sp", gat[:], GA[t].rearrange("p (j n) -> p j n", j=16), [GA], [gat])
                    fw.dma("sp", xt[:], x_src[sl, :].rearrange("(b p) d -> p b d", p=128), [x_src], [xt])
                    for oc in range(8):
                        pa = nextps()
                        for kc in range(4):
                            mm(pa[:], wbm[:, kc, oc * 128:(oc + 1) * 128], ym[:, kc, :], kc == 0, kc == 3, [wbm, ym], [pa])
                        pb = nextps()
                        for kc in range(4):
                            mm(pb[:], wbs[:, kc, oc * 128:(oc + 1) * 128], ysb[:, kc, :], kc == 0, kc == 3, [wbs, ysb], [pb])
                        tt(m1[:], pa[:], gat[:, oc, :], ALU.mult, [pa, gat], [m1])
                        tt(m2[:], pb[:], gat[:, 8 + oc, :], ALU.mult, [pb, gat], [m2])
                        tt(mg[:, oc, :], m1[:], m2[:], ALU.add, [m1, m2], [mg])
                    for b in range(4):
                        for half in range(2):
                            p = nextps()
                            for kc in range(8):
                                mm(p[:], mg[:, kc, b * 128:(b + 1) * 128], wo[:, kc, half * 512:(half + 1) * 512],
                                   kc == 0, kc == 7, [mg, wo], [p])
                            hs = slice(half * 512, (half + 1) * 512)
                            tt(m1[:], p[:], gtb[:, hs], ALU.mult, [p, gtb], [m1])
                            tt(xt[:, b, hs], xt[:, b, hs], m1[:], ALU.add, [xt, m1], [xt])
                    fw.dma("sp", X1[sl, :].rearrange("(b p) d -> p b d", p=128), xt[:], [xt], [X1])
                    norm_hT(xt, sq, ss, rstd, xn, A2[:, l, :], modF[:, l, 24:32], hT, [A2, modF])
                    fw.dma("sp", H2[t].rearrange("p (k n) -> p k n", k=8), hT[:], [hT], [H2])
                    for b in range(4):
                        fw.act(lambda: nc.scalar.copy(out=hb[:, :, t * 4 + b, :], in_=hT[:, :, b * 128 + 126:b * 128 + 128]),
                               [hT], [hb])
                fw.dma("sp", HB_i[:, :].rearrange("p (k m e) -> p k m e", k=8, m=NB), hb[:], [hb], [HB_i])
            fw.barrier()
            chk("p3")
            allgather(HB_i, HB_o)

            with contextlib.ExitStack() as es:
                wu = SB(es, [128, 8, 2 * DFF], BF16)
                wd = SB(es, [128, 22, D], BF16)
                wcv = SB(es, [128, 44, 3], F32)
                bcv = SB(es, [128, 44], F32)
                gtb = SB(es, [128, D], F32)
                for kc in range(8):
                    fw.dma("pl", wu[:, kc, :], w_up[l, kc * 128:(kc + 1) * 128, :], [w_up.tl[l]], [wu])
                for kc in range(22):
                    fw.dma("pl", wd[:, kc, :], w_down[l, kc * 128:(kc + 1) * 128, :], [w_down.tl[l]], [wd])
                fw.dma("sp", wcv[:], wconv_f[l], [wconv_f], [wcv])
                fw.dma("sp", bcv[:], bconv_f[l], [bconv_f], [bcv])
                fw.dma("sp", gtb[:], GR_o[3:4, l * 1536 + 512:(l + 1) * 1536].partition_broadcast(128), [GR_o], [gtb])
                hT = SB(es, [128, 8, 512], BF16)
                hselb = SB(es, [128, 8, NB, 2], BF16)
                uek = [0]
                cva = [SB(es, [128, 4, 128], F32) for _ in range(2)]
                sil = SB(es, [128, 4, 128], F32)
                ff = SB(es, [128, 22, 512], BF16)
                xt = SB(es, [128, 4, D], F32)
                m1 = SB(es, [128, 512], F32)
                es2 = contextlib.ExitStack()
                hcand = SB(es2, [128, 4, 8, NB, 2], BF16)
                hsel = SB(es2, [128, 8, NB, 2], F32)
                fw.dma("sp", hcand[:].rearrange("p r k m e -> p r (k m e)"),
                       HB_o[:, :].rearrange("(r p) n -> p r n", p=128), [HB_o], [hcand])
                ts(hsel[:], hcand[:, 0], halow[:, 0:1], None, ALU.mult, None, [hcand, halow], [hsel])
                for r in range(1, 4):
                    stt(hsel[:], hcand[:, r], halow[:, r:r + 1], hsel[:], ALU.mult, ALU.add, [hcand, halow, hsel], [hsel])
                if NB > 1:
                    stt(hsel[:, :, 1:NB, :], hcand[:, 3, :, 0:NB - 1, :], halow[:, 4:5], hsel[:, :, 1:NB, :],
                        ALU.mult, ALU.add, [hcand, halow, hsel], [hsel])
                fw.dve(lambda: nc.vector.tensor_copy(out=hselb[:], in_=hsel[:]), [hsel], [hselb])
                fw.barrier()
                es2.close()
                ue_t = [SB(es, [128, 4, 130], F32) for _ in range(2)]
                for t in range(TT):
                    sl = slice(t * 512, (t + 1) * 512)
                    fw.dma("sp", hT[:], H2[t].rearrange("p (k n) -> p k n", k=8), [H2], [hT])
                    fw.dma("sp", xt[:], X1[sl, :].rearrange("(b p) d -> p b d", p=128), [X1], [xt])
                    for j in range(22):
                        for which, oc in ((0, j), (1, 22 + j)):
                            p = nextps()
                            for kc in range(8):
                                mm(p[:], wu[:, kc, oc * 128:(oc + 1) * 128], hT[:, kc, :], kc == 0, kc == 7, [wu, hT], [p])
                            ph = nextps()
                            for kc in range(8):
                                mm(ph[:, 0:8], wu[:, kc, oc * 128:(oc + 1) * 128], hselb[:, kc, t * 4:(t + 1) * 4, :],
                                   kc == 0, kc == 7, [wu, hselb], [ph])
                            ue = ue_t[uek[0] % 2]
                            uek[0] += 1
                            cv = cva[which]
                            actf(ue[:, :, 2:130], p[:].rearrange("p (b n) -> p b n", b=4), AF.Identity, [p], [ue])
                            actf(ue[:, :, 0:2], ph[:, 0:8].rearrange("p (b e) -> p b e", b=4), AF.Identity, [ph], [ue])
                            actf(cv[:], p[:].rearrange("p (b n) -> p b n", b=4), AF.Identity, [p, wcv, bcv], [cv],
                                 scale=wcv[:, oc, 2:3], bias=bcv[:, oc:oc + 1])
                            stt(cv[:], ue[:, :, 1:129], wcv[:, oc, 1:2], cv[:], ALU.mult, ALU.add, [ue, wcv, cv], [cv])
                            stt(cv[:], ue[:, :, 0:128], wcv[:, oc, 0:1], cv[:], ALU.mult, ALU.add, [ue, wcv, cv], [cv])
                        actf(sil[:], cva[0][:], AF.Silu, [cva[0]], [sil])
                        tt(ff[:, j, :].rearrange("p (b n) -> p b n", b=4), sil[:], cva[1][:], ALU.mult, [sil, cva[1]], [ff])
                    for b in range(4):
                        for half in range(2):
                            p = nextps()
                            for kc in range(22):
                                mm(p[:], ff[:, kc, b * 128:(b + 1) * 128], wd[:, kc, half * 512:(half + 1) * 512],
                                   kc == 0, kc == 21, [ff, wd], [p])
                            hs = slice(half * 512, (half + 1) * 512)
                            tt(m1[:], p[:], gtb[:, hs], ALU.mult, [p, gtb], [m1])
                            tt(xt[:, b, hs], xt[:, b, hs], m1[:], ALU.add, [xt, m1], [xt])
                    fw.dma("sp", x_dst[sl, :].rearrange("(b p) d -> p b d", p=128), xt[:], [xt], [x_dst])
            fw.barrier()
        except StopBuild:
            pass
        fw.stopped = False
        for name in dump:
            t, shape, dt = scr[name]
            o = Tl(nc.dram_tensor("dbg_" + name, shape, dt, kind="ExternalOutput").ap())
            fw.dma("sp", o.t, t.t, [t], [o])
        fw.finish()
    if os.environ.get("KVERBOSE"):
        print("instr counts: pe", fw.sem_pe.count, "act", fw.sem_act.count, "dve", fw.sem_dve.count, "pool", fw.sem_pool.count,
              "cc", fw.sem_cc.count, "dma_sp", fw.n_sp, "dma_pl", fw.n_pl, "total(with waits)", fw.n_ins, flush=True)
    return nc


def _prep_inputs(inp, S, L):
    B = inp["x"].shape[0]
    NB = S // 512
    f32 = np.float32
    ident = np.eye(128, dtype=f32)
    kk = np.arange(128)
    umat = (kk[:, None] >= kk[None, :]).astype(f32)
    ubar = (kk[:, None] < kk[None, :]).astype(f32)
    tri_le = (kk[:, None] <= kk[None, :]).astype(f32)
    tri_lt = (kk[:, None] < kk[None, :]).astype(f32)
    invf = (1.0 / (10000.0 ** (np.arange(32, dtype=f32) * f32(2.0 / 64)))).astype(f32)
    invf2 = np.concatenate([invf, invf]).reshape(64, 1).astype(f32)

    def fm(v, n):
        return np.ascontiguousarray(v.reshape(L, n, 128).transpose(0, 2, 1))

    b_in = np.asarray(inp["b_in"], f32)
    big = np.zeros((L, 128, NG), f32)
    for gi, (c0, M) in enumerate(GROUPS):
        if gi == 4:
            big[:, 0:32, gi] = b_in[:, 416:448]
            big[:, 32:64, gi] = b_in[:, 384:416]
        else:
            big[:, 0:M, gi] = b_in[:, c0:c0 + M]

    def headg(g):
        o = np.zeros((L, 128, 3), f32)
        o[:, :, 0] = g[:, 0:128]
        o[:, 0:64, 1] = g[:, 128:192]
        o[:, 0:32, 2] = g[:, 160:192]
        o[:, 32:64, 2] = g[:, 128:160]
        return o

    shared = dict(
        ident=ident, umat=umat, ubar=ubar, invf=invf2,
        g1=fm(np.asarray(inp["g_norm1"], f32), 8), g2=fm(np.asarray(inp["g_norm2"], f32), 8),
        b_in_g=big, b_in=b_in,
        gql=fm(np.asarray(inp["g_q_lat"], f32), 2), gkvl=fm(np.asarray(inp["g_kv_lat"], f32), 1),
        gqh=headg(np.asarray(inp["g_q_head"], f32)), gkh=headg(np.asarray(inp["g_k_head"], f32)),
        wconv_f=np.ascontiguousarray(np.asarray(inp["w_conv"], f32).reshape(L, 3, 44, 128).transpose(0, 3, 2, 1)),
        bconv_f=fm(np.asarray(inp["b_conv"], f32), 44),
    )
    wrep = dict(w_uq="w_uq", w_ukv="w_ukv", w_bm="w_branch_mla", w_bs="w_branch_sb", w_out="w_out", w_down="w_down")
    for k, v in wrep.items():
        w = np.asarray(inp[v], f32)
        for l in range(L):
            shared[f"{k}_l{l}"] = w[l]
    wshard = {k: np.asarray(inp[k], f32) for k in ("w_in", "w_up")}
    w_ada_np = np.asarray(inp["w_ada"], f32)
    b_ada_np = np.asarray(inp["b_ada"], f32)
    x = np.asarray(inp["x"], f32)
    c = np.asarray(inp["c"], f32)
    positions = np.asarray(inp["positions"], np.int32)
    in_maps = []
    for r in range(8):
        b, cc = r // 4, r % 4
        xb = x[b].reshape(NB, 4, 128, D)[:, cc].reshape(NB * 128, D)
        pb = positions[b].reshape(NB, 4, 128)[:, cc].reshape(1, NB * 128)
        mca = np.zeros((128, 4, 128), f32)
        mst = np.zeros((128, 4, 128), f32)
        for rr in range(4):
            if rr < cc:
                mca[:, rr, :] = 1.0
                mst[:, rr, :] = 1.0
            elif rr == cc:
                mca[:, rr, :] = tri_le
                mst[:, rr, :] = tri_lt
        hw = np.zeros((128, 5), f32)
        if cc >= 1:
            hw[:, cc - 1] = 1.0
        else:
            hw[:, 4] = 1.0
        m = dict(shared)
        for k, w in wshard.items():
            K, N = w.shape[1], w.shape[2]
            for l in range(L):
                m[f"{k}_sh{l}"] = np.ascontiguousarray(w[l].reshape(K // 128, 4, 32, N)[:, cc].reshape(K // 4, N))
        for l in range(L):
            m[f"w_ada_q{l}"] = np.ascontiguousarray(w_ada_np[l][:, cc * 1536:(cc + 1) * 1536])
        m["b_ada_fq"] = np.ascontiguousarray(b_ada_np[:, cc * 1536:(cc + 1) * 1536].reshape(L, 12, 128).transpose(0, 2, 1))
        m["b_ada_q"] = np.ascontiguousarray(b_ada_np[:, cc * 1536:(cc + 1) * 1536])
        m.update(x=np.ascontiguousarray(xb), pos=np.ascontiguousarray(pb),
                 c=np.ascontiguousarray(c[b].reshape(8, 128).T), mcaus=mca, mstrict=mst, halow=hw)
        in_maps.append(m)
    return in_maps


def run(inp, S, L, stop_after=None, trace=False, dump=()):
    nc = build(S, L, stop_after, dump)
    in_maps = _prep_inputs(inp, S, L)
    res = run_bass_kernel_spmd(nc, in_maps, core_ids=list(range(8)), trace=trace)
    NB = S // 512
    B = 2
    out = np.zeros((B, S, D), np.float32)
    ov = out.reshape(B, NB, 4, 128, D)
    for r in range(8):
        b, cc = r // 4, r % 4
        ov[b, :, cc] = np.asarray(res.results[r]["y"]).reshape(NB, 128, D)
    return out, res


def kernel(**inputs):
    S = inputs["x"].shape[1]
    L = inputs["w_in"].shape[0]
    out, _ = run(inputs, S, L)
    return out
```

```python
import contextlib
import math
import numpy as np
import concourse.bass as bass
import concourse.mybir as mybir
from concourse.bass_utils import run_bass_kernel_spmd

F32 = mybir.dt.float32
BF16 = mybir.dt.bfloat16
I32 = mybir.dt.int32
AF = mybir.ActivationFunctionType
ALU = mybir.AluOpType

import os
SAFE = bool(int(os.environ.get("KSAFE", "0")))
D = 1024
DIN = 4032
DFF = 2816
EPS = 1e-6
SC_MLA = 192 ** -0.5
SC_SB = 128 ** -0.5
MAGIC = 12582912.0
TWO_PI = 2.0 * math.pi
C1 = 6.28125
C2 = float(np.float32(TWO_PI - C1))
C3 = float(TWO_PI - C1 - C2)
PI_LO = 3.1415925

GROUPS = [(0, 128), (128, 128), (256, 128), (384, 64), (4032, 64)]
GROUPS += [(448 + 128 * h, 128) for h in range(4)]
GROUPS += [(960 + 128 * h, 128) for h in range(4)]
GROUPS += [(1984 + 128 * j, 128) for j in range(8)]
GROUPS += [(3008 + 128 * j, 128) for j in range(8)]
NG = len(GROUPS)


NSLOT = 16


class StopBuild(Exception):
    pass


class Sem:
    def __init__(self, nc, name, inc):
        self.h = nc.alloc_semaphore(name=name)
        self.inc = inc
        self.count = 0


class Stream:
    def __init__(self, eng):
        self.eng = eng
        self.waited = {}


class Res:
    __slots__ = ("w", "r")

    def __init__(self):
        self.w = None
        self.r = {}


class Tl:
    def __init__(self, t, ex=False):
        self.t = t
        self.r = Res()
        self.ex = ex

    def __getitem__(self, k):
        return self.t[k]


class FW:
    def __init__(self, nc):
        self.nc = nc
        self.S_pe = Stream(nc.tensor)
        self.S_act = Stream(nc.scalar)
        self.S_dve = Stream(nc.vector)
        self.S_pool = Stream(nc.gpsimd)
        self.S_sp = Stream(nc.sync)
        self.sem_pe = Sem(nc, "s_pe", 1)
        self.sem_act = Sem(nc, "s_act", 1)
        self.sem_dve = Sem(nc, "s_dve", 1)
        self.sem_pool = Sem(nc, "s_pool", 1)
        self.sem_cc = Sem(nc, "s_cc", 1)
        self.dq_sp = [Sem(nc, f"s_dsp{i}", 16) for i in range(NSLOT)]
        self.dq_pl = [Sem(nc, f"s_dpl{i}", 16) for i in range(NSLOT)]
        self.n_sp = 0
        self.n_pl = 0
        self.n_ins = 0
        self.stopped = False

    def _issue(self, stream, sem, fn, reads, writes):
        if self.stopped:
            return None
        exr = [t for t in reads if t.ex]
        if exr:
            reads = [t for t in reads if not t.ex]
            writes = list(writes) + [t for t in exr if t not in writes]
        need = {}
        for t in reads:
            w = t.r.w
            if w is not None and need.get(w[0], 0) < w[1]:
                need[w[0]] = w[1]
        for t in writes:
            w = t.r.w
            if w is not None and need.get(w[0], 0) < w[1]:
                need[w[0]] = w[1]
            for sm, c in t.r.r.items():
                if need.get(sm, 0) < c:
                    need[sm] = c
        if SAFE:
            for sm in self.all_sems():
                if sm.count > 0:
                    need[sm] = sm.count
        for sm, c in need.items():
            if sm is self.sem_pe and stream is self.S_pe and not SAFE:
                continue
            if stream.waited.get(sm, 0) < c:
                stream.eng.wait_ge(sm.h, c * sm.inc)
                stream.waited[sm] = c
                self.n_ins += 1
        ins = fn()
        sem.count += 1
        ins.then_inc(sem.h, sem.inc)
        self.n_ins += 1
        for t in reads:
            if t.r.r.get(sem, 0) < sem.count:
                t.r.r[sem] = sem.count
        for t in writes:
            t.r.w = (sem, sem.count)
            t.r.r = {}
        return ins

    def barrier(self):
        if self.stopped:
            return
        for stream in (self.S_pe, self.S_act, self.S_dve, self.S_pool, self.S_sp):
            for sm in self.all_sems():
                if sm.count > 0 and stream.waited.get(sm, 0) < sm.count:
                    stream.eng.wait_ge(sm.h, sm.count * sm.inc)
                    stream.waited[sm] = sm.count
                    self.n_ins += 1

    def all_sems(self):
        return self.dq_sp + self.dq_pl + [self.sem_cc, self.sem_pe, self.sem_act, self.sem_dve, self.sem_pool]

    def pe(self, fn, reads, writes):
        return self._issue(self.S_pe, self.sem_pe, fn, reads, writes)

    def act(self, fn, reads, writes):
        return self._issue(self.S_act, self.sem_act, fn, reads, writes)

    def dve(self, fn, reads, writes):
        return self._issue(self.S_dve, self.sem_dve, fn, reads, writes)

    def pool(self, fn, reads, writes):
        return self._issue(self.S_pool, self.sem_pool, fn, reads, writes)

    def cc(self, fn, reads, writes):
        return self._issue(self.S_pool, self.sem_cc, fn, reads, writes)

    def dma(self, which, out, in_, reads, writes):
        nc = self.nc
        if which == "sp":
            stream, sem, eng = self.S_sp, self.dq_sp[self.n_sp % NSLOT], nc.sync
            self.n_sp += 1
        else:
            stream, sem, eng = self.S_pool, self.dq_pl[self.n_pl % NSLOT], nc.gpsimd
            self.n_pl += 1
        if self.stopped:
            return None
        if sem.count > 0 and stream.waited.get(sem, 0) < sem.count:
            stream.eng.wait_ge(sem.h, sem.count * 16)
            stream.waited[sem] = sem.count
        return self._issue(stream, sem, lambda: eng.dma_start(out=out, in_=in_), reads, writes)

    def finish(self):
        for sem in self.dq_sp + self.dq_pl + [self.sem_cc, self.sem_pe, self.sem_act, self.sem_dve, self.sem_pool]:
            if sem.count > 0:
                self.nc.sync.wait_ge(sem.h, sem.count * sem.inc)


def build(S, L, stop_after=None, dump=()):
    NB = S // 512
    TT = NB // 4
    NT = NB * 128
    nc = bass.Bass("TRN2", target_bir_lowering=False)
    fw = FW(nc)

    def din(name, shape, dt=F32):
        return Tl(nc.dram_tensor(name, list(shape), dt, kind="ExternalInput").ap())

    scr = {}

    def dscr(name, shape, dt):
        t = Tl(nc.dram_tensor(name, list(shape), dt).ap())
        scr[name] = (t, list(shape), dt)
        return t

    RG = [[0, 1, 2, 3], [4, 5, 6, 7]]
    gathers = []

    class WStack:
        def __init__(self, name, K, N, shard=False):
            self.shard, self.K, self.N = shard, K, N
            if shard:
                self.sh = [din(f"{name}_sh{l}", [K // 4, N]) for l in range(L)]
                self.bn = [Tl(nc.dram_tensor(f"{name}_bn{l}", [K // 4, N], F32).ap()) for l in range(L)]
                self.tl = [Tl(nc.dram_tensor(f"{name}_full{l}", [K, N], F32).ap()) for l in range(L)]
                gathers.append(self)
            else:
                self.tl = [din(f"{name}_l{l}", [K, N]) for l in range(L)]

        def __getitem__(self, key):
            if isinstance(key, tuple):
                return self.tl[key[0]].t[key[1:]]
            return self.tl[key].t

    x_in = din("x", [NT, D])
    pos = din("pos", [1, NT], I32)
    c_in = din("c", [128, 8])
    ident_in = din("ident", [128, 128])
    umat_in = din("umat", [128, 128])
    ubar_in = din("ubar", [128, 128])
    mcaus_in = din("mcaus", [128, 4, 128])
    mstr_in = din("mstrict", [128, 4, 128])
    halow_in = din("halow", [128, 5])
    invf_in = din("invf", [64, 1])
    w_ada_q = [din(f"w_ada_q{l}", [D, 1536]) for l in range(L)]
    b_ada_fq = din("b_ada_fq", [L, 128, 12])
    b_ada_q = din("b_ada_q", [L, 1536])
    g1_in = din("g1", [L, 128, 8])
    g2_in = din("g2", [L, 128, 8])
    w_in = WStack("w_in", D, DIN, shard=True)
    b_in_g = din("b_in_g", [L, 128, NG])
    b_in = din("b_in", [L, DIN])
    gql_in = din("gql", [L, 128, 2])
    gkvl_in = din("gkvl", [L, 128, 1])
    w_uq = WStack("w_uq", 256, 768)
    w_ukv = WStack("w_ukv", 128, 1024)
    gqh_in = din("gqh", [L, 128, 3])
    gkh_in = din("gkh", [L, 128, 3])
    w_bm = WStack("w_bm", 512, D)
    w_bs = WStack("w_bs", 512, D)
    w_out = WStack("w_out", D, D)
    w_up = WStack("w_up", D, 2 * DFF, shard=True)
    wconv_f = din("wconv_f", [L, 128, 44, 3])
    bconv_f = din("bconv_f", [L, 128, 44])
    w_down = WStack("w_down", DFF, D)
    y_out = Tl(nc.dram_tensor("y", [NT, D], F32, kind="ExternalOutput").ap())

    X1 = dscr("X1", [NT, D], F32)
    XA = dscr("XA", [NT, D], F32)
    MF_i = dscr("MF_i", [128, L * 12], F32)
    MF_o = dscr("MF_o", [4 * 128, L * 12], F32)
    GR_i = dscr("GR_i", [1, L * 1536], F32)
    GR_o = dscr("GR_o", [4, L * 1536], F32)
    CS = dscr("CS", [2, 64, NT], F32)
    QTn = dscr("QTn", [4, 128, NT], BF16)
    QTr = dscr("QTr", [4, 64, NT], BF16)
    QTs = dscr("QTs", [4, 128, NT], BF16)
    GA = dscr("GA", [TT, 128, 16 * 512], BF16)
    YTm = dscr("YTm", [4, 128, NT], BF16)
    YTs = dscr("YTs", [4, 128, NT], BF16)
    H2 = dscr("H2", [TT, 128, 8 * 512], BF16)
    KTn_i = [dscr(f"KTn_i{h}", [128, NT], BF16) for h in range(4)]
    KTn_o = [dscr(f"KTn_o{h}", [4 * 128, NT], BF16) for h in range(4)]
    KTr_i = [dscr(f"KTr_i{h}", [64, NT], BF16) for h in range(4)]
    KTr_o = [dscr(f"KTr_o{h}", [4 * 64, NT], BF16) for h in range(4)]
    KTs_i = [dscr(f"KTs_i{h}", [128, NT], BF16) for h in range(4)]
    KTs_o = [dscr(f"KTs_o{h}", [4 * 128, NT], BF16) for h in range(4)]
    VM_i = [dscr(f"VM_i{h}", [TT * 128, 512], BF16) for h in range(4)]
    VM_o = [dscr(f"VM_o{h}", [4 * TT * 128, 512], BF16) for h in range(4)]
    VS_i = [dscr(f"VS_i{h}", [TT * 128, 512], BF16) for h in range(4)]
    VS_o = [dscr(f"VS_o{h}", [4 * TT * 128, 512], BF16) for h in range(4)]
    HB_i = dscr("HB_i", [128, 8 * NB * 2], BF16)
    HB_o = dscr("HB_o", [4 * 128, 8 * NB * 2], BF16)
    RG = [[0, 1, 2, 3], [4, 5, 6, 7]]

    def allgather(src, dst):
        fw.cc(lambda: nc.gpsimd.collective_compute("AllGather", ALU.bypass, replica_groups=RG,
                                                   ins=[src.t.opt()], outs=[dst.t.opt()]),
              reads=[src], writes=[dst])

    with contextlib.ExitStack() as top:
        nm = [0]

        def SB(es, shape, dt):
            nm[0] += 1
            return Tl(es.enter_context(nc.sbuf_tensor(f"sb{nm[0]}", list(shape), dt)))

        def PS(es, shape, dt):
            nm[0] += 1
            return Tl(es.enter_context(nc.psum_tensor(f"ps{nm[0]}", list(shape), dt)), ex=True)

        ps = [PS(top, [128, 512], F32) for _ in range(8)]
        pt = []
        for i in (6, 7):
            v = Tl(ps[i].t[:].bitcast(BF16), ex=True)
            v.r = ps[i].r
            pt.append(v)
        rot = [0]

        def nextps(n=6):
            rot[0] = (rot[0] + 1) % n
            return ps[rot[0]]

        def mm(out, lhsT, rhs, start, stop, reads, writes):
            fw.pe(lambda: nc.tensor.matmul(out, lhsT=lhsT, rhs=rhs, start=start, stop=stop,
                                           skip_group_check=True), reads, writes)

        def actf(out, in_, func, reads, writes, scale=None, bias=None):
            kw = {}
            if scale is not None:
                kw["scale"] = scale
            if bias is not None:
                kw["bias"] = bias
            fw.act(lambda: nc.scalar.activation(out=out, in_=in_, func=func, **kw), reads, writes)

        def ts(out, in0, s1, s2, op0, op1, reads, writes, eng="dve"):
            if op1 is None:
                f = lambda: nc.vector.tensor_scalar(out=out, in0=in0, scalar1=s1, scalar2=None, op0=op0)
            else:
                f = lambda: nc.vector.tensor_scalar(out=out, in0=in0, scalar1=s1, scalar2=s2, op0=op0, op1=op1)
            fw.dve(f, reads, writes)

        def stt(out, in0, scalar, in1, op0, op1, reads, writes):
            fw.dve(lambda: nc.vector.scalar_tensor_tensor(out=out, in0=in0, scalar=scalar, in1=in1, op0=op0, op1=op1),
                   reads, writes)

        def tt(out, in0, in1, op, reads, writes):
            fw.dve(lambda: nc.vector.tensor_tensor(out=out, in0=in0, in1=in1, op=op), reads, writes)

        def ptt(out, in0, in1, op, reads, writes):
            fw.pool(lambda: nc.gpsimd.tensor_tensor(out=out, in0=in0, in1=in1, op=op), reads, writes)

        cst = SB(top, [128, 8], F32)
        fw.dve(lambda: nc.vector.memset(cst[:, 0:1], EPS), [], [cst])
        fw.dve(lambda: nc.vector.memset(cst[:, 1:2], 1.0), [], [cst])
        fw.dve(lambda: nc.vector.memset(cst[:, 2:3], math.pi / 2), [], [cst])
        fw.dve(lambda: nc.vector.memset(cst[:, 3:4], 0.0), [], [cst])
        ident_f = SB(top, [128, 128], F32)
        identb = SB(top, [128, 128], BF16)
        umat = SB(top, [128, 128], BF16)
        ubar = SB(top, [128, 128], BF16)
        onesb = SB(top, [128, 128], BF16)
        onesf = SB(top, [128, 128], F32)
        mcaus = SB(top, [128, 4, 128], BF16)
        mstr = SB(top, [128, 4, 128], F32)
        halow = SB(top, [128, 5], F32)
        invf = SB(top, [64, 1], F32)
        fw.dma("sp", ident_f[:], ident_in[:], [ident_in], [ident_f])
        fw.dve(lambda: nc.vector.tensor_copy(out=identb[:], in_=ident_f[:]), [ident_f], [identb])
        fw.dma("pl", umat[:], umat_in[:], [umat_in], [umat])
        fw.dma("pl", ubar[:], ubar_in[:], [ubar_in], [ubar])
        fw.dma("pl", mcaus[:], mcaus_in[:], [mcaus_in], [mcaus])
        fw.dma("sp", mstr[:], mstr_in[:], [mstr_in], [mstr])
        fw.dma("sp", halow[:], halow_in[:], [halow_in], [halow])
        fw.dma("sp", invf[:], invf_in[:], [invf_in], [invf])
        fw.dve(lambda: nc.vector.memset(onesb[:], 1.0), [], [onesb])
        fw.dve(lambda: nc.vector.memset(onesf[:], 1.0), [], [onesf])
        modF = SB(top, [128, L, 48], F32)
        A1 = SB(top, [128, L, 8], F32)
        A2 = SB(top, [128, L, 8], F32)

        def issue_weight_bounce(l):
            for ws in gathers:
                fw.dma("sp", ws.bn[l].t, ws.sh[l].t, [ws.sh[l]], [ws.bn[l]])

        def issue_weight_cc(l):
            for ws in gathers:
                for kc in range(ws.K // 128):
                    fw.cc(lambda: nc.gpsimd.collective_compute(
                        "AllGather", ALU.bypass, replica_groups=RG,
                        ins=[ws.bn[l].t[kc * 32:(kc + 1) * 32, :].opt()],
                        outs=[ws.tl[l].t[kc * 128:(kc + 1) * 128, :].opt()]),
                        reads=[ws.bn[l]], writes=[ws.tl[l]])

        def issue_weight_gathers(l):
            issue_weight_bounce(l)
            issue_weight_cc(l)

        issue_weight_gathers(0)
        with contextlib.ExitStack() as es:
            cact = SB(es, [128, 8], F32)
            wbuf = [SB(es, [128, 8, 1536], F32) for _ in range(2)]
            baf = SB(es, [128, L, 12], F32)
            brow = SB(es, [1, L, 1536], F32)
            grow = SB(es, [1, L, 1536], F32)
            modq = SB(es, [128, L, 12], F32)
            g12 = SB(es, [128, 2, 8], F32)
            fw.dma("sp", cact[:], c_in[:], [c_in], [cact])
            actf(cact[:], cact[:], AF.Silu, [cact], [cact])
            fw.dma("sp", baf[:], b_ada_fq[:].rearrange("l p j -> p l j"), [b_ada_fq], [baf])
            fw.dma("sp", brow[:], b_ada_q[:].rearrange("(o l) n -> o l n", o=1), [b_ada_q], [brow])
            for l in range(L):
                psF = nextps()
                wb = wbuf[l % 2]
                fw.dma("sp", wb[:], w_ada_q[l][:, :].rearrange("(kc p) n -> p kc n", p=128), [w_ada_q[l]], [wb])
                for j in range(12):
                    for kc in range(8):
                        mm(psF[:, j:j + 1], wb[:, kc, j * 128:(j + 1) * 128], cact[:, kc:kc + 1],
                           kc == 0, kc == 7, [wb, cact], [psF])
                tt(modq[:, l, :], psF[:, 0:12], baf[:, l, :], ALU.add, [psF, baf], [modq])
                for n3 in range(3):
                    pr = nextps()
                    for kc in range(8):
                        mm(pr[0:1, :], cact[:, kc:kc + 1], wb[:, kc, n3 * 512:(n3 + 1) * 512],
                           kc == 0, kc == 7, [wb, cact], [pr])
                    tt(grow[:, l, n3 * 512:(n3 + 1) * 512], pr[0:1, :], brow[:, l, n3 * 512:(n3 + 1) * 512],
                       ALU.add, [pr, brow], [grow])
            fw.dma("sp", MF_i[:, :].rearrange("p (l j) -> p l j", l=L), modq[:], [modq], [MF_i])
            fw.dma("sp", GR_i[:, :].rearrange("o (l n) -> o l n", l=L), grow[:], [grow], [GR_i])
            allgather(MF_i, MF_o)
            allgather(GR_i, GR_o)
            for r4 in range(4):
                fw.dma("sp", modF[:, :, r4 * 12:(r4 + 1) * 12],
                       MF_o[r4 * 128:(r4 + 1) * 128, :].rearrange("p (l j) -> p l j", l=L), [MF_o], [modF])
            for l in range(L):
                fw.dma("sp", g12[:, 0, :], g1_in[l], [g1_in], [g12])
                fw.dma("sp", g12[:, 1, :], g2_in[l], [g2_in], [g12])
                stt(A1[:, l, :], modF[:, l, 8:16], 1.0, g12[:, 0, :], ALU.add, ALU.mult, [modF, g12], [A1])
                stt(A2[:, l, :], modF[:, l, 32:40], 1.0, g12[:, 1, :], ALU.add, ALU.mult, [modF, g12], [A2])

        fw.barrier()
        with contextlib.ExitStack() as es:
            posi = SB(es, [64, 512], I32)
            a0 = SB(es, [64, 512], F32)
            a1 = SB(es, [64, 512], F32)
            a2 = SB(es, [64, 512], F32)
            sn = SB(es, [64, 512], F32)
            cs = SB(es, [64, 512], F32)
            for t in range(TT):
                sl = slice(t * 512, (t + 1) * 512)
                fw.dma("sp", posi[:], pos[0:1, sl].partition_broadcast(64), [pos], [posi])
                fw.dve(lambda: nc.vector.tensor_copy(out=a0[:], in_=posi[:]), [posi], [a0])
                ts(a0[:], a0[:], invf[:, 0:1], None, ALU.mult, None, [a0, invf], [a0])
                ts(a1[:], a0[:], 1.0 / TWO_PI, MAGIC, ALU.mult, ALU.add, [a0], [a1])
                ts(a1[:], a1[:], -MAGIC, None, ALU.add, None, [a1], [a1])
                stt(a2[:], a1[:], -C1, a0[:], ALU.mult, ALU.add, [a1, a0], [a2])
                stt(a2[:], a1[:], -C2, a2[:], ALU.mult, ALU.add, [a1, a2], [a2])
                stt(a2[:], a1[:], -C3, a2[:], ALU.mult, ALU.add, [a1, a2], [a2])
                ts(a2[:], a2[:], -PI_LO, PI_LO, ALU.max, ALU.min, [a2], [a2])
                actf(sn[:], a2[:], AF.Sin, [a2], [sn])
                actf(a1[:], a2[:], AF.Abs, [a2], [a1])
                actf(cs[:], a1[:], AF.Sin, [a1, cst], [cs], scale=-1.0, bias=cst[0:64, 2:3])
                fw.dma("sp", CS[0, :, sl], cs[:], [cs], [CS])
                fw.dma("sp", CS[1, :, sl], sn[:], [sn], [CS])

        fw.barrier()
        if stop_after == "p0":
            fw.finish()
            return nc

        def norm_hT(xt, sq, ss, rstd, xn, A_ap, sh_ap, hT, dep):
            actf(sq[:], xt[:], AF.Square, [xt], [sq])
            fw.dve(lambda: nc.vector.tensor_reduce(out=ss[:], in_=sq[:], axis=mybir.AxisListType.X, op=ALU.add),
                   [sq], [ss])
            actf(rstd[:], ss[:], AF.Sqrt, [ss, cst], [rstd], scale=1.0 / D, bias=cst[:, 0:1])
            fw.dve(lambda: nc.vector.reciprocal(out=rstd[:], in_=rstd[:]), [rstd], [rstd])
            for b in range(4):
                ts(xn[:, b, :], xt[:, b, :], rstd[:, b:b + 1], None, ALU.mult, None, [xt, rstd], [xn])
            for kc in range(8):
                p = pt[kc % 2]
                for b in range(4):
                    fw.pe(lambda: nc.tensor.transpose(out=p[:, b * 128:(b + 1) * 128],
                                                      in_=xn[:, b, kc * 128:(kc + 1) * 128], identity=identb[:]),
                          [xn, identb], [p])
                actf(hT[:, kc, :], p[:, 0:512], AF.Identity, [p] + dep, [hT], scale=A_ap[:, kc:kc + 1], bias=sh_ap[:, kc:kc + 1])

        def chk(name):
            if stop_after == name:
                fw.stopped = True

        try:
          for l in range(L):
            x_src = x_in if l == 0 else XA
            x_dst = y_out if l == L - 1 else XA

            with contextlib.ExitStack() as es:
                w1 = SB(es, [128, 8, DIN + 64], BF16)
                wq = SB(es, [128, 2, 768 + 256], BF16)
                wkv = SB(es, [128, 2, 512], BF16)
                bg = SB(es, [128, NG], F32)
                bvs = SB(es, [128, 512], F32)
                gql = SB(es, [128, 2], F32)
                gkvl = SB(es, [128, 1], F32)
                gqh = SB(es, [128, 3], F32)
                gkh = SB(es, [128, 3], F32)
                for kc in range(8):
                    fw.dma("pl", w1[:, kc, 0:DIN], w_in[l, kc * 128:(kc + 1) * 128, :], [w_in.tl[l]], [w1])
                for kc in range(2):
                    fw.dma("pl", wq[:, kc, 0:768], w_uq[l, kc * 128:(kc + 1) * 128, :], [w_uq.tl[l]], [wq])
                for t2 in range(2):
                    fw.dma("pl", wkv[:, t2, :].rearrange("p (h d) -> p h d", h=4),
                           w_ukv[l].rearrange("p (h t d) -> p t h d", h=4, t=2)[:, t2], [w_ukv.tl[l]], [wkv])
                fw.dma("sp", bg[:], b_in_g[l], [b_in_g], [bg])
                fw.dma("sp", bvs[:], b_in[l:l + 1, 1472:1984].partition_broadcast(128), [b_in], [bvs])
                fw.dma("sp", gql[:], gql_in[l], [gql_in], [gql])
                fw.dma("sp", gkvl[:], gkvl_in[l], [gkvl_in], [gkvl])
                fw.dma("sp", gqh[:], gqh_in[l], [gqh_in], [gqh])
                fw.dma("sp", gkh[:], gkh_in[l], [gkh_in], [gkh])
                actf(w1[:, :, DIN:DIN + 32], w1[:, :, 416:448], AF.Identity, [w1], [w1], scale=-1.0)
                fw.act(lambda: nc.scalar.copy(out=w1[:, :, DIN + 32:DIN + 64], in_=w1[:, :, 384:416]), [w1], [w1])
                fw.dve(lambda: nc.vector.tensor_scalar(out=bg[0:32, 4:5], in0=bg[0:32, 4:5], scalar1=-1.0, scalar2=None,
                                                       op0=ALU.mult), [bg], [bg])
                for h in range(4):
                    actf(wq[:, :, 768 + h * 64:768 + h * 64 + 32], wq[:, :, h * 192 + 160:h * 192 + 192], AF.Identity, [wq], [wq], scale=-1.0)
                    fw.act(lambda: nc.scalar.copy(out=wq[:, :, 768 + h * 64 + 32:768 + h * 64 + 64],
                                                  in_=wq[:, :, h * 192 + 128:h * 192 + 160]), [wq], [wq])

                chk("p1w")
                xt = SB(es, [128, 4, D], F32)
                sq = SB(es, [128, 4, D], F32)
                xn = SB(es, [128, 4, D], BF16)
                ss = SB(es, [128, 4], F32)
                rstd = SB(es, [128, 4], F32)
                hT = SB(es, [128, 8, 512], BF16)
                cos2 = SB(es, [64, 512], F32)
                sin2 = SB(es, [64, 512], F32)
                qg = SB(es, [128, 2, 512], BF16)
                sqq = SB(es, [128, 2, 512], BF16)
                epsq = SB(es, [128, 512], F32)
                kvg = SB(es, [128, 512], BF16)
                sqkv = SB(es, [128, 512], BF16)
                rk2 = SB(es, [128, 512], F32)
                rkb = SB(es, [128, 512], F32)
                rktok = SB(es, [128, 4], F32)
                kr = SB(es, [64, 512], F32)
                krot = SB(es, [64, 512], F32)
                sqkr = SB(es, [64, 512], BF16)
                s2sb = SB(es, [128, 512], F32)
                krr = SB(es, [64, 512], F32)
                tmpa = SB(es, [128, 512], F32)
                tmpb = SB(es, [128, 512], F32)
                rsh = SB(es, [128, 512], F32)
                sqn = SB(es, [128, 512], BF16)
                sqr = SB(es, [64, 512], BF16)
                ob = [SB(es, [128, 512], BF16) for _ in range(4)]
                obr = [SB(es, [64, 512], BF16) for _ in range(4)]
                gat = SB(es, [128, 16, 512], BF16)
                vt = SB(es, [128, 4, 512], BF16)
                obk = [0]

                def nob():
                    obk[0] += 1
                    return ob[obk[0] % 4]

                def nobr():
                    obk[0] += 1
                    return obr[obk[0] % 4]

                def inproj(gi):
                    col0, M = GROUPS[gi]
                    p = nextps()
                    for kc in range(8):
                        mm(p[0:M, :], w1[:, kc, col0:col0 + M], hT[:, kc, :], kc == 0, kc == 7, [w1, hT], [p])
                    return p

                for t in range(TT):
                    sl = slice(t * 512, (t + 1) * 512)
                    fw.dma("sp", xt[:], x_src[sl, :].rearrange("(b p) d -> p b d", p=128), [x_src], [xt])
                    fw.dma("sp", cos2[:], CS[0, :, sl], [CS], [cos2])
                    fw.dma("sp", sin2[:], CS[1, :, sl], [CS], [sin2])
                    norm_hT(xt, sq, ss, rstd, xn, A1[:, l, :], modF[:, l, 0:8], hT, [A1, modF])

                    chk("p1n")
                    for j in range(2):
                        p = inproj(j)
                        ts(qg[:, j, :], p[:], bg[:, j:j + 1], gql[:, j:j + 1], ALU.add, ALU.mult, [p, bg, gql], [qg])
                        actf(sqq[:, j, :], p[:], AF.Square, [p, bg], [sqq], bias=bg[:, j:j + 1])
                    p = nextps()
                    for j in range(2):
                        mm(p[:], onesb[:], sqq[:, j, :], j == 0, j == 1, [onesb, sqq], [p])
                    ts(epsq[:], p[:], EPS / 256.0, EPS * EPS, ALU.mult, ALU.add, [p], [epsq])
                    p = inproj(2)
                    ts(kvg[:], p[:], bg[:, 2:3], gkvl[:, 0:1], ALU.add, ALU.mult, [p, bg, gkvl], [kvg])
                    actf(sqkv[:], p[:], AF.Square, [p, bg], [sqkv], bias=bg[:, 2:3])
                    p = nextps()
                    mm(p[:], onesb[:], sqkv[:], True, True, [onesb, sqkv], [p])
                    ts(rk2[:], p[:], 1.0 / 128.0, EPS, ALU.mult, ALU.add, [p], [rk2])
                    fw.dve(lambda: nc.vector.reciprocal(out=rk2[:], in_=rk2[:]), [rk2], [rk2])
                    actf(rkb[:], rk2[:], AF.Sqrt, [rk2], [rkb])
                    p = nextps()
                    for b in range(4):
                        mm(p[:, b:b + 1], sqkv[:, b * 128:(b + 1) * 128], onesb[:, 0:1], True, True, [sqkv, onesb], [p])
                    actf(rktok[:], p[:, 0:4], AF.Sqrt, [p, cst], [rktok], scale=1.0 / 128.0, bias=cst[:, 0:1])
                    fw.dve(lambda: nc.vector.reciprocal(out=rktok[:], in_=rktok[:]), [rktok], [rktok])
                    p = inproj(3)
                    actf(kr[:], p[0:64, :], AF.Identity, [p, bg], [kr], bias=bg[0:64, 3:4])
                    actf(sqkr[:], p[0:64, :], AF.Square, [p, bg], [sqkr], bias=bg[0:64, 3:4])
                    p = inproj(4)
                    actf(krot[:], p[0:64, :], AF.Identity, [p, bg], [krot], bias=bg[0:64, 4:5])
                    p = nextps()
                    mm(p[:], onesb[0:64, :], sqkr[:], True, True, [onesb, sqkr], [p])
                    actf(s2sb[:], p[:], AF.Identity, [p], [s2sb])
                    stt(krr[:], kr[:], gkh[0:64, 1:2], cos2[:], ALU.mult, ALU.mult, [kr, gkh, cos2], [krr])
                    stt(tmpa[0:64, :], krot[:], gkh[0:64, 2:3], sin2[:], ALU.mult, ALU.mult, [krot, gkh, sin2], [tmpa])
                    tt(krr[:], krr[:], tmpa[0:64, :], ALU.add, [krr, tmpa], [krr])

                    chk("p1l")
                    for h in range(4):
                        pn = nextps()
                        for kc in range(2):
                            mm(pn[:], wq[:, kc, h * 192:h * 192 + 128], qg[:, kc, :], kc == 0, kc == 1, [wq, qg], [pn])
                        pr = nextps()
                        for kc in range(2):
                            mm(pr[0:64, :], wq[:, kc, h * 192 + 128:h * 192 + 192], qg[:, kc, :], kc == 0, kc == 1, [wq, qg], [pr])
                        pro = nextps()
                        for kc in range(2):
                            mm(pro[0:64, :], wq[:, kc, 768 + h * 64:768 + h * 64 + 64], qg[:, kc, :], kc == 0, kc == 1, [wq, qg], [pro])
                        actf(sqn[:], pn[:], AF.Square, [pn], [sqn])
                        actf(sqr[:], pr[0:64, :], AF.Square, [pr], [sqr])
                        pss = nextps()
                        mm(pss[:], onesb[:], sqn[:], True, False, [onesb, sqn], [pss])
                        mm(pss[:], onesb[0:64, :], sqr[:], False, True, [onesb, sqr], [pss])
                        stt(tmpa[:], pss[:], 1.0 / 192.0, epsq[:], ALU.mult, ALU.add, [pss, epsq], [tmpa])
                        actf(rsh[:], tmpa[:], AF.Sqrt, [tmpa], [rsh])
                        fw.dve(lambda: nc.vector.reciprocal(out=rsh[:], in_=rsh[:]), [rsh], [rsh])
                        o = nob()
                        stt(o[:], pn[:], gqh[:, 0:1], rsh[:], ALU.mult, ALU.mult, [pn, gqh, rsh], [o])
                        fw.dma("sp", QTn[h, :, sl], o[:], [o], [QTn])
                        stt(tmpa[0:64, :], pr[0:64, :], gqh[0:64, 1:2], cos2[:], ALU.mult, ALU.mult, [pr, gqh, cos2], [tmpa])
                        stt(tmpb[0:64, :], pro[0:64, :], gqh[0:64, 2:3], sin2[:], ALU.mult, ALU.mult, [pro, gqh, sin2], [tmpb])
                        tt(tmpa[0:64, :], tmpa[0:64, :], tmpb[0:64, :], ALU.add, [tmpa, tmpb], [tmpa])
                        orr = nobr()
                        tt(orr[:], tmpa[0:64, :], rsh[0:64, :], ALU.mult, [tmpa, rsh], [orr])
                        fw.dma("sp", QTr[h, :, sl], orr[:], [orr], [QTr])

                    chk("p1q")
                    for h in range(4):
                        pn = nextps()
                        mm(pn[:], wkv[:, 0, h * 128:(h + 1) * 128], kvg[:], True, True, [wkv, kvg], [pn])
                        actf(sqn[:], pn[:], AF.Square, [pn], [sqn])
                        pss = nextps()
                        mm(pss[:], onesb[:], sqn[:], True, True, [onesb, sqn], [pss])
                        tt(tmpa[:], pss[:], rk2[:], ALU.mult, [pss, rk2], [tmpa])
                        tt(tmpa[:], tmpa[:], s2sb[:], ALU.add, [tmpa, s2sb], [tmpa])
                        ts(tmpa[:], tmpa[:], 1.0 / 192.0, EPS, ALU.mult, ALU.add, [tmpa], [tmpa])
                        actf(rsh[:], tmpa[:], AF.Sqrt, [tmpa], [rsh])
                        fw.dve(lambda: nc.vector.reciprocal(out=rsh[:], in_=rsh[:]), [rsh], [rsh])
                        tt(tmpb[:], rsh[:], rkb[:], ALU.mult, [rsh, rkb], [tmpb])
                        o = nob()
                        stt(o[:], pn[:], gkh[:, 0:1], tmpb[:], ALU.mult, ALU.mult, [pn, gkh, tmpb], [o])
                        fw.dma("sp", KTn_i[h][:, sl], o[:], [o], [KTn_i[h]])
                        orr = nobr()
                        tt(orr[:], krr[:], rsh[0:64, :], ALU.mult, [krr, rsh], [orr])
                        fw.dma("sp", KTr_i[h][:, sl], orr[:], [orr], [KTr_i[h]])
                    for b in range(4):
                        p = nextps()
                        mm(p[:], kvg[:, b * 128:(b + 1) * 128], wkv[:, 1, :], True, True, [kvg, wkv], [p])
                        ts(vt[:, b, :], p[:], rktok[:, b:b + 1], None, ALU.mult, None, [p, rktok], [vt])
                    for h in range(4):
                        fw.dma("sp", VM_i[h][t * 128:(t + 1) * 128, :].rearrange("p (b d) -> p b d", b=4),
                               vt[:, :, h * 128:(h + 1) * 128], [vt], [VM_i[h]])
                    chk("p1k")
                    for h in range(4):
                        p = inproj(5 + h)
                        o = nob()
                        actf(o[:], p[:], AF.Identity, [p, bg], [o], bias=bg[:, 5 + h:6 + h])
                        fw.dma("sp", QTs[h, :, sl], o[:], [o], [QTs])
                    for h in range(4):
                        p = inproj(9 + h)
                        o = nob()
                        actf(o[:], p[:], AF.Identity, [p, bg], [o], bias=bg[:, 9 + h:10 + h])
                        fw.dma("sp", KTs_i[h][:, sl], o[:], [o], [KTs_i[h]])
                    for b in range(4):
                        p = nextps()
                        for kc in range(8):
                            mm(p[:], hT[:, kc, b * 128:(b + 1) * 128], w1[:, kc, 1472:1984], kc == 0, kc == 7, [hT, w1], [p])
                        tt(vt[:, b, :], p[:], bvs[:], ALU.add, [p, bvs], [vt])
                    for h in range(4):
                        fw.dma("sp", VS_i[h][t * 128:(t + 1) * 128, :].rearrange("p (b d) -> p b d", b=4),
                               vt[:, :, h * 128:(h + 1) * 128], [vt], [VS_i[h]])
                    chk("p1s")
                    for j in range(16):
                        p = inproj(13 + j)
                        actf(gat[:, j, :], p[:], AF.Sigmoid, [p, bg], [gat], bias=bg[:, 13 + j:14 + j])
                    fw.dma("sp", GA[t].rearrange("p (j n) -> p j n", j=16), gat[:], [gat], [GA])

            fw.barrier()
            chk("p1")
            for h in range(4):
                allgather(KTn_i[h], KTn_o[h])
                allgather(KTr_i[h], KTr_o[h])
                allgather(VM_i[h], VM_o[h])
            for h in range(4):
                allgather(KTs_i[h], KTs_o[h])
                allgather(VS_i[h], VS_o[h])

            chk("ag")
            if l + 1 < L:
                issue_weight_bounce(l + 1)
            with contextlib.ExitStack() as es:
                NM, NS = 6, 8
                qn_t = [SB(es, [128, 512], BF16) for _ in range(2)]
                qr_t = [SB(es, [64, 512], BF16) for _ in range(2)]
                mkn_t = [SB(es, [128, 512], BF16) for _ in range(NM)]
                mkr_t = [SB(es, [64, 512], BF16) for _ in range(NM)]
                mvv_t = [SB(es, [128, 512], BF16) for _ in range(NM)]
                NP = 3
                pT_t = [SB(es, [128, 512], BF16) for _ in range(NP)]
                pacc = SB(es, [128, 512], F32)
                rinv = SB(es, [128, 512], F32)
                ytm_t = [SB(es, [128, 512], BF16) for _ in range(2)]
                NE = 3
                sc_ps = [ps[0], ps[1], ps[2]]
                oacc_m = ps[7]
                cnt = {"qm": 0, "km": 0, "sm": 0, "ym": 0, "sc": 0}

                def next_sc():
                    cnt["sc"] += 1
                    return sc_ps[cnt["sc"] % 3]

                class SBStream:
                    def __init__(self, rps, oacc, zb):
                        self.rps, self.oacc, self.zb = rps, oacc, zb
                        self.qs_t = [SB(es, [128, 512], BF16) for _ in range(2)]
                        self.kn_t = [SB(es, [128, 512], BF16) for _ in range(NS)]
                        self.vv_t = [SB(es, [128, 512], BF16) for _ in range(NS)]
                        self.e_t = [SB(es, [128, 512], F32) for _ in range(NE)]
                        self.lp_t = [SB(es, [128, 512], F32) for _ in range(NE)]
                        self.hi_t = [SB(es, [128, 512], BF16) for _ in range(NE)]
                        self.lo_t = [SB(es, [128, 512], BF16) for _ in range(NE)]
                        self.ex_t = [SB(es, [128, 512], F32) for _ in range(NE)]
                        self.a_t = [SB(es, [128, 512], BF16) for _ in range(NE)]
                        self.yt_t = [SB(es, [128, 512], BF16) for _ in range(2)]
                        self.nq = self.nk = self.nsx = self.ny = 0

                stA = SBStream(ps[3], ps[5], ps[0])
                stB = SBStream(ps[4], ps[6], ps[1])

                def mla_sweep(g, h):
                    gsl = slice(g * 512, (g + 1) * 512)
                    qn = qn_t[cnt["qm"] % 2]
                    qr = qr_t[cnt["qm"] % 2]
                    cnt["qm"] += 1
                    fw.dma("sp", qn[:], QTn[h, :, gsl], [QTn], [qn])
                    fw.dma("sp", qr[:], QTr[h, :, gsl], [QTr], [qr])
                    fw.pool(lambda: nc.gpsimd.memset(pacc[:], 0.0), [], [pacc])
                    first = True
                    pend_m = []

                    def flush_m():
                        while pend_m:
                            mc0, mv, mk, mp, mf, ml = pend_m.pop(0)
                            mm(oacc_m[:, mc0:], mv[:, mk], mp[:, mc0:], mf, ml, [mv, mp], [oacc_m])

                    for r in range(4):
                        for ch in range(g + 1):
                            i = cnt["km"] % NM
                            cnt["km"] += 1
                            kn, krt, vv = mkn_t[i], mkr_t[i], mvv_t[i]
                            csl = slice(ch * 512, (ch + 1) * 512)
                            fw.dma("sp", kn[:], KTn_o[h][r * 128:(r + 1) * 128, csl], [KTn_o[h]], [kn])
                            fw.dma("sp", krt[:], KTr_o[h][r * 64:(r + 1) * 64, csl], [KTr_o[h]], [krt])
                            vrow = r * (TT * 128) + ch * 128
                            fw.dma("sp", vv[:], VM_o[h][vrow:vrow + 128, :], [VM_o[h]], [vv])
                            tail = (ch == g)
                            for jb in range(4):
                                c0 = jb * 128 if tail else 0
                                last = (r == 3 and ch == g and jb == 3)
                                s_ = ps[2]
                                pT = pT_t[cnt["sm"] % NP]
                                cnt["sm"] += 1
                                ksl = slice(jb * 128, (jb + 1) * 128)
                                mm(s_[:, c0:], kn[:, ksl], qn[:, c0:], True, False, [kn, qn], [s_])
                                mm(s_[:, c0:], krt[:, ksl], qr[:, c0:], False, True, [krt, qr], [s_])
                                flush_m()
                                yield
                                actf(pT[:, c0:], s_[:, c0:], AF.Exp, [s_], [pT], scale=SC_MLA)
                                if tail:
                                    ptt(pT[:, c0:c0 + 128], pT[:, c0:c0 + 128], mcaus[:, r, :], ALU.mult, [pT, mcaus], [pT])
                                ptt(pacc[:, c0:], pacc[:, c0:], pT[:, c0:], ALU.add, [pacc, pT], [pacc])
                                pend_m.append((c0, vv, ksl, pT, first, last))
                                first = False
                                yield
                    flush_m()
                    prs = ps[2]
                    mm(prs[:], onesf[:], pacc[:], True, True, [onesf, pacc], [prs])
                    fw.dve(lambda: nc.vector.reciprocal(out=rinv[:], in_=prs[:]), [prs], [rinv])
                    yt = ytm_t[cnt["ym"] % 2]
                    cnt["ym"] += 1
                    tt(yt[:], oacc_m[:], rinv[:], ALU.mult, [oacc_m, rinv], [yt])
                    fw.dma("pl", YTm[h, :, gsl], yt[:], [yt], [YTm])

                def sb_sweep(g, h, st):
                    gsl = slice(g * 512, (g + 1) * 512)
                    rps, oacc_s = st.rps, st.oacc
                    qs = st.qs_t[st.nq % 2]
                    st.nq += 1
                    fw.dma("sp", qs[:], QTs[h, :, gsl], [QTs], [qs])
                    sfirst = True
                    pend = []
                    pend_u = []

                    def flush_u():
                        while pend_u:
                            uc0, uhi, ulo = pend_u.pop(0)
                            mm(rps[:, uc0:], ubar[:], uhi[:, uc0:], False, False, [ubar, uhi], [rps])
                            mm(rps[:, uc0:], ubar[:], ulo[:, uc0:], False, True, [ubar, ulo], [rps])

                    def flush_pv():
                        while pend:
                            pc0, pv, pk, pa, pf, pl = pend.pop(0)
                            mm(oacc_s[:, pc0:], pv[:, pk], pa[:, pc0:], pf, pl, [pv, pa], [oacc_s])

                    for ch in range(g, -1, -1):
                        csl = slice(ch * 512, (ch + 1) * 512)
                        kss, vss = [], []
                        for r in range(4):
                            i = st.nk % NS
                            st.nk += 1
                            kn, vv = st.kn_t[i], st.vv_t[i]
                            fw.dma("sp", kn[:], KTs_o[h][r * 128:(r + 1) * 128, csl], [KTs_o[h]], [kn])
                            vrow = r * (TT * 128) + ch * 128
                            fw.dma("sp", vv[:], VS_o[h][vrow:vrow + 128, :], [VS_o[h]], [vv])
                            kss.append(kn)
                            vss.append(vv)
                        tail = (ch == g)
                        for jb in range(3, -1, -1):
                            c0 = jb * 128 if tail else 0
                            ksl = slice(jb * 128, (jb + 1) * 128)
                            for r in range(3, -1, -1):
                                last = (ch == 0 and jb == 0 and r == 0)
                                k3 = st.nsx % NE
                                st.nsx += 1
                                z = st.zb
                                e, lp, hi, lo, ex, aa = st.e_t[k3], st.lp_t[k3], st.hi_t[k3], st.lo_t[k3], st.ex_t[k3], st.a_t[k3]
                                mm(z[:, c0:], kss[r][:, ksl], qs[:, c0:], True, True, [kss[r], qs], [z])
                                flush_pv()
                                yield
                                actf(e[:, c0:], z[:, c0:], AF.Exp, [z], [e], scale=SC_SB)
                                actf(lp[:, c0:], e[:, c0:], AF.Ln, [e, cst], [lp], bias=cst[:, 1:2])
                                if tail:
                                    tt(lp[:, c0:c0 + 128], lp[:, c0:c0 + 128], mstr[:, r, :], ALU.mult, [lp, mstr], [lp])
                                    ptt(e[:, c0:c0 + 128], e[:, c0:c0 + 128], mstr[:, r, :], ALU.mult, [e, mstr], [e])
                                fw.dve(lambda: nc.vector.tensor_copy(out=hi[:, c0:], in_=lp[:, c0:]), [lp], [hi])
                                tt(lo[:, c0:], lp[:, c0:], hi[:, c0:], ALU.subtract, [lp, hi], [lo])
                                flush_u()
                                yield
                                mm(rps[:, c0:], umat[:], hi[:, c0:], sfirst, False, [umat, hi], [rps])
                                mm(rps[:, c0:], umat[:], lo[:, c0:], False, True, [umat, lo], [rps])
                                yield
                                actf(ex[:, c0:], rps[:, c0:], AF.Exp, [rps], [ex], scale=-1.0)
                                pend_u.append((c0, hi, lo))
                                tt(aa[:, c0:], e[:, c0:], ex[:, c0:], ALU.mult, [e, ex], [aa])
                                pend.append((c0, vss[r], ksl, aa, sfirst, last))
                                sfirst = False
                                yield
                    flush_pv()
                    yt = st.yt_t[st.ny % 2]
                    st.ny += 1
                    actf(yt[:], oacc_s[:], AF.Identity, [oacc_s], [yt])
                    fw.dma("pl", YTs[h, :, gsl], yt[:], [yt], [YTs])

                def chain(*gens):
                    for gen in gens:
                        yield from gen

                for g in range(TT):
                    for hp in range(2):
                        if l + 1 < L and g == min(1, TT - 1) and hp == 1:
                            issue_weight_cc(l + 1)
                        h0, h1 = 2 * hp, 2 * hp + 1
                        gens = [sb_sweep(g, h0, stA), chain(mla_sweep(g, h0), mla_sweep(g, h1)), sb_sweep(g, h1, stB)]
                        done = [False, False, False]
                        next(gens[0])
                        next(gens[0])
                        kk = 0
                        while not all(done):
                            kk += 1
                            order = [0, 1, 2]
                            for gi in order:
                                if not done[gi]:
                                    try:
                                        next(gens[gi])
                                    except StopIteration:
                                        done[gi] = True
            fw.barrier()
            chk("p2")
            with contextlib.ExitStack() as es:
                wbm = SB(es, [128, 4, D], BF16)
                wbs = SB(es, [128, 4, D], BF16)
                wo = SB(es, [128, 8, D], BF16)
                gtb = SB(es, [128, D], F32)
                for kc in range(4):
                    fw.dma("pl", wbm[:, kc, :], w_bm[l, kc * 128:(kc + 1) * 128, :], [w_bm.tl[l]], [wbm])
                    fw.dma("pl", wbs[:, kc, :], w_bs[l, kc * 128:(kc + 1) * 128, :], [w_bs.tl[l]], [wbs])
                for kc in range(8):
                    fw.dma("pl", wo[:, kc, :], w_out[l, kc * 128:(kc + 1) * 128, :], [w_out.tl[l]], [wo])
                fw.dma("sp", gtb[:], GR_o[1:2, l * 1536 + 512:(l + 1) * 1536].partition_broadcast(128), [GR_o], [gtb])
                ym = SB(es, [128, 4, 512], BF16)
                ysb = SB(es, [128, 4, 512], BF16)
                gat = SB(es, [128, 16, 512], BF16)
                xt = SB(es, [128, 4, D], F32)
                sq = SB(es, [128, 4, D], F32)
                xn = SB(es, [128, 4, D], BF16)
                ss = SB(es, [128, 4], F32)
                rstd = SB(es, [128, 4], F32)
                mg = SB(es, [128, 8, 512], BF16)
                m1 = SB(es, [128, 512], F32)
                m2 = SB(es, [128, 512], F32)
                hT = SB(es, [128, 8, 512], BF16)
                hb = SB(es, [128, 8, NB, 2], BF16)
                for t in range(TT):
                    sl = slice(t * 512, (t + 1) * 512)
                    fw.dma("sp", ym[:], YTm[:, :, sl].rearrange("h p n -> p h n"), [YTm], [ym])
                    fw.dma("sp", ysb[:], YTs[:, :, sl].rearrange("h p n -> p h n"), [YTs], [ysb])
                    fw.dma("sp", gat[:], GA[t].rearrange("p (j n) -> p j n", j=16), [GA], [gat])
                    fw.dma("sp", xt[:], x_src[sl, :].rearrange("(b p) d -> p b d", p=128), [x_src], [xt])
                    for oc in range(8):
                        pa = nextps()
                        for kc in range(4):
                            mm(pa[:], wbm[:, kc, oc * 128:(oc + 1) * 128], ym[:, kc, :], kc == 0, kc == 3, [wbm, ym], [pa])
                        pb = nextps()
                        for kc in range(4):
                            mm(pb[:], wbs[:, kc, oc * 128:(oc + 1) * 128], ysb[:, kc, :], kc == 0, kc == 3, [wbs, ysb], [pb])
                        tt(m1[:], pa[:], gat[:, oc, :], ALU.mult, [pa, gat], [m1])
                        tt(m2[:], pb[:], gat[:, 8 + oc, :], ALU.mult, [pb, gat], [m2])
                        tt(mg[:, oc, :], m1[:], m2[:], ALU.add, [m1, m2], [mg])
                    for b in range(4):
                        for half in range(2):
                            p = nextps()
                            for kc in range(8):
                                mm(p[:], mg[:, kc, b * 128:(b + 1) * 128], wo[:, kc, half * 512:(half + 1) * 512],
                                   kc == 0, kc == 7, [mg, wo], [p])
                            hs = slice(half * 512, (half + 1) * 512)
                            tt(m1[:], p[:], gtb[:, hs], ALU.mult, [p, gtb], [m1])
                            tt(xt[:, b, hs], xt[:, b, hs], m1[:], ALU.add, [xt, m1], [xt])
                    fw.dma("sp", X1[sl, :].rearrange("(b p) d -> p b d", p=128), xt[:], [xt], [X1])
                    norm_hT(xt, sq, ss, rstd, xn, A2[:, l, :], modF[:, l, 24:32], hT, [A2, modF])
                    fw.dma("sp", H2[t].rearrange("p (k n) -> p k n", k=8), hT[:], [hT], [H2])
                    for b in range(4):
                        fw.act(lambda: nc.scalar.copy(out=hb[:, :, t * 4 + b, :], in_=hT[:, :, b * 128 + 126:b * 128 + 128]),
                               [hT], [hb])
                fw.dma("sp", HB_i[:, :].rearrange("p (k m e) -> p k m e", k=8, m=NB), hb[:], [hb], [HB_i])
            fw.barrier()
            chk("p3")
            allgather(HB_i, HB_o)

            with contextlib.ExitStack() as es:
                wu = SB(es, [128, 8, 2 * DFF], BF16)
                wd = SB(es, [128, 22, D], BF16)
                wcv = SB(es, [128, 44, 3], F32)
                bcv = SB(es, [128, 44], F32)
                gtb = SB(es, [128, D], F32)
                for kc in range(8):
                    fw.dma("pl", wu[:, kc, :], w_up[l, kc * 128:(kc + 1) * 128, :], [w_up.tl[l]], [wu])
                for kc in range(22):
                    fw.dma("pl", wd[:, kc, :], w_down[l, kc * 128:(kc + 1) * 128, :], [w_down.tl[l]], [wd])
                fw.dma("sp", wcv[:], wconv_f[l], [wconv_f], [wcv])
                fw.dma("sp", bcv[:], bconv_f[l], [bconv_f], [bcv])
                fw.dma("sp", gtb[:], GR_o[3:4, l * 1536 + 512:(l + 1) * 1536].partition_broadcast(128), [GR_o], [gtb])
                hT = SB(es, [128, 8, 512], BF16)
                hselb = SB(es, [128, 8, NB, 2], BF16)
                uek = [0]
                cva = [SB(es, [128, 4, 128], F32) for _ in range(2)]
                sil = SB(es, [128, 4, 128], F32)
                ff = SB(es, [128, 22, 512], BF16)
                xt = SB(es, [128, 4, D], F32)
                m1 = SB(es, [128, 512], F32)
                es2 = contextlib.ExitStack()
                hcand = SB(es2, [128, 4, 8, NB, 2], BF16)
                hsel = SB(es2, [128, 8, NB, 2], F32)
                fw.dma("sp", hcand[:].rearrange("p r k m e -> p r (k m e)"),
                       HB_o[:, :].rearrange("(r p) n -> p r n", p=128), [HB_o], [hcand])
                ts(hsel[:], hcand[:, 0], halow[:, 0:1], None, ALU.mult, None, [hcand, halow], [hsel])
                for r in range(1, 4):
                    stt(hsel[:], hcand[:, r], halow[:, r:r + 1], hsel[:], ALU.mult, ALU.add, [hcand, halow, hsel], [hsel])
                if NB > 1:
                    stt(hsel[:, :, 1:NB, :], hcand[:, 3, :, 0:NB - 1, :], halow[:, 4:5], hsel[:, :, 1:NB, :],
                        ALU.mult, ALU.add, [hcand, halow, hsel], [hsel])
                fw.dve(lambda: nc.vector.tensor_copy(out=hselb[:], in_=hsel[:]), [hsel], [hselb])
                fw.barrier()
                es2.close()
                ue_t = [SB(es, [128, 4, 130], F32) for _ in range(2)]
                for t in range(TT):
                    sl = slice(t * 512, (t + 1) * 512)
                    fw.dma("sp", hT[:], H2[t].rearrange("p (k n) -> p k n", k=8), [H2], [hT])
                    fw.dma("sp", xt[:], X1[sl, :].rearrange("(b p) d -> p b d", p=128), [X1], [xt])
                    for j in range(22):
                        for which, oc in ((0, j), (1, 22 + j)):
                            p = nextps()
                            for kc in range(8):
                                mm(p[:], wu[:, kc, oc * 128:(oc + 1) * 128], hT[:, kc, :], kc == 0, kc == 7, [wu, hT], [p])
                            ph = nextps()
                            for kc in range(8):
                                mm(ph[:, 0:8], wu[:, kc, oc * 128:(oc + 1) * 128], hselb[:, kc, t * 4:(t + 1) * 4, :],
                                   kc == 0, kc == 7, [wu, hselb], [ph])
                            ue = ue_t[uek[0] % 2]
                            uek[0] += 1
                            cv = cva[which]
                            actf(ue[:, :, 2:130], p[:].rearrange("p (b n) -> p b n", b=4), AF.Identity, [p], [ue])
                            actf(ue[:, :, 0:2], ph[:, 0:8].rearrange("p (b e) -> p b e", b=4), AF.Identity, [ph], [ue])
                            actf(cv[:], p[:].rearrange("p (b n) -> p b n", b=4), AF.Identity, [p, wcv, bcv], [cv],
                                 scale=wcv[:, oc, 2:3], bias=bcv[:, oc:oc + 1])
                            stt(cv[:], ue[:, :, 1:129], wcv[:, oc, 1:2], cv[:], ALU.mult, ALU.add, [ue, wcv, cv], [cv])
                            stt(cv[:], ue[:, :, 0:128], wcv[:, oc, 0:1], cv[:], ALU.mult, ALU.add, [ue, wcv, cv], [cv])
                        actf(sil[:], cva[0][:], AF.Silu, [cva[0]], [sil])
                        tt(ff[:, j, :].rearrange("p (b n) -> p b n", b=4), sil[:], cva[1][:], ALU.mult, [sil, cva[1]], [ff])
                    for b in range(4):
                        for half in range(2):
                            p = nextps()
                            for kc in range(22):
                                mm(p[:], ff[:, kc, b * 128:(b + 1) * 128], wd[:, kc, half * 512:(half + 1) * 512],
                                   kc == 0, kc == 21, [ff, wd], [p])
                            hs = slice(half * 512, (half + 1) * 512)
                            tt(m1[:], p[:], gtb[:, hs], ALU.mult, [p, gtb], [m1])
                            tt(xt[:, b, hs], xt[:, b, hs], m1[:], ALU.add, [xt, m1], [xt])
                    fw.dma("sp", x_dst[sl, :].rearrange("(b p) d -> p b d", p=128), xt[:], [xt], [x_dst])
            fw.barrier()
        except StopBuild:
            pass
        fw.stopped = False
        for name in dump:
            t, shape, dt = scr[name]
            o = Tl(nc.dram_tensor("dbg_" + name, shape, dt, kind="ExternalOutput").ap())
            fw.dma("sp", o.t, t.t, [t], [o])
        fw.finish()
    if os.environ.get("KVERBOSE"):
        print("instr counts: pe", fw.sem_pe.count, "act", fw.sem_act.count, "dve", fw.sem_dve.count, "pool", fw.sem_pool.count,
              "cc", fw.sem_cc.count, "dma_sp", fw.n_sp, "dma_pl", fw.n_pl, "total(with waits)", fw.n_ins, flush=True)
    return nc


def _prep_inputs(inp, S, L):
    B = inp["x"].shape[0]
    NB = S // 512
    f32 = np.float32
    ident = np.eye(128, dtype=f32)
    kk = np.arange(128)
    umat = (kk[:, None] >= kk[None, :]).astype(f32)
    ubar = (kk[:, None] < kk[None, :]).astype(f32)
    tri_le = (kk[:, None] <= kk[None, :]).astype(f32)
    tri_lt = (kk[:, None] < kk[None, :]).astype(f32)
    invf = (1.0 / (10000.0 ** (np.arange(32, dtype=f32) * f32(2.0 / 64)))).astype(f32)
    invf2 = np.concatenate([invf, invf]).reshape(64, 1).astype(f32)

    def fm(v, n):
        return np.ascontiguousarray(v.reshape(L, n, 128).transpose(0, 2, 1))

    b_in = np.asarray(inp["b_in"], f32)
    big = np.zeros((L, 128, NG), f32)
    for gi, (c0, M) in enumerate(GROUPS):
        if gi == 4:
            big[:, 0:32, gi] = b_in[:, 416:448]
            big[:, 32:64, gi] = b_in[:, 384:416]
        else:
            big[:, 0:M, gi] = b_in[:, c0:c0 + M]

    def headg(g):
        o = np.zeros((L, 128, 3), f32)
        o[:, :, 0] = g[:, 0:128]
        o[:, 0:64, 1] = g[:, 128:192]
        o[:, 0:32, 2] = g[:, 160:192]
        o[:, 32:64, 2] = g[:, 128:160]
        return o

    shared = dict(
        ident=ident, umat=umat, ubar=ubar, invf=invf2,
        g1=fm(np.asarray(inp["g_norm1"], f32), 8), g2=fm(np.asarray(inp["g_norm2"], f32), 8),
        b_in_g=big, b_in=b_in,
        gql=fm(np.asarray(inp["g_q_lat"], f32), 2), gkvl=fm(np.asarray(inp["g_kv_lat"], f32), 1),
        gqh=headg(np.asarray(inp["g_q_head"], f32)), gkh=headg(np.asarray(inp["g_k_head"], f32)),
        wconv_f=np.ascontiguousarray(np.asarray(inp["w_conv"], f32).reshape(L, 3, 44, 128).transpose(0, 3, 2, 1)),
        bconv_f=fm(np.asarray(inp["b_conv"], f32), 44),
    )
    wrep = dict(w_uq="w_uq", w_ukv="w_ukv", w_bm="w_branch_mla", w_bs="w_branch_sb", w_out="w_out", w_down="w_down")
    for k, v in wrep.items():
        w = np.asarray(inp[v], f32)
        for l in range(L):
            shared[f"{k}_l{l}"] = w[l]
    wshard = {k: np.asarray(inp[k], f32) for k in ("w_in", "w_up")}
    w_ada_np = np.asarray(inp["w_ada"], f32)
    b_ada_np = np.asarray(inp["b_ada"], f32)
    x = np.asarray(inp["x"], f32)
    c = np.asarray(inp["c"], f32)
    positions = np.asarray(inp["positions"], np.int32)
    in_maps = []
    for r in range(8):
        b, cc = r // 4, r % 4
        xb = x[b].reshape(NB, 4, 128, D)[:, cc].reshape(NB * 128, D)
        pb = positions[b].reshape(NB, 4, 128)[:, cc].reshape(1, NB * 128)
        mca = np.zeros((128, 4, 128), f32)
        mst = np.zeros((128, 4, 128), f32)
        for rr in range(4):
            if rr < cc:
                mca[:, rr, :] = 1.0
                mst[:, rr, :] = 1.0
            elif rr == cc:
                mca[:, rr, :] = tri_le
                mst[:, rr, :] = tri_lt
        hw = np.zeros((128, 5), f32)
        if cc >= 1:
            hw[:, cc - 1] = 1.0
        else:
            hw[:, 4] = 1.0
        m = dict(shared)
        for k, w in wshard.items():
            K, N = w.shape[1], w.shape[2]
            for l in range(L):
                m[f"{k}_sh{l}"] = np.ascontiguousarray(w[l].reshape(K // 128, 4, 32, N)[:, cc].reshape(K // 4, N))
        for l in range(L):
            m[f"w_ada_q{l}"] = np.ascontiguousarray(w_ada_np[l][:, cc * 1536:(cc + 1) * 1536])
        m["b_ada_fq"] = np.ascontiguousarray(b_ada_np[:, cc * 1536:(cc + 1) * 1536].reshape(L, 12, 128).transpose(0, 2, 1))
        m["b_ada_q"] = np.ascontiguousarray(b_ada_np[:, cc * 1536:(cc + 1) * 1536])
        m.update(x=np.ascontiguousarray(xb), pos=np.ascontiguousarray(pb),
                 c=np.ascontiguousarray(c[b].reshape(8, 128).T), mcaus=mca, mstrict=mst, halow=hw)
        in_maps.append(m)
    return in_maps


def run(inp, S, L, stop_after=None, trace=False, dump=()):
    nc = build(S, L, stop_after, dump)
    in_maps = _prep_inputs(inp, S, L)
    res = run_bass_kernel_spmd(nc, in_maps, core_ids=list(range(8)), trace=trace)
    NB = S // 512
    B = 2
    out = np.zeros((B, S, D), np.float32)
    ov = out.reshape(B, NB, 4, 128, D)
    for r in range(8):
        b, cc = r // 4, r % 4
        ov[b, :, cc] = np.asarray(res.results[r]["y"]).reshape(NB, 128, D)
    return out, res


def kernel(**inputs):
    S = inputs["x"].shape[1]
    L = inputs["w_in"].shape[0]
    out, _ = run(inputs, S, L)
    return out
```
